# Optimizing a Trainium2 kernel written in Bass

```python
import jax, jax.numpy as jnp
from jax import lax
import numpy as np

D_MODEL = 1024
BATCH = 4
SEQ = 4096
DEPTH = 1
DEC_BATCH = 8
DEC_SEQ = 16
PAST_LEN = 1024

CHUNK = 64
D_RNN = D_MODEL
N_LRU_BLOCKS = 16
LRU_BLOCK = D_RNN // N_LRU_BLOCKS
LRU_C = 8.0
RNN_CONV_W = 4
N_HEADS = 8
HEAD_DIM = D_MODEL // N_HEADS
D_ATTN = N_HEADS * HEAD_DIM
Q_BLOCK = 128
D_FF = 3 * D_MODEL
FFN_CONV_W = 3
NORM_EPS = 1e-6
D_IN = 2 * D_RNN + 3 * D_ATTN + 2 * D_MODEL

kernel_name = "hawk_stickbreak_convffn_stream"


def _rmsnorm(x, g):
    x32 = x.astype(jnp.float32)
    y = x32 * lax.rsqrt(jnp.mean(x32 * x32, axis=-1, keepdims=True) + NORM_EPS)
    return y.astype(x.dtype) * g


def _causal_dwconv(x, hist, w, b):
    W = w.shape[0]
    T = x.shape[1]
    xp = jnp.concatenate([hist.astype(x.dtype), x], axis=1)
    y = xp[:, 0:T] * w[0]
    for i in range(1, W):
        y = y + xp[:, i:i + T] * w[i]
    return y + b, xp[:, T:]


def _block_diag(x, w, b):
    B, T, C = x.shape
    xb = x.reshape(B, T, N_LRU_BLOCKS, LRU_BLOCK)
    return jnp.einsum('btnc,ncd->btnd', xb, w).reshape(B, T, C) + b


def _linear_scan(a, b, h0):
    b = b.at[:, 0].add(a[:, 0] * h0)
    def comb(l, r):
        return (l[0] * r[0], r[0] * l[1] + r[1])
    _, h = lax.associative_scan(comb, (a, b), axis=1)
    return h


def _rg_lru(x, h0, wa, ba, wx, bx, lam):
    r = jax.nn.sigmoid(_block_diag(x, wa, ba).astype(jnp.float32))
    i = jax.nn.sigmoid(_block_diag(x, wx, bx).astype(jnp.float32))
    log_a = -LRU_C * r * jax.nn.softplus(-lam.astype(jnp.float32))
    a = jnp.exp(log_a)
    b = jnp.sqrt(-jnp.expm1(2.0 * log_a)) * (i * x.astype(jnp.float32))
    h = _linear_scan(a, b, h0.astype(jnp.float32))
    return h.astype(x.dtype), h[:, -1].astype(x.dtype)


def _stick_breaking(q, k, v, q_pos, k_pos):
    z = jnp.einsum('bqhd,bkhd->bhqk', q, k).astype(jnp.float32) * (HEAD_DIM ** -0.5)
    mask = k_pos[None, :] < q_pos[:, None]
    log_beta = jax.nn.log_sigmoid(z)
    log_1m = jnp.where(mask, jax.nn.log_sigmoid(-z), 0.0)
    suffix = lax.cumsum(log_1m, axis=3, reverse=True) - log_1m
    w = jnp.where(mask, jnp.exp(log_beta + suffix), 0.0)
    return jnp.einsum('bhqk,bkhd->bqhd', w.astype(v.dtype), v)


def _stick_breaking_prompt(q, k, v):
    B, T, H, Dh = q.shape
    nb = T // Q_BLOCK
    qb = q.reshape(B, nb, Q_BLOCK, H, Dh).transpose(1, 0, 2, 3, 4)
    pos = jnp.arange(T, dtype=jnp.int32)
    qpos = pos.reshape(nb, Q_BLOCK)
    out = lax.map(lambda args: _stick_breaking(args[0], k, v, args[1], pos), (qb, qpos))
    return out.transpose(1, 0, 2, 3, 4).reshape(B, T, H, Dh)


def _layer(x, conv_hist, h0, k_past, v_past, ffn_hist,
           ln1, w_in, rnn_conv_w, rnn_conv_b, lru_wa, lru_ba, lru_wx, lru_bx, lru_lambda,
           q_norm_g, k_norm_g, w_proj_rnn, w_proj_attn, w_out,
           ln2, w_up, ffn_conv_w, ffn_conv_b, w_down):
    B, T, _ = x.shape
    u = _rmsnorm(x, ln1)
    proj = u @ w_in
    splits = [D_RNN, 2 * D_RNN, 2 * D_RNN + D_ATTN, 2 * D_RNN + 2 * D_ATTN,
              2 * D_RNN + 3 * D_ATTN, 2 * D_RNN + 3 * D_ATTN + D_MODEL]
    xr, gr, q, k, v, g_rnn, g_attn = jnp.split(proj, splits, axis=-1)
    xc, conv_tail = _causal_dwconv(xr, conv_hist, rnn_conv_w, rnn_conv_b)
    hseq, h_last = _rg_lru(xc, h0, lru_wa, lru_ba, lru_wx, lru_bx, lru_lambda)
    y_rnn = hseq * jax.nn.gelu(gr, approximate=True)
    q = _rmsnorm(q.reshape(B, T, N_HEADS, HEAD_DIM), q_norm_g)
    k = _rmsnorm(k.reshape(B, T, N_HEADS, HEAD_DIM), k_norm_g)
    v = v.reshape(B, T, N_HEADS, HEAD_DIM)
    if k_past is None:
        o = _stick_breaking_prompt(q, k, v)
    else:
        P = k_past.shape[1]
        k_all = jnp.concatenate([k_past.astype(k.dtype), k], axis=1)
        v_all = jnp.concatenate([v_past.astype(v.dtype), v], axis=1)
        q_pos = P + jnp.arange(T, dtype=jnp.int32)
        k_pos = jnp.arange(P + T, dtype=jnp.int32)
        o = _stick_breaking(q, k_all, v_all, q_pos, k_pos)
    y_attn = o.reshape(B, T, D_ATTN)
    merged = jax.nn.sigmoid(g_rnn) * (y_rnn @ w_proj_rnn) + jax.nn.sigmoid(g_attn) * (y_attn @ w_proj_attn)
    x = x + merged @ w_out
    u2 = _rmsnorm(x, ln2)
    gate_pre, val = jnp.split(u2 @ w_up, 2, axis=-1)
    gc, ffn_tail = _causal_dwconv(gate_pre, ffn_hist, ffn_conv_w, ffn_conv_b)
    x = x + (jax.nn.gelu(gc, approximate=True) * val) @ w_down
    return x, (k, v, conv_tail, h_last, ffn_tail)


def setup_inputs(seed: int = 0) -> dict:
    key = jax.random.key(seed)
    ks = iter(jax.random.split(key, 32))
    L = DEPTH
    def nrm(shape, scale=1.0):
        return jax.random.normal(next(ks), shape, jnp.float32) * scale
    u = jax.random.uniform(next(ks), (L, D_RNN), jnp.float32, minval=0.9, maxval=0.999)
    s = u ** (1.0 / LRU_C)
    return {
        "x_prompt": nrm((BATCH, SEQ, D_MODEL)),
        "x_sample": nrm((DEC_BATCH, DEC_SEQ, D_MODEL)),
        "cache_k": nrm((L, DEC_BATCH, PAST_LEN, N_HEADS, HEAD_DIM)),
        "cache_v": nrm((L, DEC_BATCH, PAST_LEN, N_HEADS, HEAD_DIM)),
        "state_rnn_conv": nrm((L, DEC_BATCH, RNN_CONV_W - 1, D_RNN)),
        "state_rnn_h": nrm((L, DEC_BATCH, D_RNN), 0.5),
        "state_ffn_conv": nrm((L, DEC_BATCH, FFN_CONV_W - 1, D_FF)),
        "ln1": 1.0 + nrm((L, D_MODEL), 0.01),
        "w_in": nrm((L, D_MODEL, D_IN), D_MODEL ** -0.5),
        "rnn_conv_w": nrm((L, RNN_CONV_W, D_RNN), RNN_CONV_W ** -0.5),
        "rnn_conv_b": nrm((L, D_RNN), 0.01),
        "lru_wa": nrm((L, N_LRU_BLOCKS, LRU_BLOCK, LRU_BLOCK), LRU_BLOCK ** -0.5),
        "lru_ba": nrm((L, D_RNN), 0.01),
        "lru_wx": nrm((L, N_LRU_BLOCKS, LRU_BLOCK, LRU_BLOCK), LRU_BLOCK ** -0.5),
        "lru_bx": nrm((L, D_RNN), 0.01),
        "lru_lambda": jnp.log(s) - jnp.log1p(-s),
        "q_norm_g": 1.0 + nrm((L, HEAD_DIM), 0.01),
        "k_norm_g": 1.0 + nrm((L, HEAD_DIM), 0.01),
        "w_proj_rnn": nrm((L, D_RNN, D_MODEL), D_RNN ** -0.5),
        "w_proj_attn": nrm((L, D_ATTN, D_MODEL), D_ATTN ** -0.5),
        "w_out": nrm((L, D_MODEL, D_MODEL), D_MODEL ** -0.5),
        "ln2": 1.0 + nrm((L, D_MODEL), 0.01),
        "w_up": nrm((L, D_MODEL, 2 * D_FF), D_MODEL ** -0.5),
        "ffn_conv_w": nrm((L, FFN_CONV_W, D_FF), FFN_CONV_W ** -0.5),
        "ffn_conv_b": nrm((L, D_FF), 0.01),
        "w_down": nrm((L, D_FF, D_MODEL), D_FF ** -0.5),
    }


def reference(x_prompt, x_sample, cache_k, cache_v, state_rnn_conv, state_rnn_h, state_ffn_conv,
              ln1, w_in, rnn_conv_w, rnn_conv_b, lru_wa, lru_ba, lru_wx, lru_bx, lru_lambda,
              q_norm_g, k_norm_g, w_proj_rnn, w_proj_attn, w_out,
              ln2, w_up, ffn_conv_w, ffn_conv_b, w_down):
    assert x_sample.shape[1] <= CHUNK
    yp, ys = x_prompt, x_sample
    Bp = x_prompt.shape[0]
    st_p, st_s = [], []
    for l in range(DEPTH):
        W = (ln1[l], w_in[l], rnn_conv_w[l], rnn_conv_b[l], lru_wa[l], lru_ba[l], lru_wx[l], lru_bx[l],
             lru_lambda[l], q_norm_g[l], k_norm_g[l], w_proj_rnn[l], w_proj_attn[l], w_out[l],
             ln2[l], w_up[l], ffn_conv_w[l], ffn_conv_b[l], w_down[l])
        zc = jnp.zeros((Bp, RNN_CONV_W - 1, D_RNN), yp.dtype)
        zh = jnp.zeros((Bp, D_RNN), yp.dtype)
        zf = jnp.zeros((Bp, FFN_CONV_W - 1, D_FF), yp.dtype)
        yp, sp = _layer(yp, zc, zh, None, None, zf, *W)
        ys, ss = _layer(ys, state_rnn_conv[l], state_rnn_h[l], cache_k[l], cache_v[l], state_ffn_conv[l], *W)
        st_p.append(sp)
        st_s.append(ss)
    k_p = jnp.stack([s[0] for s in st_p])
    v_p = jnp.stack([s[1] for s in st_p])
    rc_p = jnp.stack([s[2] for s in st_p])
    h_p = jnp.stack([s[3] for s in st_p])
    fc_p = jnp.stack([s[4] for s in st_p])
    k_s = jnp.stack([s[0] for s in st_s])
    v_s = jnp.stack([s[1] for s in st_s])
    rc_s = jnp.stack([s[2] for s in st_s])
    h_s = jnp.stack([s[3] for s in st_s])
    fc_s = jnp.stack([s[4] for s in st_s])
    return (yp, ys, k_p, v_p, rc_p, h_p, fc_p, k_s, v_s, rc_s, h_s, fc_s)
```

```python
import numpy as np
from contextlib import ExitStack
import ml_dtypes
import concourse.bass as bass
import concourse.mybir as mybir
from concourse.bass_utils import run_bass_kernel_spmd

F32 = mybir.dt.float32
BF16 = mybir.dt.bfloat16
AF = mybir.ActivationFunctionType
ALU = mybir.AluOpType
AX = mybir.AxisListType

D = 1024
NH = 8
HD = 128
DFF = 3072
EPS = 1e-6
N_CORES = 8


def I(name, *a, **kw):
    return (name, a, kw)


class Buf:
    __slots__ = ("name", "w", "r", "sem", "cnt", "const")

    def __init__(self, name, const=False):
        self.name = name
        self.w = None
        self.r = []
        self.sem = None
        self.cnt = 0
        self.const = const


class Eng:
    def __init__(self, name, h, sem):
        self.name = name
        self.h = h
        self.sem = sem
        self.count = 0
        self.waited = {}


class Unit:
    __slots__ = ("id", "eng", "instrs", "deps", "dur", "kind", "semof", "slow", "is_out", "tag", "cls",
                 "nbytes", "start", "children", "nrem", "ready", "phase")


def _free_size(ap):
    n = 1
    for s_ in ap.shape[1:]:
        n *= s_
    return n


_ACT_CLS = {"Exp": "A", "Ln": "A", "Sigmoid": "B", "Gelu_apprx_tanh": "C", "Sqrt": "D"}


class K:
    def __init__(self, nc, es):
        self.nc = nc
        self.es = es
        self.engs = {}
        for name, h in (("pe", nc.tensor), ("act", nc.scalar), ("dve", nc.vector),
                        ("pool", nc.gpsimd), ("sp", nc.sync)):
            sem = es.enter_context(nc.semaphore("c_" + name))
            self.engs[name] = Eng(name, h, sem)
        self.nsem = 5
        self.units = []
        self.pe_group = None
        self.sb_bytes = 0

    def sb(self, name, shape, dt):
        n = 1
        for s_ in shape[1:]:
            n *= s_
        self.sb_bytes += n * (4 if dt == F32 else 2)
        return self.nc.alloc_sbuf_tensor(name, list(shape), dt)

    def new_sem(self, name):
        self.nsem += 1
        return self.es.enter_context(self.nc.semaphore("d_" + name))

    def _new_unit(self, eng, instrs, r, w, kind, dur):
        u = Unit()
        u.id = len(self.units)
        u.eng = eng
        u.instrs = instrs
        u.kind = kind
        u.dur = dur
        u.semof = None
        u.slow = False
        u.is_out = False
        u.tag = None
        u.cls = None
        u.nbytes = 0
        u.phase = getattr(self, "phase", "")
        deps = set()
        for b in r:
            if b.w is not None:
                deps.add(b.w)
        for b in w:
            if b.w is not None:
                deps.add(b.w)
            deps.update(b.r)
        deps.discard(u.id)
        u.deps = deps
        for b in r:
            if not b.const:
                b.r.append(u.id)
        for b in w:
            b.w = u.id
            b.r = []
        self.units.append(u)
        return u

    def op(self, eng, instr, r=(), w=(), inc=True):
        name, a, kw = instr
        out = kw.get("out", a[0] if a else None)
        n = _free_size(out) if out is not None else 64
        if eng == "pe":
            dur = 35.0 + 0.45 * max(n, 64) if name == "matmul" else 90.0
            if self.pe_group is None:
                self.pe_group = ([], [], [], 0.0)
            g = self.pe_group
            g[0].append(instr)
            g[1].extend(r)
            g[2].extend(w)
            self.pe_group = (g[0], g[1], g[2], g[3] + dur)
            if not inc:
                return
            g = self.pe_group
            self.pe_group = None
            return self._new_unit("pe", g[0], list(dict.fromkeys(g[1])), list(dict.fromkeys(g[2])), "op", g[3])
        if eng == "act":
            dur = 180.0 + 0.85 * n
        elif eng == "dve":
            dur = 120.0 + 1.05 * n
            if name == "tensor_tensor_scan":
                dur = 120.0 + 2.1 * n
        else:
            dur = 200.0 + 1.8 * n
        u = self._new_unit(eng, [instr], list(r), list(w), "op", dur)
        if eng == "act" and name == "activation":
            u.cls = _ACT_CLS.get(str(kw.get("func")).split(".")[-1])
        return u

    def dma(self, q, out, in_, r, w, semof, slow=False, is_out=False):
        return self.dma_multi(q, [(out, in_)], r, w, semof, slow=slow, is_out=is_out)

    def dma_multi(self, q, pairs, r, w, semof, slow=False, is_out=False):
        instrs = []
        nbytes = 0
        for out, in_ in pairs:
            kw = dict(out=out, in_=in_)
            if slow:
                kw["allow_slow_non_contiguous"] = True
            instrs.append(("dma_start", (), kw))
            nbytes += 128 * _free_size(out) * (4 if out.dtype == F32 else 2)
        u = self._new_unit(q, instrs, list(r), list(w), "dma", 60.0 * len(pairs))
        u.semof = semof
        u.is_out = is_out
        u.nbytes = nbytes
        return u

    def schedule(self):
        import heapq
        U = self.units
        assert self.pe_group is None
        for u in U:
            u.children = []
            u.nrem = len(u.deps)
            u.ready = 0.0
        for u in U:
            for d in u.deps:
                U[d].children.append(u.id)
        finish = [0.0] * len(U)
        eng_free = {e: 0.0 for e in self.engs}
        avail = {e: [] for e in self.engs}
        pend = {e: [] for e in self.engs}
        for u in U:
            if u.nrem == 0:
                heapq.heappush(avail[u.eng], u.id)
        dma_free = 0.0
        act_cls = None
        byp = [0]
        order = []
        nleft = len(U)
        while nleft:
            best = None
            for e in self.engs:
                pe_, av = pend[e], avail[e]
                while pe_ and pe_[0][0] <= eng_free[e]:
                    heapq.heappush(av, heapq.heappop(pe_)[1])
                if av:
                    cand = (eng_free[e], av[0], e, True)
                elif pe_:
                    cand = (pe_[0][0], pe_[0][1], e, False)
                else:
                    continue
                if best is None or cand[:2] < best[:2]:
                    best = cand
            start, uid, e, from_av = best
            if from_av:
                if e == "act" and len(avail[e]) > 1:
                    head = avail[e][0]
                    hu = U[head]
                    pick = head
                    if not (hu.cls is None or hu.cls == act_cls or act_cls is None) and byp[0] < 4:
                        popped = []
                        pick = None
                        for _ in range(min(10, len(avail[e]))):
                            c = heapq.heappop(avail[e])
                            popped.append(c)
                            if U[c].cls is None or U[c].cls == act_cls:
                                pick = c
                                break
                        for c in popped:
                            if c != pick:
                                heapq.heappush(avail[e], c)
                        if pick is None:
                            pick = heapq.heappop(avail[e])
                            byp[0] = 0
                        else:
                            byp[0] += 1
                    else:
                        heapq.heappop(avail[e])
                        byp[0] = 0
                    uid = pick
                else:
                    heapq.heappop(avail[e])
            else:
                heapq.heappop(pend[e])
            u = U[uid]
            dur = u.dur
            if e == "act" and u.cls is not None:
                if act_cls is not None and u.cls != act_cls:
                    dur += 1300.0
                act_cls = u.cls
            u.start = start
            if u.kind == "dma":
                dma_free = max(dma_free, start) + u.nbytes / 180.0
                finish[uid] = dma_free + 2000.0
                eng_free[e] = start + dur
            else:
                finish[uid] = start + dur
                eng_free[e] = start + dur
            order.append(uid)
            nleft -= 1
            for c in u.children:
                cu = U[c]
                cu.nrem -= 1
                if finish[uid] > cu.ready:
                    cu.ready = finish[uid]
                if cu.nrem == 0:
                    heapq.heappush(pend[cu.eng], (cu.ready, c))
        self.sim_ns = max(finish) if finish else 0.0
        return order

    def emit(self, order):
        U = self.units
        out_tags = []
        for uid in order:
            u = U[uid]
            E = self.engs[u.eng]
            best = {}
            for d in u.deps:
                du = U[d]
                if u.eng == "pe" and du.eng == "pe" and du.kind == "op" and u.kind == "op":
                    continue
                sem, val = du.tag
                key = sem.name
                if key not in best or best[key][1] < val:
                    best[key] = (sem, val)
            for key, (sem, val) in best.items():
                if E.waited.get(key, 0) >= val:
                    continue
                E.h.wait_ge(sem, val)
                E.waited[key] = val
            if u.kind == "dma":
                b = u.semof
                if b.sem is None:
                    b.sem = self.new_sem(b.name)
                for name, a, kw in u.instrs:
                    ins = getattr(E.h, name)(*a, **kw)
                    b.cnt += 16
                    ins.then_inc(b.sem, 16)
                u.tag = (b.sem, b.cnt)
                if u.is_out:
                    out_tags.append(u.tag)
            else:
                ins = None
                for name, a, kw in u.instrs:
                    ins = getattr(E.h, name)(*a, **kw)
                E.count += 1
                ins.then_inc(E.sem, 1)
                u.tag = (E.sem, E.count)
        return out_tags


class Ring:
    def __init__(self, k, name, n, shape, dt):
        self.items = []
        for i in range(n):
            self.items.append((k.sb(f"{name}{i}", shape, dt), Buf(f"{name}{i}")))
        self.i = 0

    def next(self):
        it = self.items[self.i % len(self.items)]
        self.i += 1
        return it


def build_program(SEQ, PAST, DEC):
    TW = 512
    assert SEQ % (2 * TW) == 0 and PAST % TW == 0 and DEC <= 128
    H = SEQ // 2
    nc = bass.Bass("TRN2", target_bir_lowering=False)
    es = ExitStack()
    k = K(nc, es)

    def din(name, shape, dt=F32):
        return nc.dram_tensor(name, list(shape), dt, kind="ExternalInput").ap()

    def dout(name, shape, dt=F32):
        return nc.dram_tensor(name, list(shape), dt, kind="ExternalOutput").ap()

    def dscr(name, shape, dt):
        return nc.dram_tensor(name, list(shape), dt, kind="Internal").ap()

    x_p = din("x_p", [SEQ, D])
    x_s = din("x_s", [DEC, D])
    ck = din("ck", [PAST, D])
    cv = din("cv", [PAST, D])
    s_rc = din("s_rc", [3, D])
    s_h = din("s_h", [1, D])
    s_fc = din("s_fc", [2, DFF])
    ln1 = din("ln1", [1, D])
    w_in = din("w_in", [D, 7 * D])
    cw = din("rnn_conv_w", [4, D])
    cb = din("rnn_conv_b", [1, D])
    lwa = din("lru_wa", [16, 64, 64])
    lba = din("lru_ba", [1, D])
    lwx = din("lru_wx", [16, 64, 64])
    lbx = din("lru_bx", [1, D])
    lam = din("lru_lambda", [1, D])
    qg = din("q_norm_g", [1, HD])
    kg = din("k_norm_g", [1, HD])
    w_pr = din("w_proj_rnn", [D, D])
    w_pa = din("w_proj_attn", [D, D])
    w_out = din("w_out", [D, D])
    ln2 = din("ln2", [1, D])
    w_up = din("w_up", [D, 2 * DFF])
    fw = din("ffn_conv_w", [3, DFF])
    fb = din("ffn_conv_b", [1, DFF])
    w_dn = din("w_down", [DFF, D])
    c_id = din("c_ident", [128, 128], BF16)
    c_L = din("c_L", [128, 128], BF16)
    c_one = din("c_ones", [128, 128], BF16)
    c_msk = din("c_mask", [128, 128], BF16)
    c_flag = din("c_flag", [128, 2])

    outs = {}
    for tag, T in (("p", H), ("s", DEC)):
        outs[tag] = dict(
            y=dout(f"y_{tag}", [T, D]), k=dout(f"k_{tag}", [T, D]), v=dout(f"v_{tag}", [T, D]),
            rc=dout(f"rc_{tag}", [3, D]), h=dout(f"h_{tag}", [1, D]), fc=dout(f"fc_{tag}", [2, DFF]))

    KTs = {"p": dscr("KTs_p", [NH, HD, SEQ], BF16), "s": dscr("KTs_s", [NH, HD, PAST], BF16)}
    Vs = {"p": dscr("Vs_p", [NH, 128, SEQ // 128, HD], BF16),
          "s": dscr("Vs_s", [NH, 128, PAST // 128, HD], BF16)}
    class _Toks(dict):
        def __init__(self, name):
            super().__init__()
            self.nm = name

        def __missing__(self, ci):
            self[ci] = Buf(f"{self.nm}_{ci}")
            return self[ci]

    KTs_b = {"p": _Toks("KTs_p"), "s": _Toks("KTs_s")}
    Vs_b = {"p": _Toks("Vs_p"), "s": _Toks("Vs_s")}

    cst = Buf("consts")
    ident = k.sb("ident", [128, 128], BF16)
    Lneg = k.sb("Lneg", [128, 128], BF16)
    onesneg = k.sb("onesneg", [128, 128], BF16)
    msk = k.sb("msk", [128, 128], BF16)
    ln1T = k.sb("ln1T", [128, 8], F32)
    ln2T = k.sb("ln2T", [128, 8], F32)
    cwT = k.sb("cwT", [128, 8, 4], F32)
    cbT = k.sb("cbT", [128, 8], F32)
    baT = k.sb("baT", [128, 8], F32)
    bxT = k.sb("bxT", [128, 8], F32)
    lamT = k.sb("lamT", [128, 8], F32)
    cA = k.sb("cA", [128, 8], F32)
    epsT = k.sb("epsT", [128, 1], F32)
    onepT = k.sb("onepT", [128, 1], F32)
    nbaT = k.sb("nbaT", [128, 8], F32)
    nbxT = k.sb("nbxT", [128, 8], F32)
    fwT = k.sb("fwT", [128, 24, 3], F32)
    fbT = k.sb("fbT", [128, 24], F32)
    gq = k.sb("gq", [128, HD], F32)
    gk = k.sb("gk", [128, HD], F32)
    wbd = k.sb("wbd", [128, 2, 8, 128], BF16)
    wbd_b = Buf("wbd")

    cpairs = []

    def cdma(out, in_, slow=False):
        cpairs.append((out, in_))

    cdma(ident[:], c_id)
    cdma(Lneg[:], c_L)
    cdma(onesneg[:], c_one)
    cdma(msk[:], c_msk)
    cdma(ln1T[:], ln1.rearrange("o (k p) -> p (o k)", p=128), slow=True)
    cdma(ln2T[:], ln2.rearrange("o (k p) -> p (o k)", p=128), slow=True)
    for i in range(4):
        cdma(cwT[:, :, i], cw[i:i + 1, :].rearrange("o (c p) -> p (o c)", p=128), slow=True)
    cdma(cbT[:], cb.rearrange("o (k p) -> p (o k)", p=128), slow=True)
    cdma(baT[:], lba.rearrange("o (k p) -> p (o k)", p=128), slow=True)
    cdma(bxT[:], lbx.rearrange("o (k p) -> p (o k)", p=128), slow=True)
    cdma(lamT[:], lam.rearrange("o (k p) -> p (o k)", p=128), slow=True)
    for i in range(3):
        cdma(fwT[:, :, i], fw[i:i + 1, :].rearrange("o (c p) -> p (o c)", p=128), slow=True)
    cdma(fbT[:], fb.rearrange("o (k p) -> p (o k)", p=128), slow=True)
    cdma(gq[:], qg.partition_broadcast(128).rearrange("p o d -> p (o d)"))
    cdma(gk[:], kg.partition_broadcast(128).rearrange("p o d -> p (o d)"))
    flagb = k.sb("flagb", [128, 2], F32)
    cdma(flagb[:], c_flag)
    k.dma_multi("sp", cpairs, r=[], w=[cst], semof=cst, slow=True)
    X = k.sb("X", [128, 4, D], F32)
    X_b = Buf("X")
    wbd_f = X[:, 0:2, :].rearrange("p a (c m) -> p a c m", c=8)
    k.op("dve", I("memset", wbd_f, 0.0), w=[wbd_b, X_b])
    for wi, src in enumerate((lwa, lwx)):
        v = src.rearrange("(c two) i o -> two i c o", two=2)
        k.dma("sp", wbd_f[0:64, wi, :, 0:64], v[0], r=[], w=[wbd_b, X_b], semof=wbd_b)
        k.dma("sp", wbd_f[64:128, wi, :, 64:128], v[1], r=[], w=[wbd_b, X_b], semof=wbd_b)
    k.op("dve", I("tensor_copy", out=wbd[:], in_=wbd_f), r=[wbd_b, X_b], w=[wbd_b])
    k.op("act", I("activation", out=cA[:], in_=lamT[:], func=AF.Exp, scale=-1.0), r=[cst], w=[cst])
    k.op("act", I("activation", out=cA[:], in_=cA[:], func=AF.Ln, bias=1.0), r=[cst], w=[cst])
    k.op("dve", I("tensor_scalar", out=cA[:], in0=cA[:], scalar1=-8.0, scalar2=None, op0=ALU.mult),
         r=[cst], w=[cst])
    k.op("dve", I("memset", epsT[:], EPS), w=[cst])
    k.op("dve", I("memset", onepT[:], 1.0000002), w=[cst])
    k.op("dve", I("tensor_scalar", out=nbaT[:], in0=baT[:], scalar1=-1.0, scalar2=None, op0=ALU.mult), r=[cst], w=[cst])
    k.op("dve", I("tensor_scalar", out=nbxT[:], in0=bxT[:], scalar1=-1.0, scalar2=None, op0=ALU.mult), r=[cst], w=[cst])

    banks = [(nc.alloc_psum_tensor(f"ps{i}", [128, 512], F32), Buf(f"ps{i}")) for i in range(2)]
    at_pairs = []
    for i in range(2):
        pt = nc.alloc_psum_tensor(f"atp{i}", [128, 1024], F32)
        ba, bb = Buf(f"atp{i}a"), Buf(f"atp{i}b")
        at_pairs.append((pt, [ba, bb]))
        banks.append((pt[:, 0:512], ba))
        banks.append((pt[:, 512:1024], bb))
    banks += [(nc.alloc_psum_tensor(f"ps{i}", [128, 512], F32), Buf(f"ps{i}")) for i in range(6, 8)]
    mm_ring = banks[0:2]
    mm_i = [0, 0]

    pfx = [False]
    pfx_ring = banks[0:7]
    mm_i.append(0)

    def mmbank():
        if pfx[0]:
            it = pfx_ring[mm_i[2] % len(pfx_ring)]
            mm_i[2] += 1
            return it
        it = mm_ring[mm_i[0] % len(mm_ring)]
        mm_i[0] += 1
        return it

    def atpair():
        it = at_pairs[mm_i[1] % len(at_pairs)]
        mm_i[1] += 1
        return it

    o_banks = banks[6:7]
    tp_t, tp_b = banks[7]
    tp_bf = tp_t[:].bitcast(BF16)

    wring = Ring(k, "wslot", 6, [128, 4096], BF16)
    stat = Ring(k, "stat", 4, [128, 16], F32)
    uT_ring = Ring(k, "uT", 2, [128, 8, TW], BF16)
    QT_ring = Ring(k, "QT", 2, [128, NH, TW], BF16)
    yr_ring = Ring(k, "yrT", 1, [128, 8, TW], BF16)
    KTst = k.sb("KTst", [128, NH, TW], BF16)
    KTst_b = Buf("KTst")
    Vb = k.sb("Vb", [128, 4, D], BF16)
    Vb_b = Buf("Vb")
    sqj = k.sb("sqj", [128, 512], F32)
    sqj_b = Buf("sqj")
    junk = sqj[:].bitcast(BF16)
    junk_b = sqj_b
    kf_ring = Ring(k, "kf", 1, [128, 512], F32)
    vf_ring = Ring(k, "vf", 1, [128, 512], F32)
    nb_ring = Ring(k, "nb", 2, [128, 512], BF16)
    xs_ring = Ring(k, "xs", 2, [128, D], BF16)
    hT = k.sb("hT", [128, 24, TW], BF16)
    m1 = hT[:, 0:16, :].rearrange("p a b -> p (a b)").bitcast(F32).rearrange("p (c t) -> p c t", c=8)
    m1_b = Buf("hT_lo")
    hT_hi_b = Buf("hT_hi")
    yaT = k.sb("yaT", [128, 8, TW], BF16)
    yaT_b = Buf("yaT")
    f32r = Ring(k, "f32r", 6, [128, TW + 4], F32)
    b16r = Ring(k, "b16r", 2, [128, TW], BF16)
    a16r = Ring(k, "a16r", 3, [128, TW], BF16)
    a16p = Ring(k, "a16p", 4, [128, 2 * TW], BF16)
    kc_ring = Ring(k, "kTc", 3, [128, TW], BF16)
    vc_ring = Ring(k, "vc", 3, [128, 4, HD], BF16)
    kin_ring = xs_ring
    xblk_ring = Ring(k, "xblk", 1, [128, D], F32)
    class _ListRing:
        def __init__(self, items):
            self.items = items
            self.i = 0

        def next(self):
            it = self.items[self.i % len(self.items)]
            self.i += 1
            return it

    hT_f32 = hT[:, :, :].rearrange("p a b -> p (a b)").bitcast(F32)
    pf32 = _ListRing(list(f32r.items) + [(hT_f32[:, i * (TW + 4):(i + 1) * (TW + 4)], Buf(f"pf32_{i}"))
                                          for i in range(11)])
    qt0 = QT_ring.items[0][0]
    pb16 = _ListRing(list(b16r.items) + [(qt0[:, i, :], Buf(f"pb16_{i}")) for i in range(NH)])
    alias_bufs = [b for _, b in pf32.items[len(f32r.items):]] + [b for _, b in pb16.items[len(b16r.items):]]

    wscr = {}

    def wcast(key, src_ap, kc, cols, after=None):
        scr = dscr("ws_" + key, [128, kc * cols], BF16)
        sb_ = Buf("ws_" + key)
        k.dma("pool", scr.rearrange("p (k c) -> p k c", k=kc), src_ap, r=([after] if after is not None else []),
              w=[sb_], semof=sb_)
        wscr[key] = (scr, sb_, kc, cols)

    def wload(key, half, k0=0, k1=None):
        scr, sb_, kc, cols = wscr[key]
        k1 = kc if k1 is None else k1
        wt, wb = wring.next()
        view = wt[:, 0:(k1 - k0) * 512].rearrange("p (k c) -> p k c", k=k1 - k0)
        src = scr.rearrange("p (k c) -> p k c", k=kc)[:, k0:k1, half * 512:(half + 1) * 512]
        k.dma("sp", view, src, r=[sb_], w=[wb], semof=wb)
        return view, wb

    w_in_v = w_in.rearrange("(k p) c -> p k c", p=128)
    w_up_v = w_up.rearrange("(k p) c -> p k c", p=128)
    w_pr_v = w_pr.rearrange("(k p) c -> p k c", p=128)
    w_pa_v = w_pa.rearrange("(k p) c -> p k c", p=128)
    w_out_v = w_out.rearrange("(k p) c -> p k c", p=128)
    w_dn_v = w_dn.rearrange("(k p) c -> p k c", p=128)

    SCALE = float(HD) ** -0.5
    def cast_rest(batch, after):
        if batch == 0:
            for gi in (2, 1, 5, 6):
                wcast(f"in{gi}", w_in_v[:, :, gi * D:(gi + 1) * D], 8, D, after)
        elif batch == 1:
            wcast("pr", w_pr_v, 8, D, after)
            wcast("pa", w_pa_v, 8, D, after)
            wcast("out", w_out_v, 8, D, after)
            wcast("up0", w_up_v[:, :, 0:D], 8, D, after)
        elif batch == 2:
            wcast("up3", w_up_v[:, :, DFF:DFF + D], 8, D, after)
            for g3 in (1, 2):
                wcast(f"up{g3}", w_up_v[:, :, g3 * D:(g3 + 1) * D], 8, D, after)
                wcast(f"up{g3 + 3}", w_up_v[:, :, DFF + g3 * D:DFF + (g3 + 1) * D], 8, D, after)
        elif batch == 3:
            for half in range(2):
                wcast(f"dn{half}", w_dn_v[:, :, half * 512:(half + 1) * 512], 24, 512, after)

    for gi in (3, 4, 0):
        wcast(f"in{gi}", w_in_v[:, :, gi * D:(gi + 1) * D], 8, D)

    def rms_to_T(src_fn, nb, P, gT, dstT, dst_b):
        for blk in range(nb):
            xa, xa_b = src_fn(blk)
            st, st_b = stat.next()
            k.op("act", I("activation", out=junk[0:P, :], in_=xa, func=AF.Square,
                                               accum_out=st[0:P, 0:1]), r=[xa_b], w=[junk_b, st_b])
            k.op("act", I("activation", out=st[0:P, 1:2], in_=st[0:P, 0:1], func=AF.Ln,
                          scale=1.0 / D, bias=epsT[0:P, 0:1]), r=[st_b, cst], w=[st_b])
            k.op("act", I("activation", out=st[0:P, 2:3], in_=st[0:P, 1:2], func=AF.Exp, scale=-0.5),
                 r=[st_b], w=[st_b])
            xs, xs_b = xs_ring.next()
            k.op("dve", I("tensor_scalar", out=xs[0:P, :], in0=xa, scalar1=st[0:P, 2:3],
                                                  scalar2=None, op0=ALU.mult), r=[xa_b, st_b], w=[xs_b])
            for kc in range(8):
                k.op("pe", I("transpose", out=tp_bf[:, kc * 128:kc * 128 + P],
                                                 in_=xs[0:P, kc * 128:(kc + 1) * 128], identity=ident[0:P, 0:P]),
                     r=[xs_b, cst], w=[tp_b], inc=(kc == 7))
            src = tp_bf.rearrange("p (k t) -> p k t", k=8)[:, :, 0:P]
            gb = gT[:, :].unsqueeze(2).to_broadcast([128, 8, P])
            k.op("dve", I("tensor_tensor", out=dstT[:, :, blk * 128:blk * 128 + P], in0=src, in1=gb,
                                                  op=ALU.mult), r=[tp_b, cst], w=[dst_b])

    def finish_sigmoid(g, g_b, Wq):
        k.op("dve", I("tensor_scalar", out=g[:, 0:Wq], in0=g[:, 0:Wq], scalar1=1.0, scalar2=None, op0=ALU.add),
             r=[g_b], w=[g_b])
        k.op("dve", I("reciprocal", out=g[:, 0:Wq], in_=g[:, 0:Wq]), r=[g_b], w=[g_b])

    def head_norm(ps, P, grow, dst, dst_b):
        st, st_b = stat.next()
        k.op("act", I("activation", out=sqj[0:P, :], in_=ps[0][0:P, :], func=AF.Square),
             r=[ps[1]], w=[sqj_b])
        k.op("dve", I("tensor_reduce", out=st[0:P, 0:4], in_=sqj[0:P, :].rearrange("p (h d) -> p h d", h=4),
                      axis=AX.X, op=ALU.add), r=[sqj_b], w=[st_b])
        k.op("act", I("activation", out=st[0:P, 4:8], in_=st[0:P, 0:4], func=AF.Ln,
                      scale=1.0 / HD, bias=epsT[0:P, 0:1]), r=[st_b, cst], w=[st_b])
        k.op("act", I("activation", out=st[0:P, 8:12], in_=st[0:P, 4:8], func=AF.Exp, scale=-0.5),
             r=[st_b], w=[st_b])
        for h in range(4):
            k.op("dve", I("scalar_tensor_tensor", out=dst[0:P, h * 128:(h + 1) * 128],
                          in0=ps[0][0:P, h * 128:(h + 1) * 128], scalar=st[0:P, 8 + h:9 + h], in1=grow[0:P, :],
                          op0=ALU.mult, op1=ALU.mult), r=[ps[1], st_b, cst], w=[dst_b])

    def to_T(src, src_b, P, dstT, dst_b, blk, h0=0, nh=8, on_dve=False):
        for h in range(nh):
            k.op("pe", I("transpose", out=tp_bf[:, h * 128:h * 128 + P], in_=src[0:P, h * 128:(h + 1) * 128],
                         identity=ident[0:P, 0:P]), r=[src_b, cst], w=[tp_b], inc=(h == nh - 1))
        s3 = tp_bf[:, 0:nh * 128].rearrange("p (k t) -> p k t", k=nh)[:, :, 0:P]
        if on_dve:
            k.op("dve", I("tensor_copy", out=dstT[:, h0:h0 + nh, blk * 128:blk * 128 + P], in_=s3), r=[tp_b], w=[dst_b])
        else:
            k.op("act", I("activation", out=dstT[:, h0:h0 + nh, blk * 128:blk * 128 + P], in_=s3, func=AF.Copy),
                 r=[tp_b], w=[dst_b])

    def run_seq(tag, x_ap, past, tiles, init, boundary=None, between=None):
        o = outs[tag]
        hist_rc = k.sb(f"hrc_{tag}", [128, 8, 3], F32)
        hstate = k.sb(f"hst_{tag}", [128, 8], F32)
        hist_fc = k.sb(f"hfc_{tag}", [128, 24, 2], F32)
        st_b = Buf(f"state_{tag}")
        fc_b = Buf(f"fstate_{tag}")
        if init is None:
            k.op("dve", I("memset", hist_rc[:], 0.0), w=[st_b])
            k.op("dve", I("memset", hstate[:], 0.0), w=[st_b])
            k.op("dve", I("memset", hist_fc[:], 0.0), w=[fc_b])
        else:
            for i in range(3):
                k.dma("sp", hist_rc[:, :, i], init[0][i:i + 1, :].rearrange("o (c p) -> p (o c)", p=128), r=[],
                      w=[st_b], semof=st_b, slow=True)
            k.dma("sp", hstate[:], init[1].rearrange("o (c p) -> p (o c)", p=128), r=[], w=[st_b], semof=st_b, slow=True)
            for i in range(2):
                k.dma("sp", hist_fc[:, :, i], init[2][i:i + 1, :].rearrange("o (c p) -> p (o c)", p=128), r=[],
                      w=[fc_b], semof=fc_b, slow=True)
        nfull = 0
        for ti, T in enumerate(tiles):
            T["nb"] = (T["W"] + 127) // 128
            T["P"] = min(T["W"], 128)
            T["pos0"] = past + T["t0"]
            T["last"] = ti == len(tiles) - 1
            T["via_scratch"] = T["W"] % 128 == 0
            T.setdefault("halo", False)
            if T["full"]:
                T["par"] = nfull % 2
                nfull += 1
            else:
                T["par"] = 0

        def load_x(T):
            k.dma("sp", X[0:T["P"], 0:T["nb"], :],
                  x_ap[T["t0"]:T["t0"] + T["W"], :].rearrange("(b p) c -> p b c", p=T["P"]), r=[], w=[X_b], semof=X_b)

        def pre(T):
            k.phase = f"{tag}{T['t0']}:pre"
            t0, Wq, nb, P, pos0 = T["t0"], T["W"], T["nb"], T["P"], T["pos0"]
            full, out0 = T["full"], T["out0"]
            uT, uT_b = uT_ring.items[T["par"]]
            QT, QT_b = QT_ring.items[T["par"]]
            yrT, yrT_b = yr_ring.items[0]
            def xin(blk):
                xb_t, xb_b = xblk_ring.next()
                r0 = t0 + blk * 128
                k.dma("sp", xb_t[0:P, :], x_ap[r0:r0 + P, :], r=[], w=[xb_b], semof=xb_b)
                return xb_t[0:P, :], xb_b
            rms_to_T(xin, nb, P, ln1T, uT, uT_b)
            groups = ((2, "q"), (3, "k"), (4, "v")) if full else ((3, "k"), (4, "v"))
            for gi, gname in groups:
                for half in range(2):
                    wv, wb = wload(f"in{gi}", half)
                    for blk in range(nb):
                        ps = mmbank()
                        for kc in range(8):
                            k.op("pe", I("matmul", ps[0][0:P, :], lhsT=uT[:, kc, blk * 128:blk * 128 + P],
                                         rhs=wv[:, kc, :], start=(kc == 0), stop=(kc == 7)),
                                 r=[uT_b, wb], w=[ps[1]], inc=(kc == 7))
                        r0 = (out0 + blk * 128) if out0 is not None else None
                        cs = slice(half * 512, (half + 1) * 512)
                        if gname == "q":
                            nbf, nbf_b = nb_ring.next()
                            head_norm(ps, P, gq, nbf, nbf_b)
                            to_T(nbf, nbf_b, P, QT, QT_b, blk, h0=4 * half, nh=4, on_dve=full)
                        elif gname == "k":
                            nbf, nbf_b = nb_ring.next()
                            if out0 is not None:
                                kf, kf_b = kf_ring.next()
                                head_norm(ps, P, gk, kf, kf_b)
                                k.op("dve", I("tensor_copy", out=nbf[0:P, :], in_=kf[0:P, :]), r=[kf_b], w=[nbf_b])
                                k.dma("sp", o["k"][r0:r0 + P, cs], kf[0:P, :], r=[kf_b], w=[], semof=kf_b, is_out=True)
                            else:
                                head_norm(ps, P, gk, nbf, nbf_b)
                            to_T(nbf, nbf_b, P, KTst, KTst_b, blk, h0=4 * half, nh=4, on_dve=full)
                        else:
                            if out0 is not None:
                                vf, vf_b = vf_ring.next()
                                k.op("act", I("activation", out=vf[0:P, :], in_=ps[0][0:P, :], func=AF.Copy),
                                     r=[ps[1]], w=[vf_b])
                                k.op("dve", I("tensor_copy", out=Vb[0:P, blk, cs], in_=vf[0:P, :]), r=[vf_b], w=[Vb_b])
                                k.dma("sp", o["v"][r0:r0 + P, cs], vf[0:P, :], r=[vf_b], w=[], semof=vf_b, is_out=True)
                            else:
                                k.op("act", I("activation", out=Vb[0:P, blk, cs], in_=ps[0][0:P, :], func=AF.Copy),
                                     r=[ps[1]], w=[Vb_b])
                        yield 2.0
            if T["via_scratch"]:
                k.dma("sp", KTs[tag].rearrange("h d t -> d h t")[:, :, pos0:pos0 + Wq], KTst[:, :, 0:Wq],
                      r=[KTst_b], w=[KTs_b[tag][pos0 // TW]], semof=KTs_b[tag][pos0 // TW])
                k.dma_multi("sp", [(Vs[tag][h, :, pos0 // 128:pos0 // 128 + nb, :], Vb[:, 0:nb, h * 128:(h + 1) * 128])
                                   for h in range(NH)], r=[Vb_b], w=[Vs_b[tag][pos0 // TW]], semof=Vs_b[tag][pos0 // TW])

            yield 0.1

        def pre_rnn(T):
            k.phase = f"{tag}{T['t0']}:rnn"
            Wq, full = T["W"], T["full"]
            uT, uT_b = uT_ring.items[T["par"]]
            yrT, yrT_b = yr_ring.items[0]
            fr = pf32 if pfx[0] else f32r
            br = pb16 if pfx[0] else b16r
            for c in range(8):
                cl = (c % 4) * 128
                if c % 4 == 0:
                    wxr, wxr_b = wload("in0", c // 4)
                    if full:
                        wgr, wgr_b = wload("in1", c // 4)
                ps = mmbank()
                for kc in range(8):
                    k.op("pe", I("matmul", ps[0][:, 0:Wq], lhsT=wxr[:, kc, cl:cl + 128],
                                 rhs=uT[:, kc, 0:Wq], start=(kc == 0), stop=(kc == 7)),
                         r=[uT_b, wxr_b], w=[ps[1]], inc=(kc == 7))
                xp, xp_b = fr.next()
                k.op("dve", I("tensor_copy", out=xp[:, 0:3], in_=hist_rc[:, c, :]), r=[st_b], w=[xp_b])
                if full:
                    k.op("dve", I("tensor_copy", out=xp[:, 3:3 + Wq], in_=ps[0][:, 0:Wq]), r=[ps[1]], w=[xp_b])
                else:
                    k.op("act", I("activation", out=xp[:, 3:3 + Wq], in_=ps[0][:, 0:Wq], func=AF.Copy),
                         r=[ps[1]], w=[xp_b])
                k.op("dve", I("tensor_copy", out=hist_rc[:, c, :], in_=xp[:, Wq:Wq + 3]), r=[xp_b], w=[st_b])
                xc, xc_b = fr.next()
                k.op("dve", I("tensor_scalar", out=xc[:, 0:Wq], in0=xp[:, 0:Wq], scalar1=cwT[:, c, 0:1],
                              scalar2=cbT[:, c:c + 1], op0=ALU.mult, op1=ALU.add), r=[xp_b, cst], w=[xc_b])
                for i in range(1, 4):
                    k.op("dve", I("scalar_tensor_tensor", out=xc[:, 0:Wq], in0=xp[:, i:i + Wq],
                                  scalar=cwT[:, c, i:i + 1], in1=xc[:, 0:Wq], op0=ALU.mult, op1=ALU.add),
                         r=[xp_b, xc_b, cst], w=[xc_b])
                xcb, xcb_b = br.next()
                k.op("dve", I("tensor_copy", out=xcb[:, 0:Wq], in_=xc[:, 0:Wq]), r=[xc_b], w=[xcb_b])
                gts = []
                for wi, bT in ((0, baT), (1, bxT)):
                    psg = mmbank()
                    k.op("pe", I("matmul", psg[0][:, 0:Wq], lhsT=wbd[:, wi, c, :], rhs=xcb[:, 0:Wq],
                                 start=True, stop=True), r=[xcb_b, wbd_b], w=[psg[1]])
                    g, g_b = fr.next()
                    if full:
                        nbT = nbaT if wi == 0 else nbxT
                        k.op("act", I("activation", out=g[:, 0:Wq], in_=psg[0][:, 0:Wq], func=AF.Exp, scale=-1.0,
                                      bias=nbT[:, c:c + 1]), r=[psg[1], cst], w=[g_b])
                        finish_sigmoid(g, g_b, Wq)
                    else:
                        k.op("act", I("activation", out=g[:, 0:Wq], in_=psg[0][:, 0:Wq], func=AF.Sigmoid,
                                      bias=bT[:, c:c + 1]), r=[psg[1], cst], w=[g_b])
                    gts.append((g, g_b))
                (rg, rg_b), (ig, ig_b) = gts
                k.op("act", I("activation", out=rg[:, 0:Wq], in_=rg[:, 0:Wq], func=AF.Exp, scale=cA[:, c:c + 1]),
                     r=[rg_b, cst], w=[rg_b])
                sq, sq_b = fr.next()
                k.op("dve", I("tensor_tensor", out=sq[:, 0:Wq], in0=rg[:, 0:Wq], in1=rg[:, 0:Wq], op=ALU.mult),
                     r=[rg_b], w=[sq_b])
                k.op("act", I("activation", out=sq[:, 0:Wq], in_=sq[:, 0:Wq], func=AF.Ln, scale=-1.0,
                              bias=onepT[:, 0:1]), r=[sq_b, cst], w=[sq_b])
                k.op("act", I("activation", out=sq[:, 0:Wq], in_=sq[:, 0:Wq], func=AF.Exp, scale=0.5),
                     r=[sq_b], w=[sq_b])
                k.op("dve", I("tensor_tensor", out=ig[:, 0:Wq], in0=ig[:, 0:Wq], in1=xc[:, 0:Wq], op=ALU.mult),
                     r=[ig_b, xc_b], w=[ig_b])
                k.op("dve", I("tensor_tensor", out=ig[:, 0:Wq], in0=ig[:, 0:Wq], in1=sq[:, 0:Wq], op=ALU.mult),
                     r=[ig_b, sq_b], w=[ig_b])
                hs, hs_b = sq, sq_b
                k.op("dve", I("tensor_tensor_scan", out=hs[:, 0:Wq], data0=rg[:, 0:Wq], data1=ig[:, 0:Wq],
                              initial=hstate[:, c:c + 1], op0=ALU.mult, op1=ALU.add),
                     r=[rg_b, ig_b, st_b], w=[hs_b])
                k.op("dve", I("tensor_copy", out=hstate[:, c:c + 1], in_=hs[:, Wq - 1:Wq]), r=[hs_b], w=[st_b])
                if full:
                    ps2 = mmbank()
                    for kc in range(8):
                        k.op("pe", I("matmul", ps2[0][:, 0:Wq], lhsT=wgr[:, kc, cl:cl + 128],
                                     rhs=uT[:, kc, 0:Wq], start=(kc == 0), stop=(kc == 7)),
                             r=[uT_b, wgr_b], w=[ps2[1]], inc=(kc == 7))
                    gl, gl_b = xp, xp_b
                    k.op("act", I("activation", out=gl[:, 0:Wq], in_=ps2[0][:, 0:Wq], func=AF.Gelu_apprx_tanh),
                         r=[ps2[1]], w=[gl_b])
                    k.op("dve", I("tensor_tensor", out=yrT[:, c, 0:Wq], in0=hs[:, 0:Wq], in1=gl[:, 0:Wq], op=ALU.mult),
                         r=[hs_b, gl_b], w=[yrT_b])
                yield 6.0
            if T["halo"]:
                k.op("dve", I("tensor_scalar", out=hstate[:], in0=hstate[:], scalar1=flagb[:, 0:1], scalar2=None,
                              op0=ALU.mult), r=[st_b, cst], w=[st_b])
                hr2 = hist_rc[:].rearrange("p c i -> p (c i)")
                k.op("dve", I("tensor_scalar", out=hr2, in0=hr2, scalar1=flagb[:, 0:1], scalar2=None, op0=ALU.mult),
                     r=[st_b, cst], w=[st_b])

        def attn(T):
            k.phase = f"{tag}{T['t0']}:attn"
            Wq, nb, pos0 = T["W"], T["nb"], T["pos0"]
            QT, QT_b = QT_ring.items[T["par"]]
            own_tile = boundary is not None and pos0 >= boundary
            chunks = [(c0, min(c0 + TW, pos0)) for c0 in range(0, pos0, TW)]
            for h in range(NH):
                o_t, o_b = o_banks[h % len(o_banks)]
                if T["via_scratch"]:
                    blist = [("diag", (pos0, pos0 + Wq), jj, 128) for jj in reversed(range(nb))]
                else:
                    blist = [("own", None, jj, min(128, Wq - jj * 128)) for jj in reversed(range(nb))]
                for (c0, c1) in reversed(chunks):
                    for jj in reversed(range((c1 - c0) // 128)):
                        blist.append(("past", (c0, c1), jj, 128))
                groups = []
                gi0 = 0
                while gi0 < len(blist):
                    if gi0 + 1 < len(blist) and blist[gi0][3] == 128 and blist[gi0 + 1][3] == 128:
                        groups.append(blist[gi0:gi0 + 2])
                        gi0 += 2
                    else:
                        groups.append(blist[gi0:gi0 + 1])
                        gi0 += 1
                Rprev = None
                cur_chunk = None
                bi = 0
                for grp in groups:
                    G = len(grp)
                    nk = grp[0][3]
                    zt, zbufs = atpair()
                    zbufs = zbufs[0:G]
                    infos = []
                    gbias = "unset"
                    for gi_, (kind, ch, jj, nk_) in enumerate(grp):
                        bias = None
                        if kind == "own":
                            lhs_k, lhs_k_b = KTst[:, h, jj * 128:jj * 128 + nk], KTst_b
                            lhs_v, lhs_v_b = Vb[0:nk, jj, h * 128:(h + 1) * 128], Vb_b
                        else:
                            if ch != cur_chunk:
                                cur_chunk = ch
                                c0, c1 = ch
                                kTc, kTc_b = kc_ring.next()
                                vc, vc_b = vc_ring.next()
                                k.dma("sp", kTc[:, 0:c1 - c0], KTs[tag][h, :, c0:c1], r=[KTs_b[tag][c0 // TW]],
                                      w=[kTc_b], semof=kTc_b)
                                k.dma("sp", vc[:, 0:(c1 - c0) // 128, :], Vs[tag][h, :, c0 // 128:c1 // 128, :],
                                      r=[Vs_b[tag][c0 // TW]], w=[vc_b], semof=vc_b)
                            lhs_k, lhs_k_b = kTc[:, jj * 128:(jj + 1) * 128], kTc_b
                            lhs_v, lhs_v_b = vc[:, jj, :], vc_b
                            if kind == "past" and own_tile and ch[0] < boundary:
                                bias = flagb[0:nk, 1:2]
                        assert gbias == "unset" or (gbias is None) == (bias is None)
                        gbias = bias
                        zv = zt[0:nk, gi_ * 512:gi_ * 512 + Wq]
                        k.op("pe", I("matmul", zv, lhsT=lhs_k, rhs=QT[:, h, 0:Wq], start=True, stop=True),
                             r=[lhs_k_b, QT_b], w=zbufs, inc=(gi_ == G - 1))
                        infos.append((kind, jj, lhs_v, lhs_v_b, zv))

                    def v3(t):
                        return t[0:nk, :].rearrange("p (b c) -> p b c", b=2)[:, 0:G, 0:Wq]

                    def mask(t, t_b):
                        for gi_, (kind, jj, _, _, _) in enumerate(infos):
                            if kind == "past":
                                continue
                            base = gi_ * 512
                            if jj > 0:
                                k.op("pool", I("memset", t[0:nk, base:base + jj * 128], 0.0), r=[t_b], w=[t_b])
                            qd = min(128, Wq - jj * 128)
                            k.op("pool", I("tensor_tensor", out=t[0:nk, base + jj * 128:base + jj * 128 + qd],
                                           in0=t[0:nk, base + jj * 128:base + jj * 128 + qd], in1=msk[0:nk, 0:qd],
                                           op=ALU.mult), r=[t_b, cst], w=[t_b])

                    ee, ee_b = a16p.next()
                    ekw = dict(out=v3(ee), in_=v3(zt), func=AF.Exp, scale=SCALE)
                    if gbias is not None:
                        ekw["bias"] = gbias
                    k.op("act", I("activation", **ekw), r=zbufs + [cst], w=[ee_b])
                    mask(ee, ee_b)
                    ss, ss_b = a16p.next()
                    k.op("act", I("activation", out=v3(ss), in_=v3(ee), func=AF.Ln, bias=1.0), r=[ee_b], w=[ss_b])
                    for gi_, (kind, jj, lhs_v, lhs_v_b, zv) in enumerate(infos):
                        ssl = ss[0:nk, gi_ * 512:gi_ * 512 + Wq]
                        lastb = bi == len(blist) - 1
                        k.op("pe", I("matmul", zv, lhsT=Lneg[0:nk, 0:nk], rhs=ssl, start=False,
                                     stop=(Rprev is None), skip_group_check=True),
                             r=[ss_b, cst] + zbufs, w=zbufs, inc=(Rprev is None))
                        if Rprev is not None:
                            rp, rp_b, rn = Rprev
                            k.op("pe", I("matmul", zv, lhsT=onesneg[0:rn, 0:nk], rhs=rp[0:rn, 0:Wq], start=False,
                                         stop=True, skip_group_check=True), r=[rp_b, cst] + zbufs, w=zbufs)
                        if not lastb:
                            rnw, rnw_b = a16r.next()
                            if Rprev is None:
                                if nk < 128:
                                    k.op("dve", I("memset", rnw[:, 0:Wq], 0.0), w=[rnw_b])
                                k.op("dve", I("tensor_copy", out=rnw[0:nk, 0:Wq], in_=ssl), r=[ss_b], w=[rnw_b])
                                Rprev = (rnw, rnw_b, 128)
                            else:
                                rp, rp_b, rn = Rprev
                                k.op("dve", I("tensor_tensor", out=rnw[0:rn, 0:Wq], in0=rp[0:rn, 0:Wq],
                                              in1=ss[0:rn, gi_ * 512:gi_ * 512 + Wq], op=ALU.add),
                                     r=[rp_b, ss_b], w=[rnw_b])
                                Rprev = (rnw, rnw_b, rn)
                        bi += 1
                    ww, ww_b = ee, ee_b
                    wkw = dict(out=v3(ww), in_=v3(zt), func=AF.Exp, scale=SCALE)
                    if gbias is not None:
                        wkw["bias"] = gbias
                    k.op("act", I("activation", **wkw), r=zbufs + [cst, ss_b], w=[ww_b])
                    mask(ww, ww_b)
                    for gi_, (kind, jj, lhs_v, lhs_v_b, zv) in enumerate(infos):
                        first = (bi - G + gi_) == 0
                        lastb = (bi - G + gi_) == len(blist) - 1
                        k.op("pe", I("matmul", o_t[:, 0:Wq], lhsT=lhs_v, rhs=ww[0:nk, gi_ * 512:gi_ * 512 + Wq],
                                     start=first, stop=lastb), r=[lhs_v_b, ww_b], w=[o_b])
                    yield 1.5 * G
                k.op("dve", I("tensor_copy", out=yaT[:, h, 0:Wq], in_=o_t[:, 0:Wq]), r=[o_b], w=[yaT_b])

        def merge(T):
            k.phase = f"{tag}{T['t0']}:merge"
            t0, Wq, nb, P = T["t0"], T["W"], T["nb"], T["P"]
            out0 = T["out0"]
            uT, uT_b = uT_ring.items[T["par"]]
            yrT, yrT_b = yr_ring.items[0]
            mT, mT_b = yrT, yrT_b
            for pi, (key, wsrc, gi, yT, yT_b) in enumerate((("pr", w_pr_v, 5, yrT, yrT_b), ("pa", w_pa_v, 6, yaT, yaT_b))):
                for c in range(8):
                    cl = (c % 4) * 128
                    if c % 4 == 0:
                        wp, wp_b = wload(key, c // 4)
                        wg, wg_b = wload(f"in{gi}", c // 4)
                    psp = mmbank()
                    for kc in range(8):
                        k.op("pe", I("matmul", psp[0][:, 0:Wq], lhsT=wp[:, kc, cl:cl + 128],
                                     rhs=yT[:, kc, 0:Wq], start=(kc == 0), stop=(kc == 7)),
                             r=[yT_b, wp_b], w=[psp[1]], inc=(kc == 7))
                    psg = mmbank()
                    for kc in range(8):
                        k.op("pe", I("matmul", psg[0][:, 0:Wq], lhsT=wg[:, kc, cl:cl + 128],
                                     rhs=uT[:, kc, 0:Wq], start=(kc == 0), stop=(kc == 7)),
                             r=[uT_b, wg_b], w=[psg[1]], inc=(kc == 7))
                    sg, sg_b = f32r.next()
                    k.op("act", I("activation", out=sg[:, 0:Wq], in_=psg[0][:, 0:Wq], func=AF.Exp, scale=-1.0),
                         r=[psg[1]], w=[sg_b])
                    finish_sigmoid(sg, sg_b, Wq)
                    if pi == 0:
                        k.op("dve", I("tensor_tensor", out=m1[:, c, 0:Wq], in0=psp[0][:, 0:Wq], in1=sg[:, 0:Wq],
                                      op=ALU.mult), r=[psp[1], sg_b], w=[m1_b])
                    else:
                        k.op("dve", I("tensor_tensor", out=sg[:, 0:Wq], in0=psp[0][:, 0:Wq], in1=sg[:, 0:Wq],
                                      op=ALU.mult), r=[psp[1], sg_b], w=[sg_b])
                        k.op("pool", I("tensor_tensor", out=mT[:, c, 0:Wq], in0=sg[:, 0:Wq], in1=m1[:, c, 0:Wq],
                                       op=ALU.add), r=[sg_b, m1_b], w=[mT_b])
                    yield 4.0
        def post_rest(T):
            k.phase = f"{tag}{T['t0']}:post"
            t0, Wq, nb, P = T["t0"], T["W"], T["nb"], T["P"]
            out0 = T["out0"]
            yrT, yrT_b = yr_ring.items[0]
            mT, mT_b = yrT, yrT_b
            load_x(T)
            for half in range(2):
                wo, wo_b = wload("out", half)
                for blk in range(nb):
                    ps = mmbank()
                    for kc in range(8):
                        k.op("pe", I("matmul", ps[0][0:P, :], lhsT=mT[:, kc, blk * 128:blk * 128 + P],
                                     rhs=wo[:, kc, :], start=(kc == 0), stop=(kc == 7)),
                             r=[mT_b, wo_b], w=[ps[1]], inc=(kc == 7))
                    k.op("dve", I("tensor_tensor", out=X[0:P, blk, half * 512:(half + 1) * 512], in0=ps[0][0:P, :],
                                  in1=X[0:P, blk, half * 512:(half + 1) * 512], op=ALU.add), r=[ps[1], X_b], w=[X_b])
                    yield 2.0
            u2T, u2T_b = yrT, yrT_b
            rms_to_T(lambda blk: (X[0:P, blk, :], X_b), nb, P, ln2T, u2T, u2T_b)
            for g3 in range(3):
                for cc in range(8):
                    c = g3 * 8 + cc
                    cl = (cc % 4) * 128
                    if cc % 4 == 0:
                        wga, wga_b = wload(f"up{g3}", cc // 4)
                        if out0 is not None:
                            wva, wva_b = wload(f"up{g3 + 3}", cc // 4)
                    psa = mmbank()
                    for kc in range(8):
                        k.op("pe", I("matmul", psa[0][:, 0:Wq], lhsT=wga[:, kc, cl:cl + 128],
                                     rhs=u2T[:, kc, 0:Wq], start=(kc == 0), stop=(kc == 7)),
                             r=[u2T_b, wga_b], w=[psa[1]], inc=(kc == 7))
                    gp, gp_b = f32r.next()
                    k.op("dve", I("tensor_copy", out=gp[:, 0:2], in_=hist_fc[:, c, :]), r=[fc_b], w=[gp_b])
                    k.op("dve", I("tensor_copy", out=gp[:, 2:2 + Wq], in_=psa[0][:, 0:Wq]), r=[psa[1]], w=[gp_b])
                    k.op("dve", I("tensor_copy", out=hist_fc[:, c, :], in_=gp[:, Wq:Wq + 2]), r=[gp_b], w=[fc_b])
                    if out0 is None:
                        yield 2.0
                        continue
                    psv = mmbank()
                    for kc in range(8):
                        k.op("pe", I("matmul", psv[0][:, 0:Wq], lhsT=wva[:, kc, cl:cl + 128],
                                     rhs=u2T[:, kc, 0:Wq], start=(kc == 0), stop=(kc == 7)),
                             r=[u2T_b, wva_b], w=[psv[1]], inc=(kc == 7))
                    gc, gc_b = f32r.next()
                    k.op("dve", I("tensor_scalar", out=gc[:, 0:Wq], in0=gp[:, 0:Wq], scalar1=fwT[:, c, 0:1],
                                  scalar2=fbT[:, c:c + 1], op0=ALU.mult, op1=ALU.add), r=[gp_b, cst], w=[gc_b])
                    for i in range(1, 3):
                        k.op("dve", I("scalar_tensor_tensor", out=gc[:, 0:Wq], in0=gp[:, i:i + Wq],
                                      scalar=fwT[:, c, i:i + 1], in1=gc[:, 0:Wq], op0=ALU.mult, op1=ALU.add),
                             r=[gp_b, gc_b, cst], w=[gc_b])
                    k.op("act", I("activation", out=gc[:, 0:Wq], in_=gc[:, 0:Wq], func=AF.Gelu_apprx_tanh),
                         r=[gc_b], w=[gc_b])
                    k.op("dve", I("tensor_tensor", out=hT[:, c, 0:Wq], in0=psv[0][:, 0:Wq], in1=gc[:, 0:Wq],
                                  op=ALU.mult), r=[psv[1], gc_b], w=[m1_b if c < 16 else hT_hi_b])
                    yield 4.0
            if T["halo"]:
                hf2 = hist_fc[:].rearrange("p c i -> p (c i)")
                k.op("dve", I("tensor_scalar", out=hf2, in0=hf2, scalar1=flagb[:, 0:1], scalar2=None, op0=ALU.mult),
                     r=[fc_b, cst], w=[fc_b])
            if out0 is None:
                return
            for half in range(2):
                wds = [wload(f"dn{half}", 0, 8 * j, 8 * j + 8) for j in range(3)]
                for blk in range(nb):
                    ps = mmbank()
                    for kc in range(24):
                        wsl, wsl_b = wds[kc // 8]
                        k.op("pe", I("matmul", ps[0][0:P, :], lhsT=hT[:, kc, blk * 128:blk * 128 + P],
                                     rhs=wsl[:, kc % 8, :], start=(kc == 0), stop=(kc == 23)),
                             r=[m1_b, hT_hi_b, wsl_b], w=[ps[1]], inc=(kc == 23))
                    k.op("dve", I("tensor_tensor", out=X[0:P, blk, half * 512:(half + 1) * 512], in0=ps[0][0:P, :],
                                  in1=X[0:P, blk, half * 512:(half + 1) * 512], op=ALU.add), r=[ps[1], X_b], w=[X_b])
                    yield 2.0
            k.dma("sp", o["y"][out0:out0 + Wq, :].rearrange("(b p) c -> p b c", p=P), X[0:P, 0:nb, :],
                  r=[X_b], w=[], semof=X_b, is_out=True)

        def drain(g):
            for _ in g:
                pass

        def n_blocks(T):
            return NH * (T["nb"] + sum((min(c0 + TW, T["pos0"]) - c0) // 128 for c0 in range(0, T["pos0"], TW)))

        def w_pre(T):
            return 2.0 * (3 if T["full"] else 2) * 2 * T["nb"] + 0.1

        def w_post(T):
            if T["out0"] is None:
                return 2.0 * 2 * T["nb"] + 24 * 2.0
            return 2.0 * 2 * T["nb"] + 24 * 4.0 + 2.0 * 2 * T["nb"]

        def interleave(ga, wa, gbs, wb):
            gb = (x for g in gbs for x in g)
            ca = cb = 0.0
            da = db = False
            while not (da and db):
                if db or (not da and ca * wb <= cb * wa):
                    try:
                        ca += next(ga)
                    except StopIteration:
                        da = True
                else:
                    try:
                        cb += next(gb)
                    except StopIteration:
                        db = True

        fulls = [T for T in tiles if T["full"]]
        npref = 0
        for T in tiles:
            if not T["full"]:
                pfx[0] = True
                drain(pre(T))
                if tag == "p":
                    cast_rest(npref, uT_ring.items[T["par"]][1])
                npref += 1
                drain(pre_rnn(T))
                pfx[0] = False
        if tag == "p":
            for j in range(npref, 4):
                cast_rest(j, None)
            if npref:
                st0, st0_b = stat.next()
                k.op("dve", I("memset", st0[0:1, 0:1], 0.0), r=alias_bufs,
                     w=[st0_b, m1_b, hT_hi_b, QT_ring.items[0][1]])
        if between is not None:
            between()
        if fulls:
            drain(pre(fulls[0]))
            drain(pre_rnn(fulls[0]))
        import itertools
        for j, T in enumerate(fulls):
            ga = attn(T)
            nbh = n_blocks(T) // NH
            if j > 0:
                head0 = itertools.islice(ga, nbh - 1)
                interleave(head0, 1.5 * (nbh - 1), [merge(fulls[j - 1])], 64.0)
            gbs, wb = [], 0.0
            if j > 0:
                gbs.append(post_rest(fulls[j - 1]))
                wb += w_post(fulls[j - 1])
                gbs.append(pre_rnn(T))
                wb += 48.0
            if j + 1 < len(fulls):
                gbs.append(pre(fulls[j + 1]))
                wb += w_pre(fulls[j + 1])
            interleave(ga, 1.5 * (n_blocks(T) - (nbh - 1 if j > 0 else 0)), gbs, max(wb, 1.0))
        if fulls:
            drain(merge(fulls[-1]))
            drain(post_rest(fulls[-1]))

        for i in range(3):
            k.dma("sp", o["rc"][i:i + 1, :].rearrange("o (c p) -> p (o c)", p=128), hist_rc[:, :, i], r=[st_b], w=[],
                  semof=st_b, slow=True, is_out=True)
        k.dma("sp", o["h"].rearrange("o (c p) -> p (o c)", p=128), hstate[:], r=[st_b], w=[], semof=st_b, slow=True,
              is_out=True)
        for i in range(2):
            k.dma("sp", o["fc"][i:i + 1, :].rearrange("o (c p) -> p (o c)", p=128), hist_fc[:, :, i], r=[fc_b], w=[],
                  semof=fc_b, slow=True, is_out=True)

    for blk in range(PAST // 128):
        kin, kin_b = kin_ring.next()
        k.dma("pool", kin[:, :], ck[blk * 128:(blk + 1) * 128, :], r=[], w=[kin_b], semof=kin_b)
        to_T(kin, kin_b, 128, KTst, KTst_b, blk % 4)
        if blk % 4 == 3:
            c0 = (blk // 4) * TW
            k.dma("sp", KTs["s"].rearrange("h d t -> d h t")[:, :, c0:c0 + TW], KTst[:, :, :],
                  r=[KTst_b], w=[KTs_b["s"][blk // 4]], semof=KTs_b["s"][blk // 4])
    cv_v = cv.rearrange("(b p) (h d) -> h p b d", p=128, h=NH)
    for ci in range(PAST // TW):
        k.dma_multi("pool", [(Vs["s"][h, :, ci * 4:ci * 4 + 4, :], cv_v[h, :, ci * 4:ci * 4 + 4, :]) for h in range(NH)],
                    r=[], w=[Vs_b["s"][ci]], semof=Vs_b["s"][ci])

    tiles_p = []
    t = 0
    while t < H - 128:
        w_ = min(TW, H - 128 - t)
        tiles_p.append(dict(t0=t, W=w_, full=False, out0=None))
        t += w_
    tiles_p.append(dict(t0=H - 128, W=128, full=True, out0=None, halo=True))
    for j in range(H // TW):
        tiles_p.append(dict(t0=H + TW * j, W=TW, full=True, out0=TW * j))
    run_seq("p", x_p, 0, tiles_p, None, boundary=H,
            between=lambda: run_seq("s", x_s, PAST, [dict(t0=0, W=DEC, full=True, out0=0)], (s_rc, s_h, s_fc)))

    order = k.schedule()
    out_tags = k.emit(order)
    E = k.engs["sp"]
    best = {}
    for sem, val in out_tags:
        if sem.name not in best or best[sem.name][1] < val:
            best[sem.name] = (sem, val)
    for sem, val in best.values():
        E.h.wait_ge(sem, val)
    k.stats = dict(nsem=k.nsem, sb_bytes=k.sb_bytes, sim_us=k.sim_ns / 1e3, nunits=len(k.units),
                   counts={n: e.count for n, e in k.engs.items()})
    es.close()
    return nc, k.stats


def host_consts():
    bf = ml_dtypes.bfloat16
    ident = np.eye(128, dtype=np.float32).astype(bf)
    kk = np.arange(128)
    inv = np.float32(np.sqrt(np.float32(HD)))
    L = (-inv * (kk[:, None] >= kk[None, :]).astype(np.float32)).astype(bf)
    ones = (-inv * np.ones((128, 128), np.float32)).astype(bf)
    msk = (kk[:, None] < kk[None, :]).astype(np.float32)
    return dict(c_ident=ident, c_L=L, c_ones=ones, c_mask=msk.astype(bf))


_W_NAMES = ["ln1", "w_in", "rnn_conv_w", "rnn_conv_b", "lru_wa", "lru_ba", "lru_wx", "lru_bx", "lru_lambda",
            "q_norm_g", "k_norm_g", "w_proj_rnn", "w_proj_attn", "w_out", "ln2", "w_up", "ffn_conv_w",
            "ffn_conv_b", "w_down"]


def make_in_maps(inputs, n_cores):
    f = lambda a: np.ascontiguousarray(np.asarray(a, dtype=np.float32))
    xp = f(inputs["x_prompt"])
    xsm = f(inputs["x_sample"])
    B, SEQ, _ = xp.shape
    DB, DEC, _ = xsm.shape
    H = SEQ // 2
    PAST = inputs["cache_k"].shape[2]
    ckk = f(inputs["cache_k"])[0].reshape(DB, PAST, D)
    cvv = f(inputs["cache_v"])[0].reshape(DB, PAST, D)
    src = f(inputs["state_rnn_conv"])[0]
    sh = f(inputs["state_rnn_h"])[0]
    sfc = f(inputs["state_ffn_conv"])[0]
    shared = {}
    for n in _W_NAMES:
        a = f(inputs[n])[0]
        if a.ndim == 1:
            a = a[None, :]
        shared[n] = np.ascontiguousarray(a)
    shared.update(host_consts())
    maps = []
    for c in range(n_cores):
        b, g = (c // 2) % B, c % 2
        m = dict(shared)
        if g == 1:
            m["x_p"] = xp[b]
            flag = np.array([1.0, 0.0], np.float32)
        else:
            m["x_p"] = np.ascontiguousarray(np.concatenate([xp[b, :H], xp[b, :H]], axis=0))
            flag = np.array([0.0, -30000.0], np.float32)
        m["c_flag"] = np.ascontiguousarray(np.broadcast_to(flag[None, :], (128, 2)))
        m["x_s"] = xsm[c % DB]
        m["ck"] = ckk[c % DB]
        m["cv"] = cvv[c % DB]
        m["s_rc"] = src[c % DB]
        m["s_h"] = sh[c % DB][None, :]
        m["s_fc"] = sfc[c % DB]
        maps.append(m)
    return maps, (B, SEQ, DB, DEC, PAST)


_CACHE = {}


def run(inputs, n_cores=N_CORES):
    maps, (B, SEQ, DB, DEC, PAST) = make_in_maps(inputs, n_cores)
    H = SEQ // 2
    key = (SEQ, PAST, DEC)
    if key not in _CACHE:
        _CACHE[key] = build_program(SEQ, PAST, DEC)
    nc, stats = _CACHE[key]
    res = run_bass_kernel_spmd(nc, maps, core_ids=list(range(n_cores)))
    R = res.results
    nb = min(B, n_cores // 2)
    ns = min(DB, n_cores)

    def halves(name, shape):
        return np.stack([np.concatenate([np.asarray(R[2 * b][name], dtype=np.float32).reshape(shape),
                                         np.asarray(R[2 * b + 1][name], dtype=np.float32).reshape(shape)], axis=0)
                         for b in range(nb)])

    def fin(name, shape):
        return np.stack([np.asarray(R[2 * b + 1][name], dtype=np.float32).reshape(shape) for b in range(nb)])

    def samp(name, shape):
        return np.stack([np.asarray(R[c][name], dtype=np.float32).reshape(shape) for c in range(ns)])

    y_p = halves("y_p", (H, D))
    k_p = halves("k_p", (H, NH, HD))[None]
    v_p = halves("v_p", (H, NH, HD))[None]
    rc_p = fin("rc_p", (3, D))[None]
    h_p = fin("h_p", (D,))[None]
    fc_p = fin("fc_p", (2, DFF))[None]
    y_s = samp("y_s", (DEC, D))
    k_s = samp("k_s", (DEC, NH, HD))[None]
    v_s = samp("v_s", (DEC, NH, HD))[None]
    rc_s = samp("rc_s", (3, D))[None]
    h_s = samp("h_s", (D,))[None]
    fc_s = samp("fc_s", (2, DFF))[None]
    return (y_p, y_s, k_p, v_p, rc_p, h_p, fc_p, k_s, v_s, rc_s, h_s, fc_s)


def kernel(**inputs):
    return run(inputs, N_CORES)
```

```python
import numpy as np
from contextlib import ExitStack
import ml_dtypes
import concourse.bass as bass
import concourse.mybir as mybir
from concourse.bass_utils import run_bass_kernel_spmd

F32 = mybir.dt.float32
BF16 = mybir.dt.bfloat16
AF = mybir.ActivationFunctionType
ALU = mybir.AluOpType
AX = mybir.AxisListType

D = 1024
NH = 8
HD = 128
DFF = 3072
EPS = 1e-6
N_CORES = 8


def I(name, *a, **kw):
    return (name, a, kw)


class Buf:
    __slots__ = ("name", "w", "r", "sem", "cnt", "const")

    def __init__(self, name, const=False):
        self.name = name
        self.w = None
        self.r = []
        self.sem = None
        self.cnt = 0
        self.const = const


class Eng:
    def __init__(self, name, h, sem):
        self.name = name
        self.h = h
        self.sem = sem
        self.count = 0
        self.waited = {}


class Unit:
    __slots__ = ("id", "eng", "instrs", "deps", "dur", "kind", "semof", "slow", "is_out", "tag", "cls",
                 "nbytes", "start", "children", "nrem", "ready", "phase")


def _free_size(ap):
    n = 1
    for s_ in ap.shape[1:]:
        n *= s_
    return n


_ACT_CLS = {"Exp": "A", "Ln": "A", "Sigmoid": "B", "Gelu_apprx_tanh": "C", "Sqrt": "D"}


class K:
    def __init__(self, nc, es):
        self.nc = nc
        self.es = es
        self.engs = {}
        for name, h in (("pe", nc.tensor), ("act", nc.scalar), ("dve", nc.vector),
                        ("pool", nc.gpsimd), ("sp", nc.sync)):
            sem = es.enter_context(nc.semaphore("c_" + name))
            self.engs[name] = Eng(name, h, sem)
        self.nsem = 5
        self.units = []
        self.pe_group = None
        self.sb_bytes = 0

    def sb(self, name, shape, dt):
        n = 1
        for s_ in shape[1:]:
            n *= s_
        self.sb_bytes += n * (4 if dt == F32 else 2)
        return self.nc.alloc_sbuf_tensor(name, list(shape), dt)

    def new_sem(self, name):
        self.nsem += 1
        return self.es.enter_context(self.nc.semaphore("d_" + name))

    def _new_unit(self, eng, instrs, r, w, kind, dur):
        u = Unit()
        u.id = len(self.units)
        u.eng = eng
        u.instrs = instrs
        u.kind = kind
        u.dur = dur
        u.semof = None
        u.slow = False
        u.is_out = False
        u.tag = None
        u.cls = None
        u.nbytes = 0
        u.phase = getattr(self, "phase", "")
        deps = set()
        for b in r:
            if b.w is not None:
                deps.add(b.w)
        for b in w:
            if b.w is not None:
                deps.add(b.w)
            deps.update(b.r)
        deps.discard(u.id)
        u.deps = deps
        for b in r:
            if not b.const:
                b.r.append(u.id)
        for b in w:
            b.w = u.id
            b.r = []
        self.units.append(u)
        return u

    def op(self, eng, instr, r=(), w=(), inc=True):
        name, a, kw = instr
        out = kw.get("out", a[0] if a else None)
        n = _free_size(out) if out is not None else 64
        if eng == "pe":
            dur = 35.0 + 0.45 * max(n, 64) if name == "matmul" else 90.0
            if self.pe_group is None:
                self.pe_group = ([], [], [], 0.0)
            g = self.pe_group
            g[0].append(instr)
            g[1].extend(r)
            g[2].extend(w)
            self.pe_group = (g[0], g[1], g[2], g[3] + dur)
            if not inc:
                return
            g = self.pe_group
            self.pe_group = None
            return self._new_unit("pe", g[0], list(dict.fromkeys(g[1])), list(dict.fromkeys(g[2])), "op", g[3])
        if eng == "act":
            dur = 180.0 + 0.85 * n
        elif eng == "dve":
            dur = 120.0 + 1.05 * n
            if name == "tensor_tensor_scan":
                dur = 120.0 + 2.1 * n
        else:
            dur = 200.0 + 1.8 * n
        u = self._new_unit(eng, [instr], list(r), list(w), "op", dur)
        if eng == "act" and name == "activation":
            u.cls = _ACT_CLS.get(str(kw.get("func")).split(".")[-1])
        return u

    def dma(self, q, out, in_, r, w, semof, slow=False, is_out=False):
        return self.dma_multi(q, [(out, in_)], r, w, semof, slow=slow, is_out=is_out)

    def dma_multi(self, q, pairs, r, w, semof, slow=False, is_out=False):
        instrs = []
        nbytes = 0
        for out, in_ in pairs:
            kw = dict(out=out, in_=in_)
            if slow:
                kw["allow_slow_non_contiguous"] = True
            instrs.append(("dma_start", (), kw))
            nbytes += 128 * _free_size(out) * (4 if out.dtype == F32 else 2)
        u = self._new_unit(q, instrs, list(r), list(w), "dma", 60.0 * len(pairs))
        u.semof = semof
        u.is_out = is_out
        u.nbytes = nbytes
        return u

    def schedule(self):
        import heapq
        U = self.units
        assert self.pe_group is None
        for u in U:
            u.children = []
            u.nrem = len(u.deps)
            u.ready = 0.0
        for u in U:
            for d in u.deps:
                U[d].children.append(u.id)
        finish = [0.0] * len(U)
        eng_free = {e: 0.0 for e in self.engs}
        avail = {e: [] for e in self.engs}
        pend = {e: [] for e in self.engs}
        for u in U:
            if u.nrem == 0:
                heapq.heappush(avail[u.eng], u.id)
        dma_free = 0.0
        act_cls = None
        byp = [0]
        order = []
        nleft = len(U)
        while nleft:
            best = None
            for e in self.engs:
                pe_, av = pend[e], avail[e]
                while pe_ and pe_[0][0] <= eng_free[e]:
                    heapq.heappush(av, heapq.heappop(pe_)[1])
                if av:
                    cand = (eng_free[e], av[0], e, True)
                elif pe_:
                    cand = (pe_[0][0], pe_[0][1], e, False)
                else:
                    continue
                if best is None or cand[:2] < best[:2]:
                    best = cand
            start, uid, e, from_av = best
            if from_av:
                if e == "act" and len(avail[e]) > 1:
                    head = avail[e][0]
                    hu = U[head]
                    pick = head
                    if not (hu.cls is None or hu.cls == act_cls or act_cls is None) and byp[0] < 4:
                        popped = []
                        pick = None
                        for _ in range(min(10, len(avail[e]))):
                            c = heapq.heappop(avail[e])
                            popped.append(c)
                            if U[c].cls is None or U[c].cls == act_cls:
                                pick = c
                                break
                        for c in popped:
                            if c != pick:
                                heapq.heappush(avail[e], c)
                        if pick is None:
                            pick = heapq.heappop(avail[e])
                            byp[0] = 0
                        else:
                            byp[0] += 1
                    else:
                        heapq.heappop(avail[e])
                        byp[0] = 0
                    uid = pick
                else:
                    heapq.heappop(avail[e])
            else:
                heapq.heappop(pend[e])
            u = U[uid]
            dur = u.dur
            if e == "act" and u.cls is not None:
                if act_cls is not None and u.cls != act_cls:
                    dur += 1300.0
                act_cls = u.cls
            u.start = start
            if u.kind == "dma":
                dma_free = max(dma_free, start) + u.nbytes / 180.0
                finish[uid] = dma_free + 2000.0
                eng_free[e] = start + dur
            else:
                finish[uid] = start + dur
                eng_free[e] = start + dur
            order.append(uid)
            nleft -= 1
            for c in u.children:
                cu = U[c]
                cu.nrem -= 1
                if finish[uid] > cu.ready:
                    cu.ready = finish[uid]
                if cu.nrem == 0:
                    heapq.heappush(pend[cu.eng], (cu.ready, c))
        self.sim_ns = max(finish) if finish else 0.0
        return order

    def emit(self, order):
        U = self.units
        out_tags = []
        for uid in order:
            u = U[uid]
            E = self.engs[u.eng]
            best = {}
            for d in u.deps:
                du = U[d]
                if u.eng == "pe" and du.eng == "pe" and du.kind == "op" and u.kind == "op":
                    continue
                sem, val = du.tag
                key = sem.name
                if key not in best or best[key][1] < val:
                    best[key] = (sem, val)
            for key, (sem, val) in best.items():
                if E.waited.get(key, 0) >= val:
                    continue
                E.h.wait_ge(sem, val)
                E.waited[key] = val
            if u.kind == "dma":
                b = u.semof
                if b.sem is None:
                    b.sem = self.new_sem(b.name)
                for name, a, kw in u.instrs:
                    ins = getattr(E.h, name)(*a, **kw)
                    b.cnt += 16
                    ins.then_inc(b.sem, 16)
                u.tag = (b.sem, b.cnt)
                if u.is_out:
                    out_tags.append(u.tag)
            else:
                ins = None
                for name, a, kw in u.instrs:
                    ins = getattr(E.h, name)(*a, **kw)
                E.count += 1
                ins.then_inc(E.sem, 1)
                u.tag = (E.sem, E.count)
        return out_tags


class Ring:
    def __init__(self, k, name, n, shape, dt):
        self.items = []
        for i in range(n):
            self.items.append((k.sb(f"{name}{i}", shape, dt), Buf(f"{name}{i}")))
        self.i = 0

    def next(self):
        it = self.items[self.i % len(self.items)]
        self.i += 1
        return it


def build_program(SEQ, PAST, DEC):
    TW = 512
    assert SEQ % (2 * TW) == 0 and PAST % TW == 0 and DEC <= 128
    H = SEQ // 2
    nc = bass.Bass("TRN2", target_bir_lowering=False)
    es = ExitStack()
    k = K(nc, es)

    def din(name, shape, dt=F32):
        return nc.dram_tensor(name, list(shape), dt, kind="ExternalInput").ap()

    def dout(name, shape, dt=F32):
        return nc.dram_tensor(name, list(shape), dt, kind="ExternalOutput").ap()

    def dscr(name, shape, dt):
        return nc.dram_tensor(name, list(shape), dt, kind="Internal").ap()

    x_p = din("x_p", [SEQ, D])
    x_s = din("x_s", [DEC, D])
    ck = din("ck", [PAST, D])
    cv = din("cv", [PAST, D])
    s_rc = din("s_rc", [3, D])
    s_h = din("s_h", [1, D])
    s_fc = din("s_fc", [2, DFF])
    ln1 = din("ln1", [1, D])
    w_in = din("w_in", [D, 7 * D])
    cw = din("rnn_conv_w", [4, D])
    cb = din("rnn_conv_b", [1, D])
    lwa = din("lru_wa", [16, 64, 64])
    lba = din("lru_ba", [1, D])
    lwx = din("lru_wx", [16, 64, 64])
    lbx = din("lru_bx", [1, D])
    lam = din("lru_lambda", [1, D])
    qg = din("q_norm_g", [1, HD])
    kg = din("k_norm_g", [1, HD])
    w_pr = din("w_proj_rnn", [D, D])
    w_pa = din("w_proj_attn", [D, D])
    w_out = din("w_out", [D, D])
    ln2 = din("ln2", [1, D])
    w_up = din("w_up", [D, 2 * DFF])
    fw = din("ffn_conv_w", [3, DFF])
    fb = din("ffn_conv_b", [1, DFF])
    w_dn = din("w_down", [DFF, D])
    c_id = din("c_ident", [128, 128], BF16)
    c_L = din("c_L", [128, 128], BF16)
    c_one = din("c_ones", [128, 128], BF16)
    c_msk = din("c_mask", [128, 128], BF16)
    c_flag = din("c_flag", [128, 2])

    outs = {}
    for tag, T in (("p", H), ("s", DEC)):
        outs[tag] = dict(
            y=dout(f"y_{tag}", [T, D]), k=dout(f"k_{tag}", [T, D]), v=dout(f"v_{tag}", [T, D]),
            rc=dout(f"rc_{tag}", [3, D]), h=dout(f"h_{tag}", [1, D]), fc=dout(f"fc_{tag}", [2, DFF]))

    KTs = {"p": dscr("KTs_p", [NH, HD, SEQ], BF16), "s": dscr("KTs_s", [NH, HD, PAST], BF16)}
    Vs = {"p": dscr("Vs_p", [NH, 128, SEQ // 128, HD], BF16),
          "s": dscr("Vs_s", [NH, 128, PAST // 128, HD], BF16)}
    class _Toks(dict):
        def __init__(self, name):
            super().__init__()
            self.nm = name

        def __missing__(self, ci):
            self[ci] = Buf(f"{self.nm}_{ci}")
            return self[ci]

    KTs_b = {"p": _Toks("KTs_p"), "s": _Toks("KTs_s")}
    Vs_b = {"p": _Toks("Vs_p"), "s": _Toks("Vs_s")}

    cst = Buf("consts")
    ident = k.sb("ident", [128, 128], BF16)
    Lneg = k.sb("Lneg", [128, 128], BF16)
    onesneg = k.sb("onesneg", [128, 128], BF16)
    msk = k.sb("msk", [128, 128], BF16)
    ln1T = k.sb("ln1T", [128, 8], F32)
    ln2T = k.sb("ln2T", [128, 8], F32)
    cwT = k.sb("cwT", [128, 8, 4], F32)
    cbT = k.sb("cbT", [128, 8], F32)
    baT = k.sb("baT", [128, 8], F32)
    bxT = k.sb("bxT", [128, 8], F32)
    lamT = k.sb("lamT", [128, 8], F32)
    cA = k.sb("cA", [128, 8], F32)
    epsT = k.sb("epsT", [128, 1], F32)
    onepT = k.sb("onepT", [128, 1], F32)
    fwT = k.sb("fwT", [128, 24, 3], F32)
    fbT = k.sb("fbT", [128, 24], F32)
    gq = k.sb("gq", [128, HD], F32)
    gk = k.sb("gk", [128, HD], F32)
    wbd = k.sb("wbd", [128, 2, 8, 128], BF16)
    wbd_b = Buf("wbd")

    cpairs = []

    def cdma(out, in_, slow=False):
        cpairs.append((out, in_))

    cdma(ident[:], c_id)
    cdma(Lneg[:], c_L)
    cdma(onesneg[:], c_one)
    cdma(msk[:], c_msk)
    cdma(ln1T[:], ln1.rearrange("o (k p) -> p (o k)", p=128), slow=True)
    cdma(ln2T[:], ln2.rearrange("o (k p) -> p (o k)", p=128), slow=True)
    for i in range(4):
        cdma(cwT[:, :, i], cw[i:i + 1, :].rearrange("o (c p) -> p (o c)", p=128), slow=True)
    cdma(cbT[:], cb.rearrange("o (k p) -> p (o k)", p=128), slow=True)
    cdma(baT[:], lba.rearrange("o (k p) -> p (o k)", p=128), slow=True)
    cdma(bxT[:], lbx.rearrange("o (k p) -> p (o k)", p=128), slow=True)
    cdma(lamT[:], lam.rearrange("o (k p) -> p (o k)", p=128), slow=True)
    for i in range(3):
        cdma(fwT[:, :, i], fw[i:i + 1, :].rearrange("o (c p) -> p (o c)", p=128), slow=True)
    cdma(fbT[:], fb.rearrange("o (k p) -> p (o k)", p=128), slow=True)
    cdma(gq[:], qg.partition_broadcast(128).rearrange("p o d -> p (o d)"))
    cdma(gk[:], kg.partition_broadcast(128).rearrange("p o d -> p (o d)"))
    flagb = k.sb("flagb", [128, 2], F32)
    cdma(flagb[:], c_flag)
    k.dma_multi("sp", cpairs, r=[], w=[cst], semof=cst, slow=True)
    X = k.sb("X", [128, 4, D], F32)
    X_b = Buf("X")
    wbd_f = X[:, 0:2, :].rearrange("p a (c m) -> p a c m", c=8)
    k.op("dve", I("memset", wbd_f, 0.0), w=[wbd_b, X_b])
    for wi, src in enumerate((lwa, lwx)):
        v = src.rearrange("(c two) i o -> two i c o", two=2)
        k.dma("sp", wbd_f[0:64, wi, :, 0:64], v[0], r=[], w=[wbd_b, X_b], semof=wbd_b)
        k.dma("sp", wbd_f[64:128, wi, :, 64:128], v[1], r=[], w=[wbd_b, X_b], semof=wbd_b)
    k.op("dve", I("tensor_copy", out=wbd[:], in_=wbd_f), r=[wbd_b, X_b], w=[wbd_b])
    k.op("act", I("activation", out=cA[:], in_=lamT[:], func=AF.Exp, scale=-1.0), r=[cst], w=[cst])
    k.op("act", I("activation", out=cA[:], in_=cA[:], func=AF.Ln, bias=1.0), r=[cst], w=[cst])
    k.op("dve", I("tensor_scalar", out=cA[:], in0=cA[:], scalar1=-8.0, scalar2=None, op0=ALU.mult),
         r=[cst], w=[cst])
    k.op("dve", I("memset", epsT[:], EPS), w=[cst])
    k.op("dve", I("memset", onepT[:], 1.0000002), w=[cst])

    banks = [(nc.alloc_psum_tensor(f"ps{i}", [128, 512], F32), Buf(f"ps{i}")) for i in range(2)]
    at_pairs = []
    for i in range(2):
        pt = nc.alloc_psum_tensor(f"atp{i}", [128, 1024], F32)
        ba, bb = Buf(f"atp{i}a"), Buf(f"atp{i}b")
        at_pairs.append((pt, [ba, bb]))
        banks.append((pt[:, 0:512], ba))
        banks.append((pt[:, 512:1024], bb))
    banks += [(nc.alloc_psum_tensor(f"ps{i}", [128, 512], F32), Buf(f"ps{i}")) for i in range(6, 8)]
    mm_ring = banks[0:2]
    mm_i = [0, 0]

    pfx = [False]
    pfx_ring = banks[0:7]
    mm_i.append(0)

    def mmbank():
        if pfx[0]:
            it = pfx_ring[mm_i[2] % len(pfx_ring)]
            mm_i[2] += 1
            return it
        it = mm_ring[mm_i[0] % len(mm_ring)]
        mm_i[0] += 1
        return it

    def atpair():
        it = at_pairs[mm_i[1] % len(at_pairs)]
        mm_i[1] += 1
        return it

    o_banks = banks[6:7]
    tp_t, tp_b = banks[7]
    tp_bf = tp_t[:].bitcast(BF16)

    wring = Ring(k, "wslot", 6, [128, 4096], BF16)
    stat = Ring(k, "stat", 4, [128, 16], F32)
    uT_ring = Ring(k, "uT", 2, [128, 8, TW], BF16)
    QT_ring = Ring(k, "QT", 2, [128, NH, TW], BF16)
    yr_ring = Ring(k, "yrT", 1, [128, 8, TW], BF16)
    KTst = k.sb("KTst", [128, NH, TW], BF16)
    KTst_b = Buf("KTst")
    Vb = k.sb("Vb", [128, 4, D], BF16)
    Vb_b = Buf("Vb")
    sqj = k.sb("sqj", [128, 512], F32)
    sqj_b = Buf("sqj")
    junk = sqj[:].bitcast(BF16)
    junk_b = sqj_b
    kf_ring = Ring(k, "kf", 1, [128, 512], F32)
    vf_ring = Ring(k, "vf", 1, [128, 512], F32)
    nb_ring = Ring(k, "nb", 2, [128, 512], BF16)
    xs_ring = Ring(k, "xs", 2, [128, D], BF16)
    hT = k.sb("hT", [128, 24, TW], BF16)
    m1 = hT[:, 0:16, :].rearrange("p a b -> p (a b)").bitcast(F32).rearrange("p (c t) -> p c t", c=8)
    m1_b = Buf("hT_lo")
    hT_hi_b = Buf("hT_hi")
    yaT = k.sb("yaT", [128, 8, TW], BF16)
    yaT_b = Buf("yaT")
    f32r = Ring(k, "f32r", 6, [128, TW + 4], F32)
    b16r = Ring(k, "b16r", 2, [128, TW], BF16)
    a16r = Ring(k, "a16r", 3, [128, TW], BF16)
    a16p = Ring(k, "a16p", 4, [128, 2 * TW], BF16)
    kc_ring = Ring(k, "kTc", 3, [128, TW], BF16)
    vc_ring = Ring(k, "vc", 3, [128, 4, HD], BF16)
    kin_ring = xs_ring
    xblk_ring = Ring(k, "xblk", 1, [128, D], F32)
    class _ListRing:
        def __init__(self, items):
            self.items = items
            self.i = 0

        def next(self):
            it = self.items[self.i % len(self.items)]
            self.i += 1
            return it

    hT_f32 = hT[:, :, :].rearrange("p a b -> p (a b)").bitcast(F32)
    pf32 = _ListRing(list(f32r.items) + [(hT_f32[:, i * (TW + 4):(i + 1) * (TW + 4)], Buf(f"pf32_{i}"))
                                          for i in range(11)])
    qt0 = QT_ring.items[0][0]
    pb16 = _ListRing(list(b16r.items) + [(qt0[:, i, :], Buf(f"pb16_{i}")) for i in range(NH)])
    alias_bufs = [b for _, b in pf32.items[len(f32r.items):]] + [b for _, b in pb16.items[len(b16r.items):]]

    wscr = {}

    def wcast(key, src_ap, kc, cols, after=None):
        scr = dscr("ws_" + key, [128, kc * cols], BF16)
        sb_ = Buf("ws_" + key)
        k.dma("pool", scr.rearrange("p (k c) -> p k c", k=kc), src_ap, r=([after] if after is not None else []),
              w=[sb_], semof=sb_)
        wscr[key] = (scr, sb_, kc, cols)

    def wload(key, half, k0=0, k1=None):
        scr, sb_, kc, cols = wscr[key]
        k1 = kc if k1 is None else k1
        wt, wb = wring.next()
        view = wt[:, 0:(k1 - k0) * 512].rearrange("p (k c) -> p k c", k=k1 - k0)
        src = scr.rearrange("p (k c) -> p k c", k=kc)[:, k0:k1, half * 512:(half + 1) * 512]
        k.dma("sp", view, src, r=[sb_], w=[wb], semof=wb)
        return view, wb

    w_in_v = w_in.rearrange("(k p) c -> p k c", p=128)
    w_up_v = w_up.rearrange("(k p) c -> p k c", p=128)
    w_pr_v = w_pr.rearrange("(k p) c -> p k c", p=128)
    w_pa_v = w_pa.rearrange("(k p) c -> p k c", p=128)
    w_out_v = w_out.rearrange("(k p) c -> p k c", p=128)
    w_dn_v = w_dn.rearrange("(k p) c -> p k c", p=128)

    SCALE = float(HD) ** -0.5
    def cast_rest(batch, after):
        if batch == 0:
            for gi in (2, 1, 5, 6):
                wcast(f"in{gi}", w_in_v[:, :, gi * D:(gi + 1) * D], 8, D, after)
        elif batch == 1:
            wcast("pr", w_pr_v, 8, D, after)
            wcast("pa", w_pa_v, 8, D, after)
            wcast("out", w_out_v, 8, D, after)
            wcast("up0", w_up_v[:, :, 0:D], 8, D, after)
        elif batch == 2:
            wcast("up3", w_up_v[:, :, DFF:DFF + D], 8, D, after)
            for g3 in (1, 2):
                wcast(f"up{g3}", w_up_v[:, :, g3 * D:(g3 + 1) * D], 8, D, after)
                wcast(f"up{g3 + 3}", w_up_v[:, :, DFF + g3 * D:DFF + (g3 + 1) * D], 8, D, after)
        elif batch == 3:
            for half in range(2):
                wcast(f"dn{half}", w_dn_v[:, :, half * 512:(half + 1) * 512], 24, 512, after)

    for gi in (3, 4, 0):
        wcast(f"in{gi}", w_in_v[:, :, gi * D:(gi + 1) * D], 8, D)

    def rms_to_T(src_fn, nb, P, gT, dstT, dst_b):
        for blk in range(nb):
            xa, xa_b = src_fn(blk)
            st, st_b = stat.next()
            k.op("act", I("activation", out=junk[0:P, :], in_=xa, func=AF.Square,
                                               accum_out=st[0:P, 0:1]), r=[xa_b], w=[junk_b, st_b])
            k.op("act", I("activation", out=st[0:P, 1:2], in_=st[0:P, 0:1], func=AF.Ln,
                          scale=1.0 / D, bias=epsT[0:P, 0:1]), r=[st_b, cst], w=[st_b])
            k.op("act", I("activation", out=st[0:P, 2:3], in_=st[0:P, 1:2], func=AF.Exp, scale=-0.5),
                 r=[st_b], w=[st_b])
            xs, xs_b = xs_ring.next()
            k.op("dve", I("tensor_scalar", out=xs[0:P, :], in0=xa, scalar1=st[0:P, 2:3],
                                                  scalar2=None, op0=ALU.mult), r=[xa_b, st_b], w=[xs_b])
            for kc in range(8):
                k.op("pe", I("transpose", out=tp_bf[:, kc * 128:kc * 128 + P],
                                                 in_=xs[0:P, kc * 128:(kc + 1) * 128], identity=ident[0:P, 0:P]),
                     r=[xs_b, cst], w=[tp_b], inc=(kc == 7))
            src = tp_bf.rearrange("p (k t) -> p k t", k=8)[:, :, 0:P]
            gb = gT[:, :].unsqueeze(2).to_broadcast([128, 8, P])
            k.op("dve", I("tensor_tensor", out=dstT[:, :, blk * 128:blk * 128 + P], in0=src, in1=gb,
                                                  op=ALU.mult), r=[tp_b, cst], w=[dst_b])

    def head_norm(ps, P, grow, dst, dst_b):
        st, st_b = stat.next()
        k.op("act", I("activation", out=sqj[0:P, :], in_=ps[0][0:P, :], func=AF.Square),
             r=[ps[1]], w=[sqj_b])
        k.op("dve", I("tensor_reduce", out=st[0:P, 0:4], in_=sqj[0:P, :].rearrange("p (h d) -> p h d", h=4),
                      axis=AX.X, op=ALU.add), r=[sqj_b], w=[st_b])
        k.op("act", I("activation", out=st[0:P, 4:8], in_=st[0:P, 0:4], func=AF.Ln,
                      scale=1.0 / HD, bias=epsT[0:P, 0:1]), r=[st_b, cst], w=[st_b])
        k.op("act", I("activation", out=st[0:P, 8:12], in_=st[0:P, 4:8], func=AF.Exp, scale=-0.5),
             r=[st_b], w=[st_b])
        for h in range(4):
            k.op("dve", I("scalar_tensor_tensor", out=dst[0:P, h * 128:(h + 1) * 128],
                          in0=ps[0][0:P, h * 128:(h + 1) * 128], scalar=st[0:P, 8 + h:9 + h], in1=grow[0:P, :],
                          op0=ALU.mult, op1=ALU.mult), r=[ps[1], st_b, cst], w=[dst_b])

    def to_T(src, src_b, P, dstT, dst_b, blk, h0=0, nh=8, on_dve=False):
        for h in range(nh):
            k.op("pe", I("transpose", out=tp_bf[:, h * 128:h * 128 + P], in_=src[0:P, h * 128:(h + 1) * 128],
                         identity=ident[0:P, 0:P]), r=[src_b, cst], w=[tp_b], inc=(h == nh - 1))
        s3 = tp_bf[:, 0:nh * 128].rearrange("p (k t) -> p k t", k=nh)[:, :, 0:P]
        if on_dve:
            k.op("dve", I("tensor_copy", out=dstT[:, h0:h0 + nh, blk * 128:blk * 128 + P], in_=s3), r=[tp_b], w=[dst_b])
        else:
            k.op("act", I("activation", out=dstT[:, h0:h0 + nh, blk * 128:blk * 128 + P], in_=s3, func=AF.Copy),
                 r=[tp_b], w=[dst_b])

    def run_seq(tag, x_ap, past, tiles, init, boundary=None, between=None):
        o = outs[tag]
        hist_rc = k.sb(f"hrc_{tag}", [128, 8, 3], F32)
        hstate = k.sb(f"hst_{tag}", [128, 8], F32)
        hist_fc = k.sb(f"hfc_{tag}", [128, 24, 2], F32)
        st_b = Buf(f"state_{tag}")
        fc_b = Buf(f"fstate_{tag}")
        if init is None:
            k.op("dve", I("memset", hist_rc[:], 0.0), w=[st_b])
            k.op("dve", I("memset", hstate[:], 0.0), w=[st_b])
            k.op("dve", I("memset", hist_fc[:], 0.0), w=[fc_b])
        else:
            for i in range(3):
                k.dma("sp", hist_rc[:, :, i], init[0][i:i + 1, :].rearrange("o (c p) -> p (o c)", p=128), r=[],
                      w=[st_b], semof=st_b, slow=True)
            k.dma("sp", hstate[:], init[1].rearrange("o (c p) -> p (o c)", p=128), r=[], w=[st_b], semof=st_b, slow=True)
            for i in range(2):
                k.dma("sp", hist_fc[:, :, i], init[2][i:i + 1, :].rearrange("o (c p) -> p (o c)", p=128), r=[],
                      w=[fc_b], semof=fc_b, slow=True)
        nfull = 0
        for ti, T in enumerate(tiles):
            T["nb"] = (T["W"] + 127) // 128
            T["P"] = min(T["W"], 128)
            T["pos0"] = past + T["t0"]
            T["last"] = ti == len(tiles) - 1
            T["via_scratch"] = T["W"] % 128 == 0
            T.setdefault("halo", False)
            if T["full"]:
                T["par"] = nfull % 2
                nfull += 1
            else:
                T["par"] = 0

        def load_x(T):
            k.dma("sp", X[0:T["P"], 0:T["nb"], :],
                  x_ap[T["t0"]:T["t0"] + T["W"], :].rearrange("(b p) c -> p b c", p=T["P"]), r=[], w=[X_b], semof=X_b)

        def pre(T):
            k.phase = f"{tag}{T['t0']}:pre"
            t0, Wq, nb, P, pos0 = T["t0"], T["W"], T["nb"], T["P"], T["pos0"]
            full, out0 = T["full"], T["out0"]
            uT, uT_b = uT_ring.items[T["par"]]
            QT, QT_b = QT_ring.items[T["par"]]
            yrT, yrT_b = yr_ring.items[0]
            def xin(blk):
                xb_t, xb_b = xblk_ring.next()
                r0 = t0 + blk * 128
                k.dma("sp", xb_t[0:P, :], x_ap[r0:r0 + P, :], r=[], w=[xb_b], semof=xb_b)
                return xb_t[0:P, :], xb_b
            rms_to_T(xin, nb, P, ln1T, uT, uT_b)
            groups = ((2, "q"), (3, "k"), (4, "v")) if full else ((3, "k"), (4, "v"))
            for gi, gname in groups:
                for half in range(2):
                    wv, wb = wload(f"in{gi}", half)
                    for blk in range(nb):
                        ps = mmbank()
                        for kc in range(8):
                            k.op("pe", I("matmul", ps[0][0:P, :], lhsT=uT[:, kc, blk * 128:blk * 128 + P],
                                         rhs=wv[:, kc, :], start=(kc == 0), stop=(kc == 7)),
                                 r=[uT_b, wb], w=[ps[1]], inc=(kc == 7))
                        r0 = (out0 + blk * 128) if out0 is not None else None
                        cs = slice(half * 512, (half + 1) * 512)
                        if gname == "q":
                            nbf, nbf_b = nb_ring.next()
                            head_norm(ps, P, gq, nbf, nbf_b)
                            to_T(nbf, nbf_b, P, QT, QT_b, blk, h0=4 * half, nh=4, on_dve=full)
                        elif gname == "k":
                            nbf, nbf_b = nb_ring.next()
                            if out0 is not None:
                                kf, kf_b = kf_ring.next()
                                head_norm(ps, P, gk, kf, kf_b)
                                k.op("dve", I("tensor_copy", out=nbf[0:P, :], in_=kf[0:P, :]), r=[kf_b], w=[nbf_b])
                                k.dma("sp", o["k"][r0:r0 + P, cs], kf[0:P, :], r=[kf_b], w=[], semof=kf_b, is_out=True)
                            else:
                                head_norm(ps, P, gk, nbf, nbf_b)
                            to_T(nbf, nbf_b, P, KTst, KTst_b, blk, h0=4 * half, nh=4, on_dve=full)
                        else:
                            if out0 is not None:
                                vf, vf_b = vf_ring.next()
                                k.op("act", I("activation", out=vf[0:P, :], in_=ps[0][0:P, :], func=AF.Copy),
                                     r=[ps[1]], w=[vf_b])
                                k.op("dve", I("tensor_copy", out=Vb[0:P, blk, cs], in_=vf[0:P, :]), r=[vf_b], w=[Vb_b])
                                k.dma("sp", o["v"][r0:r0 + P, cs], vf[0:P, :], r=[vf_b], w=[], semof=vf_b, is_out=True)
                            else:
                                k.op("act", I("activation", out=Vb[0:P, blk, cs], in_=ps[0][0:P, :], func=AF.Copy),
                                     r=[ps[1]], w=[Vb_b])
                        yield 2.0
            if T["via_scratch"]:
                k.dma("sp", KTs[tag].rearrange("h d t -> d h t")[:, :, pos0:pos0 + Wq], KTst[:, :, 0:Wq],
                      r=[KTst_b], w=[KTs_b[tag][pos0 // TW]], semof=KTs_b[tag][pos0 // TW])
                k.dma_multi("sp", [(Vs[tag][h, :, pos0 // 128:pos0 // 128 + nb, :], Vb[:, 0:nb, h * 128:(h + 1) * 128])
                                   for h in range(NH)], r=[Vb_b], w=[Vs_b[tag][pos0 // TW]], semof=Vs_b[tag][pos0 // TW])

            yield 0.1

        def pre_rnn(T):
            k.phase = f"{tag}{T['t0']}:rnn"
            Wq, full = T["W"], T["full"]
            uT, uT_b = uT_ring.items[T["par"]]
            yrT, yrT_b = yr_ring.items[0]
            fr = pf32 if pfx[0] else f32r
            br = pb16 if pfx[0] else b16r
            for c in range(8):
                cl = (c % 4) * 128
                if c % 4 == 0:
                    wxr, wxr_b = wload("in0", c // 4)
                    if full:
                        wgr, wgr_b = wload("in1", c // 4)
                ps = mmbank()
                for kc in range(8):
                    k.op("pe", I("matmul", ps[0][:, 0:Wq], lhsT=wxr[:, kc, cl:cl + 128],
                                 rhs=uT[:, kc, 0:Wq], start=(kc == 0), stop=(kc == 7)),
                         r=[uT_b, wxr_b], w=[ps[1]], inc=(kc == 7))
                xp, xp_b = fr.next()
                k.op("dve", I("tensor_copy", out=xp[:, 0:3], in_=hist_rc[:, c, :]), r=[st_b], w=[xp_b])
                if full:
                    k.op("dve", I("tensor_copy", out=xp[:, 3:3 + Wq], in_=ps[0][:, 0:Wq]), r=[ps[1]], w=[xp_b])
                else:
                    k.op("act", I("activation", out=xp[:, 3:3 + Wq], in_=ps[0][:, 0:Wq], func=AF.Copy),
                         r=[ps[1]], w=[xp_b])
                k.op("dve", I("tensor_copy", out=hist_rc[:, c, :], in_=xp[:, Wq:Wq + 3]), r=[xp_b], w=[st_b])
                xc, xc_b = fr.next()
                k.op("dve", I("tensor_scalar", out=xc[:, 0:Wq], in0=xp[:, 0:Wq], scalar1=cwT[:, c, 0:1],
                              scalar2=cbT[:, c:c + 1], op0=ALU.mult, op1=ALU.add), r=[xp_b, cst], w=[xc_b])
                for i in range(1, 4):
                    k.op("dve", I("scalar_tensor_tensor", out=xc[:, 0:Wq], in0=xp[:, i:i + Wq],
                                  scalar=cwT[:, c, i:i + 1], in1=xc[:, 0:Wq], op0=ALU.mult, op1=ALU.add),
                         r=[xp_b, xc_b, cst], w=[xc_b])
                xcb, xcb_b = br.next()
                k.op("dve", I("tensor_copy", out=xcb[:, 0:Wq], in_=xc[:, 0:Wq]), r=[xc_b], w=[xcb_b])
                gts = []
                for wi, bT in ((0, baT), (1, bxT)):
                    psg = mmbank()
                    k.op("pe", I("matmul", psg[0][:, 0:Wq], lhsT=wbd[:, wi, c, :], rhs=xcb[:, 0:Wq],
                                 start=True, stop=True), r=[xcb_b, wbd_b], w=[psg[1]])
                    g, g_b = fr.next()
                    k.op("act", I("activation", out=g[:, 0:Wq], in_=psg[0][:, 0:Wq], func=AF.Sigmoid,
                                  bias=bT[:, c:c + 1]), r=[psg[1], cst], w=[g_b])
                    gts.append((g, g_b))
                (rg, rg_b), (ig, ig_b) = gts
                k.op("act", I("activation", out=rg[:, 0:Wq], in_=rg[:, 0:Wq], func=AF.Exp, scale=cA[:, c:c + 1]),
                     r=[rg_b, cst], w=[rg_b])
                sq, sq_b = fr.next()
                k.op("dve", I("tensor_tensor", out=sq[:, 0:Wq], in0=rg[:, 0:Wq], in1=rg[:, 0:Wq], op=ALU.mult),
                     r=[rg_b], w=[sq_b])
                k.op("act", I("activation", out=sq[:, 0:Wq], in_=sq[:, 0:Wq], func=AF.Ln, scale=-1.0,
                              bias=onepT[:, 0:1]), r=[sq_b, cst], w=[sq_b])
                k.op("act", I("activation", out=sq[:, 0:Wq], in_=sq[:, 0:Wq], func=AF.Exp, scale=0.5),
                     r=[sq_b], w=[sq_b])
                k.op("dve", I("tensor_tensor", out=ig[:, 0:Wq], in0=ig[:, 0:Wq], in1=xc[:, 0:Wq], op=ALU.mult),
                     r=[ig_b, xc_b], w=[ig_b])
                k.op("dve", I("tensor_tensor", out=ig[:, 0:Wq], in0=ig[:, 0:Wq], in1=sq[:, 0:Wq], op=ALU.mult),
                     r=[ig_b, sq_b], w=[ig_b])
                hs, hs_b = sq, sq_b
                k.op("dve", I("tensor_tensor_scan", out=hs[:, 0:Wq], data0=rg[:, 0:Wq], data1=ig[:, 0:Wq],
                              initial=hstate[:, c:c + 1], op0=ALU.mult, op1=ALU.add),
                     r=[rg_b, ig_b, st_b], w=[hs_b])
                k.op("dve", I("tensor_copy", out=hstate[:, c:c + 1], in_=hs[:, Wq - 1:Wq]), r=[hs_b], w=[st_b])
                if full:
                    ps2 = mmbank()
                    for kc in range(8):
                        k.op("pe", I("matmul", ps2[0][:, 0:Wq], lhsT=wgr[:, kc, cl:cl + 128],
                                     rhs=uT[:, kc, 0:Wq], start=(kc == 0), stop=(kc == 7)),
                             r=[uT_b, wgr_b], w=[ps2[1]], inc=(kc == 7))
                    gl, gl_b = xp, xp_b
                    k.op("act", I("activation", out=gl[:, 0:Wq], in_=ps2[0][:, 0:Wq], func=AF.Gelu_apprx_tanh),
                         r=[ps2[1]], w=[gl_b])
                    k.op("dve", I("tensor_tensor", out=yrT[:, c, 0:Wq], in0=hs[:, 0:Wq], in1=gl[:, 0:Wq], op=ALU.mult),
                         r=[hs_b, gl_b], w=[yrT_b])
                yield 6.0
            if T["halo"]:
                k.op("dve", I("tensor_scalar", out=hstate[:], in0=hstate[:], scalar1=flagb[:, 0:1], scalar2=None,
                              op0=ALU.mult), r=[st_b, cst], w=[st_b])
                hr2 = hist_rc[:].rearrange("p c i -> p (c i)")
                k.op("dve", I("tensor_scalar", out=hr2, in0=hr2, scalar1=flagb[:, 0:1], scalar2=None, op0=ALU.mult),
                     r=[st_b, cst], w=[st_b])

        def attn(T):
            k.phase = f"{tag}{T['t0']}:attn"
            Wq, nb, pos0 = T["W"], T["nb"], T["pos0"]
            QT, QT_b = QT_ring.items[T["par"]]
            own_tile = boundary is not None and pos0 >= boundary
            chunks = [(c0, min(c0 + TW, pos0)) for c0 in range(0, pos0, TW)]
            for h in range(NH):
                o_t, o_b = o_banks[h % len(o_banks)]
                if T["via_scratch"]:
                    blist = [("diag", (pos0, pos0 + Wq), jj, 128) for jj in reversed(range(nb))]
                else:
                    blist = [("own", None, jj, min(128, Wq - jj * 128)) for jj in reversed(range(nb))]
                for (c0, c1) in reversed(chunks):
                    for jj in reversed(range((c1 - c0) // 128)):
                        blist.append(("past", (c0, c1), jj, 128))
                groups = []
                gi0 = 0
                while gi0 < len(blist):
                    if gi0 + 1 < len(blist) and blist[gi0][3] == 128 and blist[gi0 + 1][3] == 128:
                        groups.append(blist[gi0:gi0 + 2])
                        gi0 += 2
                    else:
                        groups.append(blist[gi0:gi0 + 1])
                        gi0 += 1
                Rprev = None
                cur_chunk = None
                bi = 0
                for grp in groups:
                    G = len(grp)
                    nk = grp[0][3]
                    zt, zbufs = atpair()
                    zbufs = zbufs[0:G]
                    infos = []
                    gbias = "unset"
                    for gi_, (kind, ch, jj, nk_) in enumerate(grp):
                        bias = None
                        if kind == "own":
                            lhs_k, lhs_k_b = KTst[:, h, jj * 128:jj * 128 + nk], KTst_b
                            lhs_v, lhs_v_b = Vb[0:nk, jj, h * 128:(h + 1) * 128], Vb_b
                        else:
                            if ch != cur_chunk:
                                cur_chunk = ch
                                c0, c1 = ch
                                kTc, kTc_b = kc_ring.next()
                                vc, vc_b = vc_ring.next()
                                k.dma("sp", kTc[:, 0:c1 - c0], KTs[tag][h, :, c0:c1], r=[KTs_b[tag][c0 // TW]],
                                      w=[kTc_b], semof=kTc_b)
                                k.dma("sp", vc[:, 0:(c1 - c0) // 128, :], Vs[tag][h, :, c0 // 128:c1 // 128, :],
                                      r=[Vs_b[tag][c0 // TW]], w=[vc_b], semof=vc_b)
                            lhs_k, lhs_k_b = kTc[:, jj * 128:(jj + 1) * 128], kTc_b
                            lhs_v, lhs_v_b = vc[:, jj, :], vc_b
                            if kind == "past" and own_tile and ch[0] < boundary:
                                bias = flagb[0:nk, 1:2]
                        assert gbias == "unset" or (gbias is None) == (bias is None)
                        gbias = bias
                        zv = zt[0:nk, gi_ * 512:gi_ * 512 + Wq]
                        k.op("pe", I("matmul", zv, lhsT=lhs_k, rhs=QT[:, h, 0:Wq], start=True, stop=True),
                             r=[lhs_k_b, QT_b], w=zbufs, inc=(gi_ == G - 1))
                        infos.append((kind, jj, lhs_v, lhs_v_b, zv))

                    def v3(t):
                        return t[0:nk, :].rearrange("p (b c) -> p b c", b=2)[:, 0:G, 0:Wq]

                    def mask(t, t_b):
                        for gi_, (kind, jj, _, _, _) in enumerate(infos):
                            if kind == "past":
                                continue
                            base = gi_ * 512
                            if jj > 0:
                                k.op("pool", I("memset", t[0:nk, base:base + jj * 128], 0.0), r=[t_b], w=[t_b])
                            qd = min(128, Wq - jj * 128)
                            k.op("pool", I("tensor_tensor", out=t[0:nk, base + jj * 128:base + jj * 128 + qd],
                                           in0=t[0:nk, base + jj * 128:base + jj * 128 + qd], in1=msk[0:nk, 0:qd],
                                           op=ALU.mult), r=[t_b, cst], w=[t_b])

                    ee, ee_b = a16p.next()
                    ekw = dict(out=v3(ee), in_=v3(zt), func=AF.Exp, scale=SCALE)
                    if gbias is not None:
                        ekw["bias"] = gbias
                    k.op("act", I("activation", **ekw), r=zbufs + [cst], w=[ee_b])
                    mask(ee, ee_b)
                    ss, ss_b = a16p.next()
                    k.op("act", I("activation", out=v3(ss), in_=v3(ee), func=AF.Ln, bias=1.0), r=[ee_b], w=[ss_b])
                    for gi_, (kind, jj, lhs_v, lhs_v_b, zv) in enumerate(infos):
                        ssl = ss[0:nk, gi_ * 512:gi_ * 512 + Wq]
                        lastb = bi == len(blist) - 1
                        k.op("pe", I("matmul", zv, lhsT=Lneg[0:nk, 0:nk], rhs=ssl, start=False,
                                     stop=(Rprev is None), skip_group_check=True),
                             r=[ss_b, cst] + zbufs, w=zbufs, inc=(Rprev is None))
                        if Rprev is not None:
                            rp, rp_b, rn = Rprev
                            k.op("pe", I("matmul", zv, lhsT=onesneg[0:rn, 0:nk], rhs=rp[0:rn, 0:Wq], start=False,
                                         stop=True, skip_group_check=True), r=[rp_b, cst] + zbufs, w=zbufs)
                        if not lastb:
                            rnw, rnw_b = a16r.next()
                            if Rprev is None:
                                if nk < 128:
                                    k.op("dve", I("memset", rnw[:, 0:Wq], 0.0), w=[rnw_b])
                                k.op("dve", I("tensor_copy", out=rnw[0:nk, 0:Wq], in_=ssl), r=[ss_b], w=[rnw_b])
                                Rprev = (rnw, rnw_b, 128)
                            else:
                                rp, rp_b, rn = Rprev
                                k.op("dve", I("tensor_tensor", out=rnw[0:rn, 0:Wq], in0=rp[0:rn, 0:Wq],
                                              in1=ss[0:rn, gi_ * 512:gi_ * 512 + Wq], op=ALU.add),
                                     r=[rp_b, ss_b], w=[rnw_b])
                                Rprev = (rnw, rnw_b, rn)
                        bi += 1
                    ww, ww_b = ee, ee_b
                    wkw = dict(out=v3(ww), in_=v3(zt), func=AF.Exp, scale=SCALE)
                    if gbias is not None:
                        wkw["bias"] = gbias
                    k.op("act", I("activation", **wkw), r=zbufs + [cst, ss_b], w=[ww_b])
                    mask(ww, ww_b)
                    for gi_, (kind, jj, lhs_v, lhs_v_b, zv) in enumerate(infos):
                        first = (bi - G + gi_) == 0
                        lastb = (bi - G + gi_) == len(blist) - 1
                        k.op("pe", I("matmul", o_t[:, 0:Wq], lhsT=lhs_v, rhs=ww[0:nk, gi_ * 512:gi_ * 512 + Wq],
                                     start=first, stop=lastb), r=[lhs_v_b, ww_b], w=[o_b])
                    yield 1.5 * G
                k.op("dve", I("tensor_copy", out=yaT[:, h, 0:Wq], in_=o_t[:, 0:Wq]), r=[o_b], w=[yaT_b])

        def merge(T):
            k.phase = f"{tag}{T['t0']}:merge"
            t0, Wq, nb, P = T["t0"], T["W"], T["nb"], T["P"]
            out0 = T["out0"]
            uT, uT_b = uT_ring.items[T["par"]]
            yrT, yrT_b = yr_ring.items[0]
            mT, mT_b = yrT, yrT_b
            for pi, (key, wsrc, gi, yT, yT_b) in enumerate((("pr", w_pr_v, 5, yrT, yrT_b), ("pa", w_pa_v, 6, yaT, yaT_b))):
                for c in range(8):
                    cl = (c % 4) * 128
                    if c % 4 == 0:
                        wp, wp_b = wload(key, c // 4)
                        wg, wg_b = wload(f"in{gi}", c // 4)
                    psp = mmbank()
                    for kc in range(8):
                        k.op("pe", I("matmul", psp[0][:, 0:Wq], lhsT=wp[:, kc, cl:cl + 128],
                                     rhs=yT[:, kc, 0:Wq], start=(kc == 0), stop=(kc == 7)),
                             r=[yT_b, wp_b], w=[psp[1]], inc=(kc == 7))
                    psg = mmbank()
                    for kc in range(8):
                        k.op("pe", I("matmul", psg[0][:, 0:Wq], lhsT=wg[:, kc, cl:cl + 128],
                                     rhs=uT[:, kc, 0:Wq], start=(kc == 0), stop=(kc == 7)),
                             r=[uT_b, wg_b], w=[psg[1]], inc=(kc == 7))
                    sg, sg_b = f32r.next()
                    k.op("act", I("activation", out=sg[:, 0:Wq], in_=psg[0][:, 0:Wq], func=AF.Sigmoid),
                         r=[psg[1]], w=[sg_b])
                    if pi == 0:
                        k.op("dve", I("tensor_tensor", out=m1[:, c, 0:Wq], in0=psp[0][:, 0:Wq], in1=sg[:, 0:Wq],
                                      op=ALU.mult), r=[psp[1], sg_b], w=[m1_b])
                    else:
                        k.op("dve", I("tensor_tensor", out=sg[:, 0:Wq], in0=psp[0][:, 0:Wq], in1=sg[:, 0:Wq],
                                      op=ALU.mult), r=[psp[1], sg_b], w=[sg_b])
                        k.op("pool", I("tensor_tensor", out=mT[:, c, 0:Wq], in0=sg[:, 0:Wq], in1=m1[:, c, 0:Wq],
                                       op=ALU.add), r=[sg_b, m1_b], w=[mT_b])
                    yield 4.0
        def post_rest(T):
            k.phase = f"{tag}{T['t0']}:post"
            t0, Wq, nb, P = T["t0"], T["W"], T["nb"], T["P"]
            out0 = T["out0"]
            yrT, yrT_b = yr_ring.items[0]
            mT, mT_b = yrT, yrT_b
            load_x(T)
            for half in range(2):
                wo, wo_b = wload("out", half)
                for blk in range(nb):
                    ps = mmbank()
                    for kc in range(8):
                        k.op("pe", I("matmul", ps[0][0:P, :], lhsT=mT[:, kc, blk * 128:blk * 128 + P],
                                     rhs=wo[:, kc, :], start=(kc == 0), stop=(kc == 7)),
                             r=[mT_b, wo_b], w=[ps[1]], inc=(kc == 7))
                    k.op("dve", I("tensor_tensor", out=X[0:P, blk, half * 512:(half + 1) * 512], in0=ps[0][0:P, :],
                                  in1=X[0:P, blk, half * 512:(half + 1) * 512], op=ALU.add), r=[ps[1], X_b], w=[X_b])
                    yield 2.0
            u2T, u2T_b = yrT, yrT_b
            rms_to_T(lambda blk: (X[0:P, blk, :], X_b), nb, P, ln2T, u2T, u2T_b)
            for g3 in range(3):
                for cc in range(8):
                    c = g3 * 8 + cc
                    cl = (cc % 4) * 128
                    if cc % 4 == 0:
                        wga, wga_b = wload(f"up{g3}", cc // 4)
                        if out0 is not None:
                            wva, wva_b = wload(f"up{g3 + 3}", cc // 4)
                    psa = mmbank()
                    for kc in range(8):
                        k.op("pe", I("matmul", psa[0][:, 0:Wq], lhsT=wga[:, kc, cl:cl + 128],
                                     rhs=u2T[:, kc, 0:Wq], start=(kc == 0), stop=(kc == 7)),
                             r=[u2T_b, wga_b], w=[psa[1]], inc=(kc == 7))
                    gp, gp_b = f32r.next()
                    k.op("dve", I("tensor_copy", out=gp[:, 0:2], in_=hist_fc[:, c, :]), r=[fc_b], w=[gp_b])
                    k.op("dve", I("tensor_copy", out=gp[:, 2:2 + Wq], in_=psa[0][:, 0:Wq]), r=[psa[1]], w=[gp_b])
                    k.op("dve", I("tensor_copy", out=hist_fc[:, c, :], in_=gp[:, Wq:Wq + 2]), r=[gp_b], w=[fc_b])
                    if out0 is None:
                        yield 2.0
                        continue
                    psv = mmbank()
                    for kc in range(8):
                        k.op("pe", I("matmul", psv[0][:, 0:Wq], lhsT=wva[:, kc, cl:cl + 128],
                                     rhs=u2T[:, kc, 0:Wq], start=(kc == 0), stop=(kc == 7)),
                             r=[u2T_b, wva_b], w=[psv[1]], inc=(kc == 7))
                    gc, gc_b = f32r.next()
                    k.op("dve", I("tensor_scalar", out=gc[:, 0:Wq], in0=gp[:, 0:Wq], scalar1=fwT[:, c, 0:1],
                                  scalar2=fbT[:, c:c + 1], op0=ALU.mult, op1=ALU.add), r=[gp_b, cst], w=[gc_b])
                    for i in range(1, 3):
                        k.op("dve", I("scalar_tensor_tensor", out=gc[:, 0:Wq], in0=gp[:, i:i + Wq],
                                      scalar=fwT[:, c, i:i + 1], in1=gc[:, 0:Wq], op0=ALU.mult, op1=ALU.add),
                             r=[gp_b, gc_b, cst], w=[gc_b])
                    k.op("act", I("activation", out=gc[:, 0:Wq], in_=gc[:, 0:Wq], func=AF.Gelu_apprx_tanh),
                         r=[gc_b], w=[gc_b])
                    k.op("dve", I("tensor_tensor", out=hT[:, c, 0:Wq], in0=psv[0][:, 0:Wq], in1=gc[:, 0:Wq],
                                  op=ALU.mult), r=[psv[1], gc_b], w=[m1_b if c < 16 else hT_hi_b])
                    yield 4.0
            if T["halo"]:
                hf2 = hist_fc[:].rearrange("p c i -> p (c i)")
                k.op("dve", I("tensor_scalar", out=hf2, in0=hf2, scalar1=flagb[:, 0:1], scalar2=None, op0=ALU.mult),
                     r=[fc_b, cst], w=[fc_b])
            if out0 is None:
                return
            for half in range(2):
                wds = [wload(f"dn{half}", 0, 8 * j, 8 * j + 8) for j in range(3)]
                for blk in range(nb):
                    ps = mmbank()
                    for kc in range(24):
                        wsl, wsl_b = wds[kc // 8]
                        k.op("pe", I("matmul", ps[0][0:P, :], lhsT=hT[:, kc, blk * 128:blk * 128 + P],
                                     rhs=wsl[:, kc % 8, :], start=(kc == 0), stop=(kc == 23)),
                             r=[m1_b, hT_hi_b, wsl_b], w=[ps[1]], inc=(kc == 23))
                    k.op("dve", I("tensor_tensor", out=X[0:P, blk, half * 512:(half + 1) * 512], in0=ps[0][0:P, :],
                                  in1=X[0:P, blk, half * 512:(half + 1) * 512], op=ALU.add), r=[ps[1], X_b], w=[X_b])
                    yield 2.0
            k.dma("sp", o["y"][out0:out0 + Wq, :].rearrange("(b p) c -> p b c", p=P), X[0:P, 0:nb, :],
                  r=[X_b], w=[], semof=X_b, is_out=True)

        def drain(g):
            for _ in g:
                pass

        def n_blocks(T):
            return NH * (T["nb"] + sum((min(c0 + TW, T["pos0"]) - c0) // 128 for c0 in range(0, T["pos0"], TW)))

        def w_pre(T):
            return 2.0 * (3 if T["full"] else 2) * 2 * T["nb"] + 0.1

        def w_post(T):
            if T["out0"] is None:
                return 2.0 * 2 * T["nb"] + 24 * 2.0
            return 2.0 * 2 * T["nb"] + 24 * 4.0 + 2.0 * 2 * T["nb"]

        def interleave(ga, wa, gbs, wb):
            gb = (x for g in gbs for x in g)
            ca = cb = 0.0
            da = db = False
            while not (da and db):
                if db or (not da and ca * wb <= cb * wa):
                    try:
                        ca += next(ga)
                    except StopIteration:
                        da = True
                else:
                    try:
                        cb += next(gb)
                    except StopIteration:
                        db = True

        fulls = [T for T in tiles if T["full"]]
        npref = 0
        for T in tiles:
            if not T["full"]:
                pfx[0] = True
                drain(pre(T))
                if tag == "p":
                    cast_rest(npref, uT_ring.items[T["par"]][1])
                npref += 1
                drain(pre_rnn(T))
                pfx[0] = False
        if tag == "p":
            for j in range(npref, 4):
                cast_rest(j, None)
            if npref:
                st0, st0_b = stat.next()
                k.op("dve", I("memset", st0[0:1, 0:1], 0.0), r=alias_bufs,
                     w=[st0_b, m1_b, hT_hi_b, QT_ring.items[0][1]])
        if fulls:
            drain(pre(fulls[0]))
            drain(pre_rnn(fulls[0]))
        import itertools
        for j, T in enumerate(fulls):
            ga = attn(T)
            nbh = n_blocks(T) // NH
            if j > 0:
                head0 = itertools.islice(ga, nbh - 1)
                interleave(head0, 1.5 * (nbh - 1), [merge(fulls[j - 1])], 64.0)
            gbs, wb = [], 0.0
            if j > 0:
                gbs.append(post_rest(fulls[j - 1]))
                wb += w_post(fulls[j - 1])
                gbs.append(pre_rnn(T))
                wb += 48.0
            if j + 1 < len(fulls):
                gbs.append(pre(fulls[j + 1]))
                wb += w_pre(fulls[j + 1])
            interleave(ga, 1.5 * (n_blocks(T) - (nbh - 1 if j > 0 else 0)), gbs, max(wb, 1.0))
        if fulls:
            drain(merge(fulls[-1]))
            drain(post_rest(fulls[-1]))
        if between is not None:
            between()

        for i in range(3):
            k.dma("sp", o["rc"][i:i + 1, :].rearrange("o (c p) -> p (o c)", p=128), hist_rc[:, :, i], r=[st_b], w=[],
                  semof=st_b, slow=True, is_out=True)
        k.dma("sp", o["h"].rearrange("o (c p) -> p (o c)", p=128), hstate[:], r=[st_b], w=[], semof=st_b, slow=True,
              is_out=True)
        for i in range(2):
            k.dma("sp", o["fc"][i:i + 1, :].rearrange("o (c p) -> p (o c)", p=128), hist_fc[:, :, i], r=[fc_b], w=[],
                  semof=fc_b, slow=True, is_out=True)

    for blk in range(PAST // 128):
        kin, kin_b = kin_ring.next()
        k.dma("pool", kin[:, :], ck[blk * 128:(blk + 1) * 128, :], r=[], w=[kin_b], semof=kin_b)
        to_T(kin, kin_b, 128, KTst, KTst_b, blk % 4)
        if blk % 4 == 3:
            c0 = (blk // 4) * TW
            k.dma("sp", KTs["s"].rearrange("h d t -> d h t")[:, :, c0:c0 + TW], KTst[:, :, :],
                  r=[KTst_b], w=[KTs_b["s"][blk // 4]], semof=KTs_b["s"][blk // 4])
    cv_v = cv.rearrange("(b p) (h d) -> h p b d", p=128, h=NH)
    for ci in range(PAST // TW):
        k.dma_multi("pool", [(Vs["s"][h, :, ci * 4:ci * 4 + 4, :], cv_v[h, :, ci * 4:ci * 4 + 4, :]) for h in range(NH)],
                    r=[], w=[Vs_b["s"][ci]], semof=Vs_b["s"][ci])

    tiles_p = []
    t = 0
    while t < H - 128:
        w_ = min(TW, H - 128 - t)
        tiles_p.append(dict(t0=t, W=w_, full=False, out0=None))
        t += w_
    tiles_p.append(dict(t0=H - 128, W=128, full=True, out0=None, halo=True))
    for j in range(H // TW):
        tiles_p.append(dict(t0=H + TW * j, W=TW, full=True, out0=TW * j))
    run_seq("p", x_p, 0, tiles_p, None, boundary=H,
            between=lambda: run_seq("s", x_s, PAST, [dict(t0=0, W=DEC, full=True, out0=0)], (s_rc, s_h, s_fc)))

    order = k.schedule()
    out_tags = k.emit(order)
    E = k.engs["sp"]
    best = {}
    for sem, val in out_tags:
        if sem.name not in best or best[sem.name][1] < val:
            best[sem.name] = (sem, val)
    for sem, val in best.values():
        E.h.wait_ge(sem, val)
    k.stats = dict(nsem=k.nsem, sb_bytes=k.sb_bytes, sim_us=k.sim_ns / 1e3, nunits=len(k.units),
                   counts={n: e.count for n, e in k.engs.items()})
    es.close()
    return nc, k.stats


def host_consts():
    bf = ml_dtypes.bfloat16
    ident = np.eye(128, dtype=np.float32).astype(bf)
    kk = np.arange(128)
    inv = np.float32(np.sqrt(np.float32(HD)))
    L = (-inv * (kk[:, None] >= kk[None, :]).astype(np.float32)).astype(bf)
    ones = (-inv * np.ones((128, 128), np.float32)).astype(bf)
    msk = (kk[:, None] < kk[None, :]).astype(np.float32)
    return dict(c_ident=ident, c_L=L, c_ones=ones, c_mask=msk.astype(bf))


_W_NAMES = ["ln1", "w_in", "rnn_conv_w", "rnn_conv_b", "lru_wa", "lru_ba", "lru_wx", "lru_bx", "lru_lambda",
            "q_norm_g", "k_norm_g", "w_proj_rnn", "w_proj_attn", "w_out", "ln2", "w_up", "ffn_conv_w",
            "ffn_conv_b", "w_down"]


def make_in_maps(inputs, n_cores):
    f = lambda a: np.ascontiguousarray(np.asarray(a, dtype=np.float32))
    xp = f(inputs["x_prompt"])
    xsm = f(inputs["x_sample"])
    B, SEQ, _ = xp.shape
    DB, DEC, _ = xsm.shape
    H = SEQ // 2
    PAST = inputs["cache_k"].shape[2]
    ckk = f(inputs["cache_k"])[0].reshape(DB, PAST, D)
    cvv = f(inputs["cache_v"])[0].reshape(DB, PAST, D)
    src = f(inputs["state_rnn_conv"])[0]
    sh = f(inputs["state_rnn_h"])[0]
    sfc = f(inputs["state_ffn_conv"])[0]
    shared = {}
    for n in _W_NAMES:
        a = f(inputs[n])[0]
        if a.ndim == 1:
            a = a[None, :]
        shared[n] = np.ascontiguousarray(a)
    shared.update(host_consts())
    maps = []
    for c in range(n_cores):
        b, g = (c // 2) % B, c % 2
        m = dict(shared)
        if g == 1:
            m["x_p"] = xp[b]
            flag = np.array([1.0, 0.0], np.float32)
        else:
            m["x_p"] = np.ascontiguousarray(np.concatenate([xp[b, :H], xp[b, :H]], axis=0))
            flag = np.array([0.0, -30000.0], np.float32)
        m["c_flag"] = np.ascontiguousarray(np.broadcast_to(flag[None, :], (128, 2)))
        m["x_s"] = xsm[c % DB]
        m["ck"] = ckk[c % DB]
        m["cv"] = cvv[c % DB]
        m["s_rc"] = src[c % DB]
        m["s_h"] = sh[c % DB][None, :]
        m["s_fc"] = sfc[c % DB]
        maps.append(m)
    return maps, (B, SEQ, DB, DEC, PAST)


_CACHE = {}


def run(inputs, n_cores=N_CORES):
    maps, (B, SEQ, DB, DEC, PAST) = make_in_maps(inputs, n_cores)
    H = SEQ // 2
    key = (SEQ, PAST, DEC)
    if key not in _CACHE:
        _CACHE[key] = build_program(SEQ, PAST, DEC)
    nc, stats = _CACHE[key]
    res = run_bass_kernel_spmd(nc, maps, core_ids=list(range(n_cores)))
    R = res.results
    nb = min(B, n_cores // 2)
    ns = min(DB, n_cores)

    def halves(name, shape):
        return np.stack([np.concatenate([np.asarray(R[2 * b][name], dtype=np.float32).reshape(shape),
                                         np.asarray(R[2 * b + 1][name], dtype=np.float32).reshape(shape)], axis=0)
                         for b in range(nb)])

    def fin(name, shape):
        return np.stack([np.asarray(R[2 * b + 1][name], dtype=np.float32).reshape(shape) for b in range(nb)])

    def samp(name, shape):
        return np.stack([np.asarray(R[c][name], dtype=np.float32).reshape(shape) for c in range(ns)])

    y_p = halves("y_p", (H, D))
    k_p = halves("k_p", (H, NH, HD))[None]
    v_p = halves("v_p", (H, NH, HD))[None]
    rc_p = fin("rc_p", (3, D))[None]
    h_p = fin("h_p", (D,))[None]
    fc_p = fin("fc_p", (2, DFF))[None]
    y_s = samp("y_s", (DEC, D))
    k_s = samp("k_s", (DEC, NH, HD))[None]
    v_s = samp("v_s", (DEC, NH, HD))[None]
    rc_s = samp("rc_s", (3, D))[None]
    h_s = samp("h_s", (D,))[None]
    fc_s = samp("fc_s", (2, DFF))[None]
    return (y_p, y_s, k_p, v_p, rc_p, h_p, fc_p, k_s, v_s, rc_s, h_s, fc_s)


def kernel(**inputs):
    return run(inputs, N_CORES)
```

```python
import numpy as np
from contextlib import ExitStack
import ml_dtypes
import concourse.bass as bass
import concourse.mybir as mybir
from concourse.bass_utils import run_bass_kernel_spmd

F32 = mybir.dt.float32
BF16 = mybir.dt.bfloat16
AF = mybir.ActivationFunctionType
ALU = mybir.AluOpType
AX = mybir.AxisListType

D = 1024
NH = 8
HD = 128
DFF = 3072
EPS = 1e-6
N_CORES = 8


def I(name, *a, **kw):
    return (name, a, kw)


class Buf:
    __slots__ = ("name", "w", "r", "sem", "cnt", "const")

    def __init__(self, name, const=False):
        self.name = name
        self.w = None
        self.r = []
        self.sem = None
        self.cnt = 0
        self.const = const


class Eng:
    def __init__(self, name, h, sem):
        self.name = name
        self.h = h
        self.sem = sem
        self.count = 0
        self.waited = {}


class Unit:
    __slots__ = ("id", "eng", "instrs", "deps", "dur", "kind", "semof", "slow", "is_out", "tag", "cls",
                 "nbytes", "start", "children", "nrem", "ready", "phase")


def _free_size(ap):
    n = 1
    for s_ in ap.shape[1:]:
        n *= s_
    return n


_ACT_CLS = {"Exp": "A", "Ln": "A", "Sigmoid": "B", "Gelu_apprx_tanh": "C", "Sqrt": "D"}


class K:
    def __init__(self, nc, es):
        self.nc = nc
        self.es = es
        self.engs = {}
        for name, h in (("pe", nc.tensor), ("act", nc.scalar), ("dve", nc.vector),
                        ("pool", nc.gpsimd), ("sp", nc.sync)):
            sem = es.enter_context(nc.semaphore("c_" + name))
            self.engs[name] = Eng(name, h, sem)
        self.nsem = 5
        self.units = []
        self.pe_group = None
        self.sb_bytes = 0

    def sb(self, name, shape, dt):
        n = 1
        for s_ in shape[1:]:
            n *= s_
        self.sb_bytes += n * (4 if dt == F32 else 2)
        return self.nc.alloc_sbuf_tensor(name, list(shape), dt)

    def new_sem(self, name):
        self.nsem += 1
        return self.es.enter_context(self.nc.semaphore("d_" + name))

    def _new_unit(self, eng, instrs, r, w, kind, dur):
        u = Unit()
        u.id = len(self.units)
        u.eng = eng
        u.instrs = instrs
        u.kind = kind
        u.dur = dur
        u.semof = None
        u.slow = False
        u.is_out = False
        u.tag = None
        u.cls = None
        u.nbytes = 0
        u.phase = getattr(self, "phase", "")
        deps = set()
        for b in r:
            if b.w is not None:
                deps.add(b.w)
        for b in w:
            if b.w is not None:
                deps.add(b.w)
            deps.update(b.r)
        deps.discard(u.id)
        u.deps = deps
        for b in r:
            if not b.const:
                b.r.append(u.id)
        for b in w:
            b.w = u.id
            b.r = []
        self.units.append(u)
        return u

    def op(self, eng, instr, r=(), w=(), inc=True):
        name, a, kw = instr
        out = kw.get("out", a[0] if a else None)
        n = _free_size(out) if out is not None else 64
        if eng == "pe":
            dur = 35.0 + 0.45 * max(n, 64) if name == "matmul" else 90.0
            if self.pe_group is None:
                self.pe_group = ([], [], [], 0.0)
            g = self.pe_group
            g[0].append(instr)
            g[1].extend(r)
            g[2].extend(w)
            self.pe_group = (g[0], g[1], g[2], g[3] + dur)
            if not inc:
                return
            g = self.pe_group
            self.pe_group = None
            return self._new_unit("pe", g[0], list(dict.fromkeys(g[1])), list(dict.fromkeys(g[2])), "op", g[3])
        if eng == "act":
            dur = 180.0 + 0.85 * n
        elif eng == "dve":
            dur = 120.0 + 1.05 * n
            if name == "tensor_tensor_scan":
                dur = 120.0 + 2.1 * n
        else:
            dur = 200.0 + 1.8 * n
        u = self._new_unit(eng, [instr], list(r), list(w), "op", dur)
        if eng == "act" and name == "activation":
            u.cls = _ACT_CLS.get(str(kw.get("func")).split(".")[-1])
        return u

    def dma(self, q, out, in_, r, w, semof, slow=False, is_out=False):
        return self.dma_multi(q, [(out, in_)], r, w, semof, slow=slow, is_out=is_out)

    def dma_multi(self, q, pairs, r, w, semof, slow=False, is_out=False):
        instrs = []
        nbytes = 0
        for out, in_ in pairs:
            kw = dict(out=out, in_=in_)
            if slow:
                kw["allow_slow_non_contiguous"] = True
            instrs.append(("dma_start", (), kw))
            nbytes += 128 * _free_size(out) * (4 if out.dtype == F32 else 2)
        u = self._new_unit(q, instrs, list(r), list(w), "dma", 60.0 * len(pairs))
        u.semof = semof
        u.is_out = is_out
        u.nbytes = nbytes
        return u

    def schedule(self):
        import heapq
        U = self.units
        assert self.pe_group is None
        for u in U:
            u.children = []
            u.nrem = len(u.deps)
            u.ready = 0.0
        for u in U:
            for d in u.deps:
                U[d].children.append(u.id)
        finish = [0.0] * len(U)
        eng_free = {e: 0.0 for e in self.engs}
        avail = {e: [] for e in self.engs}
        pend = {e: [] for e in self.engs}
        for u in U:
            if u.nrem == 0:
                heapq.heappush(avail[u.eng], u.id)
        dma_free = 0.0
        act_cls = None
        byp = [0]
        order = []
        nleft = len(U)
        while nleft:
            best = None
            for e in self.engs:
                pe_, av = pend[e], avail[e]
                while pe_ and pe_[0][0] <= eng_free[e]:
                    heapq.heappush(av, heapq.heappop(pe_)[1])
                if av:
                    cand = (eng_free[e], av[0], e, True)
                elif pe_:
                    cand = (pe_[0][0], pe_[0][1], e, False)
                else:
                    continue
                if best is None or cand[:2] < best[:2]:
                    best = cand
            start, uid, e, from_av = best
            if from_av:
                if e == "act" and len(avail[e]) > 1:
                    head = avail[e][0]
                    hu = U[head]
                    pick = head
                    if not (hu.cls is None or hu.cls == act_cls or act_cls is None) and byp[0] < 4:
                        popped = []
                        pick = None
                        for _ in range(min(10, len(avail[e]))):
                            c = heapq.heappop(avail[e])
                            popped.append(c)
                            if U[c].cls is None or U[c].cls == act_cls:
                                pick = c
                                break
                        for c in popped:
                            if c != pick:
                                heapq.heappush(avail[e], c)
                        if pick is None:
                            pick = heapq.heappop(avail[e])
                            byp[0] = 0
                        else:
                            byp[0] += 1
                    else:
                        heapq.heappop(avail[e])
                        byp[0] = 0
                    uid = pick
                else:
                    heapq.heappop(avail[e])
            else:
                heapq.heappop(pend[e])
            u = U[uid]
            dur = u.dur
            if e == "act" and u.cls is not None:
                if act_cls is not None and u.cls != act_cls:
                    dur += 1300.0
                act_cls = u.cls
            u.start = start
            if u.kind == "dma":
                dma_free = max(dma_free, start) + u.nbytes / 180.0
                finish[uid] = dma_free + 2000.0
                eng_free[e] = start + dur
            else:
                finish[uid] = start + dur
                eng_free[e] = start + dur
            order.append(uid)
            nleft -= 1
            for c in u.children:
                cu = U[c]
                cu.nrem -= 1
                if finish[uid] > cu.ready:
                    cu.ready = finish[uid]
                if cu.nrem == 0:
                    heapq.heappush(pend[cu.eng], (cu.ready, c))
        self.sim_ns = max(finish) if finish else 0.0
        return order

    def emit(self, order):
        U = self.units
        out_tags = []
        for uid in order:
            u = U[uid]
            E = self.engs[u.eng]
            best = {}
            for d in u.deps:
                du = U[d]
                if u.eng == "pe" and du.eng == "pe" and du.kind == "op" and u.kind == "op":
                    continue
                sem, val = du.tag
                key = sem.name
                if key not in best or best[key][1] < val:
                    best[key] = (sem, val)
            for key, (sem, val) in best.items():
                if E.waited.get(key, 0) >= val:
                    continue
                E.h.wait_ge(sem, val)
                E.waited[key] = val
            if u.kind == "dma":
                b = u.semof
                if b.sem is None:
                    b.sem = self.new_sem(b.name)
                for name, a, kw in u.instrs:
                    ins = getattr(E.h, name)(*a, **kw)
                    b.cnt += 16
                    ins.then_inc(b.sem, 16)
                u.tag = (b.sem, b.cnt)
                if u.is_out:
                    out_tags.append(u.tag)
            else:
                ins = None
                for name, a, kw in u.instrs:
                    ins = getattr(E.h, name)(*a, **kw)
                E.count += 1
                ins.then_inc(E.sem, 1)
                u.tag = (E.sem, E.count)
        return out_tags


class Ring:
    def __init__(self, k, name, n, shape, dt):
        self.items = []
        for i in range(n):
            self.items.append((k.sb(f"{name}{i}", shape, dt), Buf(f"{name}{i}")))
        self.i = 0

    def next(self):
        it = self.items[self.i % len(self.items)]
        self.i += 1
        return it


def build_program(SEQ, PAST, DEC):
    TW = 512
    assert SEQ % (2 * TW) == 0 and PAST % TW == 0 and DEC <= 128
    H = SEQ // 2
    nc = bass.Bass("TRN2", target_bir_lowering=False)
    es = ExitStack()
    k = K(nc, es)

    def din(name, shape, dt=F32):
        return nc.dram_tensor(name, list(shape), dt, kind="ExternalInput").ap()

    def dout(name, shape, dt=F32):
        return nc.dram_tensor(name, list(shape), dt, kind="ExternalOutput").ap()

    def dscr(name, shape, dt):
        return nc.dram_tensor(name, list(shape), dt, kind="Internal").ap()

    x_p = din("x_p", [SEQ, D])
    x_s = din("x_s", [DEC, D])
    ck = din("ck", [PAST, D])
    cv = din("cv", [PAST, D])
    s_rc = din("s_rc", [3, D])
    s_h = din("s_h", [1, D])
    s_fc = din("s_fc", [2, DFF])
    ln1 = din("ln1", [1, D])
    w_in = din("w_in", [D, 7 * D])
    cw = din("rnn_conv_w", [4, D])
    cb = din("rnn_conv_b", [1, D])
    lwa = din("lru_wa", [16, 64, 64])
    lba = din("lru_ba", [1, D])
    lwx = din("lru_wx", [16, 64, 64])
    lbx = din("lru_bx", [1, D])
    lam = din("lru_lambda", [1, D])
    qg = din("q_norm_g", [1, HD])
    kg = din("k_norm_g", [1, HD])
    w_pr = din("w_proj_rnn", [D, D])
    w_pa = din("w_proj_attn", [D, D])
    w_out = din("w_out", [D, D])
    ln2 = din("ln2", [1, D])
    w_up = din("w_up", [D, 2 * DFF])
    fw = din("ffn_conv_w", [3, DFF])
    fb = din("ffn_conv_b", [1, DFF])
    w_dn = din("w_down", [DFF, D])
    c_id = din("c_ident", [128, 128], BF16)
    c_L = din("c_L", [128, 128], BF16)
    c_one = din("c_ones", [128, 128], BF16)
    c_msk = din("c_mask", [128, 128], BF16)
    c_flag = din("c_flag", [128, 2])

    outs = {}
    for tag, T in (("p", H), ("s", DEC)):
        outs[tag] = dict(
            y=dout(f"y_{tag}", [T, D]), k=dout(f"k_{tag}", [T, D]), v=dout(f"v_{tag}", [T, D]),
            rc=dout(f"rc_{tag}", [3, D]), h=dout(f"h_{tag}", [1, D]), fc=dout(f"fc_{tag}", [2, DFF]))

    KTs = {"p": dscr("KTs_p", [NH, HD, SEQ], BF16), "s": dscr("KTs_s", [NH, HD, PAST], BF16)}
    Vs = {"p": dscr("Vs_p", [NH, 128, SEQ // 128, HD], BF16),
          "s": dscr("Vs_s", [NH, 128, PAST // 128, HD], BF16)}
    class _Toks(dict):
        def __init__(self, name):
            super().__init__()
            self.nm = name

        def __missing__(self, ci):
            self[ci] = Buf(f"{self.nm}_{ci}")
            return self[ci]

    KTs_b = {"p": _Toks("KTs_p"), "s": _Toks("KTs_s")}
    Vs_b = {"p": _Toks("Vs_p"), "s": _Toks("Vs_s")}

    cst = Buf("consts")
    ident = k.sb("ident", [128, 128], BF16)
    Lneg = k.sb("Lneg", [128, 128], BF16)
    onesneg = k.sb("onesneg", [128, 128], BF16)
    msk = k.sb("msk", [128, 128], BF16)
    ln1T = k.sb("ln1T", [128, 8], F32)
    ln2T = k.sb("ln2T", [128, 8], F32)
    cwT = k.sb("cwT", [128, 8, 4], F32)
    cbT = k.sb("cbT", [128, 8], F32)
    baT = k.sb("baT", [128, 8], F32)
    bxT = k.sb("bxT", [128, 8], F32)
    lamT = k.sb("lamT", [128, 8], F32)
    cA = k.sb("cA", [128, 8], F32)
    epsT = k.sb("epsT", [128, 1], F32)
    onepT = k.sb("onepT", [128, 1], F32)
    fwT = k.sb("fwT", [128, 24, 3], F32)
    fbT = k.sb("fbT", [128, 24], F32)
    gq = k.sb("gq", [128, HD], F32)
    gk = k.sb("gk", [128, HD], F32)
    wbd = k.sb("wbd", [128, 2, 8, 128], BF16)
    wbd_b = Buf("wbd")

    cpairs = []

    def cdma(out, in_, slow=False):
        cpairs.append((out, in_))

    cdma(ident[:], c_id)
    cdma(Lneg[:], c_L)
    cdma(onesneg[:], c_one)
    cdma(msk[:], c_msk)
    cdma(ln1T[:], ln1.rearrange("o (k p) -> p (o k)", p=128), slow=True)
    cdma(ln2T[:], ln2.rearrange("o (k p) -> p (o k)", p=128), slow=True)
    for i in range(4):
        cdma(cwT[:, :, i], cw[i:i + 1, :].rearrange("o (c p) -> p (o c)", p=128), slow=True)
    cdma(cbT[:], cb.rearrange("o (k p) -> p (o k)", p=128), slow=True)
    cdma(baT[:], lba.rearrange("o (k p) -> p (o k)", p=128), slow=True)
    cdma(bxT[:], lbx.rearrange("o (k p) -> p (o k)", p=128), slow=True)
    cdma(lamT[:], lam.rearrange("o (k p) -> p (o k)", p=128), slow=True)
    for i in range(3):
        cdma(fwT[:, :, i], fw[i:i + 1, :].rearrange("o (c p) -> p (o c)", p=128), slow=True)
    cdma(fbT[:], fb.rearrange("o (k p) -> p (o k)", p=128), slow=True)
    cdma(gq[:], qg.partition_broadcast(128).rearrange("p o d -> p (o d)"))
    cdma(gk[:], kg.partition_broadcast(128).rearrange("p o d -> p (o d)"))
    flagb = k.sb("flagb", [128, 2], F32)
    cdma(flagb[:], c_flag)
    k.dma_multi("sp", cpairs, r=[], w=[cst], semof=cst, slow=True)
    X = k.sb("X", [128, 4, D], F32)
    X_b = Buf("X")
    wbd_f = X[:, 0:2, :].rearrange("p a (c m) -> p a c m", c=8)
    k.op("dve", I("memset", wbd_f, 0.0), w=[wbd_b, X_b])
    for wi, src in enumerate((lwa, lwx)):
        v = src.rearrange("(c two) i o -> two i c o", two=2)
        k.dma("sp", wbd_f[0:64, wi, :, 0:64], v[0], r=[], w=[wbd_b, X_b], semof=wbd_b)
        k.dma("sp", wbd_f[64:128, wi, :, 64:128], v[1], r=[], w=[wbd_b, X_b], semof=wbd_b)
    k.op("dve", I("tensor_copy", out=wbd[:], in_=wbd_f), r=[wbd_b, X_b], w=[wbd_b])
    k.op("act", I("activation", out=cA[:], in_=lamT[:], func=AF.Exp, scale=-1.0), r=[cst], w=[cst])
    k.op("act", I("activation", out=cA[:], in_=cA[:], func=AF.Ln, bias=1.0), r=[cst], w=[cst])
    k.op("dve", I("tensor_scalar", out=cA[:], in0=cA[:], scalar1=-8.0, scalar2=None, op0=ALU.mult),
         r=[cst], w=[cst])
    k.op("dve", I("memset", epsT[:], EPS), w=[cst])
    k.op("dve", I("memset", onepT[:], 1.0000002), w=[cst])

    banks = [(nc.alloc_psum_tensor(f"ps{i}", [128, 512], F32), Buf(f"ps{i}")) for i in range(2)]
    at_pairs = []
    for i in range(2):
        pt = nc.alloc_psum_tensor(f"atp{i}", [128, 1024], F32)
        ba, bb = Buf(f"atp{i}a"), Buf(f"atp{i}b")
        at_pairs.append((pt, [ba, bb]))
        banks.append((pt[:, 0:512], ba))
        banks.append((pt[:, 512:1024], bb))
    banks += [(nc.alloc_psum_tensor(f"ps{i}", [128, 512], F32), Buf(f"ps{i}")) for i in range(6, 8)]
    mm_ring = banks[0:2]
    mm_i = [0, 0]

    pfx = [False]
    pfx_ring = banks[0:7]
    mm_i.append(0)

    def mmbank():
        if pfx[0]:
            it = pfx_ring[mm_i[2] % len(pfx_ring)]
            mm_i[2] += 1
            return it
        it = mm_ring[mm_i[0] % len(mm_ring)]
        mm_i[0] += 1
        return it

    def atpair():
        it = at_pairs[mm_i[1] % len(at_pairs)]
        mm_i[1] += 1
        return it

    o_banks = banks[6:7]
    tp_t, tp_b = banks[7]
    tp_bf = tp_t[:].bitcast(BF16)

    wring = Ring(k, "wslot", 6, [128, 4096], BF16)
    stat = Ring(k, "stat", 4, [128, 16], F32)
    uT_ring = Ring(k, "uT", 2, [128, 8, TW], BF16)
    QT_ring = Ring(k, "QT", 2, [128, NH, TW], BF16)
    yr_ring = Ring(k, "yrT", 1, [128, 8, TW], BF16)
    KTst = k.sb("KTst", [128, NH, TW], BF16)
    KTst_b = Buf("KTst")
    Vb = k.sb("Vb", [128, 4, D], BF16)
    Vb_b = Buf("Vb")
    sqj = k.sb("sqj", [128, 512], F32)
    sqj_b = Buf("sqj")
    junk = sqj[:].bitcast(BF16)
    junk_b = sqj_b
    kf_ring = Ring(k, "kf", 1, [128, 512], F32)
    vf_ring = Ring(k, "vf", 1, [128, 512], F32)
    nb_ring = Ring(k, "nb", 2, [128, 512], BF16)
    xs_ring = Ring(k, "xs", 2, [128, D], BF16)
    hT = k.sb("hT", [128, 24, TW], BF16)
    m1 = hT[:, 0:16, :].rearrange("p a b -> p (a b)").bitcast(F32).rearrange("p (c t) -> p c t", c=8)
    m1_b = Buf("hT_lo")
    hT_hi_b = Buf("hT_hi")
    yaT = k.sb("yaT", [128, 8, TW], BF16)
    yaT_b = Buf("yaT")
    f32r = Ring(k, "f32r", 6, [128, TW + 4], F32)
    b16r = Ring(k, "b16r", 2, [128, TW], BF16)
    a16r = Ring(k, "a16r", 3, [128, TW], BF16)
    a16p = Ring(k, "a16p", 4, [128, 2 * TW], BF16)
    kc_ring = Ring(k, "kTc", 3, [128, TW], BF16)
    vc_ring = Ring(k, "vc", 3, [128, 4, HD], BF16)
    kin_ring = xs_ring
    xblk_ring = Ring(k, "xblk", 1, [128, D], F32)
    class _ListRing:
        def __init__(self, items):
            self.items = items
            self.i = 0

        def next(self):
            it = self.items[self.i % len(self.items)]
            self.i += 1
            return it

    hT_f32 = hT[:, :, :].rearrange("p a b -> p (a b)").bitcast(F32)
    pf32 = _ListRing(list(f32r.items) + [(hT_f32[:, i * (TW + 4):(i + 1) * (TW + 4)], Buf(f"pf32_{i}"))
                                          for i in range(11)])
    qt0 = QT_ring.items[0][0]
    pb16 = _ListRing(list(b16r.items) + [(qt0[:, i, :], Buf(f"pb16_{i}")) for i in range(NH)])
    alias_bufs = [b for _, b in pf32.items[len(f32r.items):]] + [b for _, b in pb16.items[len(b16r.items):]]

    wscr = {}

    def wcast(key, src_ap, kc, cols, after=None):
        scr = dscr("ws_" + key, [128, kc * cols], BF16)
        sb_ = Buf("ws_" + key)
        k.dma("pool", scr.rearrange("p (k c) -> p k c", k=kc), src_ap, r=([after] if after is not None else []),
              w=[sb_], semof=sb_)
        wscr[key] = (scr, sb_, kc, cols)

    def wload(key, half, k0=0, k1=None):
        scr, sb_, kc, cols = wscr[key]
        k1 = kc if k1 is None else k1
        wt, wb = wring.next()
        view = wt[:, 0:(k1 - k0) * 512].rearrange("p (k c) -> p k c", k=k1 - k0)
        src = scr.rearrange("p (k c) -> p k c", k=kc)[:, k0:k1, half * 512:(half + 1) * 512]
        k.dma("sp", view, src, r=[sb_], w=[wb], semof=wb)
        return view, wb

    w_in_v = w_in.rearrange("(k p) c -> p k c", p=128)
    w_up_v = w_up.rearrange("(k p) c -> p k c", p=128)
    w_pr_v = w_pr.rearrange("(k p) c -> p k c", p=128)
    w_pa_v = w_pa.rearrange("(k p) c -> p k c", p=128)
    w_out_v = w_out.rearrange("(k p) c -> p k c", p=128)
    w_dn_v = w_dn.rearrange("(k p) c -> p k c", p=128)

    CS = 11.3125
    SCALE = 1.0 / CS
    k.op("dve", I("tensor_scalar", out=gq[:], in0=gq[:], scalar1=float(HD) ** -0.5 * CS, scalar2=None, op0=ALU.mult),
         r=[cst], w=[cst])
    def cast_rest(batch, after):
        if batch == 0:
            for gi in (2, 1, 5, 6):
                wcast(f"in{gi}", w_in_v[:, :, gi * D:(gi + 1) * D], 8, D, after)
        elif batch == 1:
            wcast("pr", w_pr_v, 8, D, after)
            wcast("pa", w_pa_v, 8, D, after)
            wcast("out", w_out_v, 8, D, after)
            wcast("up0", w_up_v[:, :, 0:D], 8, D, after)
        elif batch == 2:
            wcast("up3", w_up_v[:, :, DFF:DFF + D], 8, D, after)
            for g3 in (1, 2):
                wcast(f"up{g3}", w_up_v[:, :, g3 * D:(g3 + 1) * D], 8, D, after)
                wcast(f"up{g3 + 3}", w_up_v[:, :, DFF + g3 * D:DFF + (g3 + 1) * D], 8, D, after)
        elif batch == 3:
            for half in range(2):
                wcast(f"dn{half}", w_dn_v[:, :, half * 512:(half + 1) * 512], 24, 512, after)

    for gi in (3, 4, 0):
        wcast(f"in{gi}", w_in_v[:, :, gi * D:(gi + 1) * D], 8, D)

    def rms_to_T(src_fn, nb, P, gT, dstT, dst_b):
        for blk in range(nb):
            xa, xa_b = src_fn(blk)
            st, st_b = stat.next()
            k.op("act", I("activation", out=junk[0:P, :], in_=xa, func=AF.Square,
                                               accum_out=st[0:P, 0:1]), r=[xa_b], w=[junk_b, st_b])
            k.op("act", I("activation", out=st[0:P, 1:2], in_=st[0:P, 0:1], func=AF.Ln,
                          scale=1.0 / D, bias=epsT[0:P, 0:1]), r=[st_b, cst], w=[st_b])
            k.op("act", I("activation", out=st[0:P, 2:3], in_=st[0:P, 1:2], func=AF.Exp, scale=-0.5),
                 r=[st_b], w=[st_b])
            xs, xs_b = xs_ring.next()
            k.op("dve", I("tensor_scalar", out=xs[0:P, :], in0=xa, scalar1=st[0:P, 2:3],
                                                  scalar2=None, op0=ALU.mult), r=[xa_b, st_b], w=[xs_b])
            for kc in range(8):
                k.op("pe", I("transpose", out=tp_bf[:, kc * 128:kc * 128 + P],
                                                 in_=xs[0:P, kc * 128:(kc + 1) * 128], identity=ident[0:P, 0:P]),
                     r=[xs_b, cst], w=[tp_b], inc=(kc == 7))
            src = tp_bf.rearrange("p (k t) -> p k t", k=8)[:, :, 0:P]
            gb = gT[:, :].unsqueeze(2).to_broadcast([128, 8, P])
            k.op("dve", I("tensor_tensor", out=dstT[:, :, blk * 128:blk * 128 + P], in0=src, in1=gb,
                                                  op=ALU.mult), r=[tp_b, cst], w=[dst_b])

    def head_norm(ps, P, grow, dst, dst_b):
        st, st_b = stat.next()
        k.op("act", I("activation", out=sqj[0:P, :], in_=ps[0][0:P, :], func=AF.Square),
             r=[ps[1]], w=[sqj_b])
        k.op("dve", I("tensor_reduce", out=st[0:P, 0:4], in_=sqj[0:P, :].rearrange("p (h d) -> p h d", h=4),
                      axis=AX.X, op=ALU.add), r=[sqj_b], w=[st_b])
        k.op("act", I("activation", out=st[0:P, 4:8], in_=st[0:P, 0:4], func=AF.Ln,
                      scale=1.0 / HD, bias=epsT[0:P, 0:1]), r=[st_b, cst], w=[st_b])
        k.op("act", I("activation", out=st[0:P, 8:12], in_=st[0:P, 4:8], func=AF.Exp, scale=-0.5),
             r=[st_b], w=[st_b])
        for h in range(4):
            k.op("dve", I("scalar_tensor_tensor", out=dst[0:P, h * 128:(h + 1) * 128],
                          in0=ps[0][0:P, h * 128:(h + 1) * 128], scalar=st[0:P, 8 + h:9 + h], in1=grow[0:P, :],
                          op0=ALU.mult, op1=ALU.mult), r=[ps[1], st_b, cst], w=[dst_b])

    def to_T(src, src_b, P, dstT, dst_b, blk, h0=0, nh=8, on_dve=False):
        for h in range(nh):
            k.op("pe", I("transpose", out=tp_bf[:, h * 128:h * 128 + P], in_=src[0:P, h * 128:(h + 1) * 128],
                         identity=ident[0:P, 0:P]), r=[src_b, cst], w=[tp_b], inc=(h == nh - 1))
        s3 = tp_bf[:, 0:nh * 128].rearrange("p (k t) -> p k t", k=nh)[:, :, 0:P]
        if on_dve:
            k.op("dve", I("tensor_copy", out=dstT[:, h0:h0 + nh, blk * 128:blk * 128 + P], in_=s3), r=[tp_b], w=[dst_b])
        else:
            k.op("act", I("activation", out=dstT[:, h0:h0 + nh, blk * 128:blk * 128 + P], in_=s3, func=AF.Copy),
                 r=[tp_b], w=[dst_b])

    def run_seq(tag, x_ap, past, tiles, init, boundary=None, between=None):
        o = outs[tag]
        hist_rc = k.sb(f"hrc_{tag}", [128, 8, 3], F32)
        hstate = k.sb(f"hst_{tag}", [128, 8], F32)
        hist_fc = k.sb(f"hfc_{tag}", [128, 24, 2], F32)
        st_b = Buf(f"state_{tag}")
        fc_b = Buf(f"fstate_{tag}")
        if init is None:
            k.op("dve", I("memset", hist_rc[:], 0.0), w=[st_b])
            k.op("dve", I("memset", hstate[:], 0.0), w=[st_b])
            k.op("dve", I("memset", hist_fc[:], 0.0), w=[fc_b])
        else:
            for i in range(3):
                k.dma("sp", hist_rc[:, :, i], init[0][i:i + 1, :].rearrange("o (c p) -> p (o c)", p=128), r=[],
                      w=[st_b], semof=st_b, slow=True)
            k.dma("sp", hstate[:], init[1].rearrange("o (c p) -> p (o c)", p=128), r=[], w=[st_b], semof=st_b, slow=True)
            for i in range(2):
                k.dma("sp", hist_fc[:, :, i], init[2][i:i + 1, :].rearrange("o (c p) -> p (o c)", p=128), r=[],
                      w=[fc_b], semof=fc_b, slow=True)
        nfull = 0
        for ti, T in enumerate(tiles):
            T["nb"] = (T["W"] + 127) // 128
            T["P"] = min(T["W"], 128)
            T["pos0"] = past + T["t0"]
            T["last"] = ti == len(tiles) - 1
            T["via_scratch"] = T["W"] % 128 == 0
            T.setdefault("halo", False)
            if T["full"]:
                T["par"] = nfull % 2
                nfull += 1
            else:
                T["par"] = 0

        def load_x(T):
            k.dma("sp", X[0:T["P"], 0:T["nb"], :],
                  x_ap[T["t0"]:T["t0"] + T["W"], :].rearrange("(b p) c -> p b c", p=T["P"]), r=[], w=[X_b], semof=X_b)

        def pre(T):
            k.phase = f"{tag}{T['t0']}:pre"
            t0, Wq, nb, P, pos0 = T["t0"], T["W"], T["nb"], T["P"], T["pos0"]
            full, out0 = T["full"], T["out0"]
            uT, uT_b = uT_ring.items[T["par"]]
            QT, QT_b = QT_ring.items[T["par"]]
            yrT, yrT_b = yr_ring.items[0]
            def xin(blk):
                xb_t, xb_b = xblk_ring.next()
                r0 = t0 + blk * 128
                k.dma("sp", xb_t[0:P, :], x_ap[r0:r0 + P, :], r=[], w=[xb_b], semof=xb_b)
                return xb_t[0:P, :], xb_b
            rms_to_T(xin, nb, P, ln1T, uT, uT_b)
            groups = ((2, "q"), (3, "k"), (4, "v")) if full else ((3, "k"), (4, "v"))
            for gi, gname in groups:
                for half in range(2):
                    wv, wb = wload(f"in{gi}", half)
                    for blk in range(nb):
                        ps = mmbank()
                        for kc in range(8):
                            k.op("pe", I("matmul", ps[0][0:P, :], lhsT=uT[:, kc, blk * 128:blk * 128 + P],
                                         rhs=wv[:, kc, :], start=(kc == 0), stop=(kc == 7)),
                                 r=[uT_b, wb], w=[ps[1]], inc=(kc == 7))
                        r0 = (out0 + blk * 128) if out0 is not None else None
                        cs = slice(half * 512, (half + 1) * 512)
                        if gname == "q":
                            nbf, nbf_b = nb_ring.next()
                            head_norm(ps, P, gq, nbf, nbf_b)
                            to_T(nbf, nbf_b, P, QT, QT_b, blk, h0=4 * half, nh=4, on_dve=full)
                        elif gname == "k":
                            nbf, nbf_b = nb_ring.next()
                            if out0 is not None:
                                kf, kf_b = kf_ring.next()
                                head_norm(ps, P, gk, kf, kf_b)
                                k.op("dve", I("tensor_copy", out=nbf[0:P, :], in_=kf[0:P, :]), r=[kf_b], w=[nbf_b])
                                k.dma("sp", o["k"][r0:r0 + P, cs], kf[0:P, :], r=[kf_b], w=[], semof=kf_b, is_out=True)
                            else:
                                head_norm(ps, P, gk, nbf, nbf_b)
                            to_T(nbf, nbf_b, P, KTst, KTst_b, blk, h0=4 * half, nh=4, on_dve=full)
                        else:
                            if out0 is not None:
                                vf, vf_b = vf_ring.next()
                                k.op("act", I("activation", out=vf[0:P, :], in_=ps[0][0:P, :], func=AF.Copy),
                                     r=[ps[1]], w=[vf_b])
                                k.op("dve", I("tensor_copy", out=Vb[0:P, blk, cs], in_=vf[0:P, :]), r=[vf_b], w=[Vb_b])
                                k.dma("sp", o["v"][r0:r0 + P, cs], vf[0:P, :], r=[vf_b], w=[], semof=vf_b, is_out=True)
                            else:
                                k.op("act", I("activation", out=Vb[0:P, blk, cs], in_=ps[0][0:P, :], func=AF.Copy),
                                     r=[ps[1]], w=[Vb_b])
                        yield 2.0
            if T["via_scratch"]:
                k.dma("sp", KTs[tag].rearrange("h d t -> d h t")[:, :, pos0:pos0 + Wq], KTst[:, :, 0:Wq],
                      r=[KTst_b], w=[KTs_b[tag][pos0 // TW]], semof=KTs_b[tag][pos0 // TW])
                k.dma_multi("sp", [(Vs[tag][h, :, pos0 // 128:pos0 // 128 + nb, :], Vb[:, 0:nb, h * 128:(h + 1) * 128])
                                   for h in range(NH)], r=[Vb_b], w=[Vs_b[tag][pos0 // TW]], semof=Vs_b[tag][pos0 // TW])

            yield 0.1

        def pre_rnn(T):
            k.phase = f"{tag}{T['t0']}:rnn"
            Wq, full = T["W"], T["full"]
            uT, uT_b = uT_ring.items[T["par"]]
            yrT, yrT_b = yr_ring.items[0]
            fr = pf32 if pfx[0] else f32r
            br = pb16 if pfx[0] else b16r
            for c in range(8):
                cl = (c % 4) * 128
                if c % 4 == 0:
                    wxr, wxr_b = wload("in0", c // 4)
                    if full:
                        wgr, wgr_b = wload("in1", c // 4)
                ps = mmbank()
                for kc in range(8):
                    k.op("pe", I("matmul", ps[0][:, 0:Wq], lhsT=wxr[:, kc, cl:cl + 128],
                                 rhs=uT[:, kc, 0:Wq], start=(kc == 0), stop=(kc == 7)),
                         r=[uT_b, wxr_b], w=[ps[1]], inc=(kc == 7))
                xp, xp_b = fr.next()
                k.op("dve", I("tensor_copy", out=xp[:, 0:3], in_=hist_rc[:, c, :]), r=[st_b], w=[xp_b])
                if full:
                    k.op("dve", I("tensor_copy", out=xp[:, 3:3 + Wq], in_=ps[0][:, 0:Wq]), r=[ps[1]], w=[xp_b])
                else:
                    k.op("act", I("activation", out=xp[:, 3:3 + Wq], in_=ps[0][:, 0:Wq], func=AF.Copy),
                         r=[ps[1]], w=[xp_b])
                k.op("dve", I("tensor_copy", out=hist_rc[:, c, :], in_=xp[:, Wq:Wq + 3]), r=[xp_b], w=[st_b])
                xc, xc_b = fr.next()
                k.op("dve", I("tensor_scalar", out=xc[:, 0:Wq], in0=xp[:, 0:Wq], scalar1=cwT[:, c, 0:1],
                              scalar2=cbT[:, c:c + 1], op0=ALU.mult, op1=ALU.add), r=[xp_b, cst], w=[xc_b])
                for i in range(1, 4):
                    k.op("dve", I("scalar_tensor_tensor", out=xc[:, 0:Wq], in0=xp[:, i:i + Wq],
                                  scalar=cwT[:, c, i:i + 1], in1=xc[:, 0:Wq], op0=ALU.mult, op1=ALU.add),
                         r=[xp_b, xc_b, cst], w=[xc_b])
                xcb, xcb_b = br.next()
                k.op("dve", I("tensor_copy", out=xcb[:, 0:Wq], in_=xc[:, 0:Wq]), r=[xc_b], w=[xcb_b])
                gts = []
                for wi, bT in ((0, baT), (1, bxT)):
                    psg = mmbank()
                    k.op("pe", I("matmul", psg[0][:, 0:Wq], lhsT=wbd[:, wi, c, :], rhs=xcb[:, 0:Wq],
                                 start=True, stop=True), r=[xcb_b, wbd_b], w=[psg[1]])
                    g, g_b = fr.next()
                    k.op("act", I("activation", out=g[:, 0:Wq], in_=psg[0][:, 0:Wq], func=AF.Sigmoid,
                                  bias=bT[:, c:c + 1]), r=[psg[1], cst], w=[g_b])
                    gts.append((g, g_b))
                (rg, rg_b), (ig, ig_b) = gts
                k.op("act", I("activation", out=rg[:, 0:Wq], in_=rg[:, 0:Wq], func=AF.Exp, scale=cA[:, c:c + 1]),
                     r=[rg_b, cst], w=[rg_b])
                sq, sq_b = fr.next()
                k.op("dve", I("scalar_tensor_tensor", out=sq[:, 0:Wq], in0=rg[:, 0:Wq], scalar=0.99999994,
                              in1=rg[:, 0:Wq], op0=ALU.min, op1=ALU.mult), r=[rg_b], w=[sq_b])
                k.op("act", I("activation", out=sq[:, 0:Wq], in_=sq[:, 0:Wq], func=AF.Ln, scale=-1.0, bias=1.0),
                     r=[sq_b], w=[sq_b])
                k.op("act", I("activation", out=sq[:, 0:Wq], in_=sq[:, 0:Wq], func=AF.Exp, scale=0.5),
                     r=[sq_b], w=[sq_b])
                k.op("dve", I("tensor_tensor", out=ig[:, 0:Wq], in0=ig[:, 0:Wq], in1=xc[:, 0:Wq], op=ALU.mult),
                     r=[ig_b, xc_b], w=[ig_b])
                k.op("dve", I("tensor_tensor", out=ig[:, 0:Wq], in0=ig[:, 0:Wq], in1=sq[:, 0:Wq], op=ALU.mult),
                     r=[ig_b, sq_b], w=[ig_b])
                hs, hs_b = sq, sq_b
                k.op("dve", I("tensor_tensor_scan", out=hs[:, 0:Wq], data0=rg[:, 0:Wq], data1=ig[:, 0:Wq],
                              initial=hstate[:, c:c + 1], op0=ALU.mult, op1=ALU.add),
                     r=[rg_b, ig_b, st_b], w=[hs_b])
                k.op("dve", I("tensor_copy", out=hstate[:, c:c + 1], in_=hs[:, Wq - 1:Wq]), r=[hs_b], w=[st_b])
                if full:
                    ps2 = mmbank()
                    for kc in range(8):
                        k.op("pe", I("matmul", ps2[0][:, 0:Wq], lhsT=wgr[:, kc, cl:cl + 128],
                                     rhs=uT[:, kc, 0:Wq], start=(kc == 0), stop=(kc == 7)),
                             r=[uT_b, wgr_b], w=[ps2[1]], inc=(kc == 7))
                    gl, gl_b = xp, xp_b
                    k.op("act", I("activation", out=gl[:, 0:Wq], in_=ps2[0][:, 0:Wq], func=AF.Gelu_apprx_tanh),
                         r=[ps2[1]], w=[gl_b])
                    k.op("dve", I("tensor_tensor", out=yrT[:, c, 0:Wq], in0=hs[:, 0:Wq], in1=gl[:, 0:Wq], op=ALU.mult),
                         r=[hs_b, gl_b], w=[yrT_b])
                yield 6.0
            if T["halo"]:
                k.op("dve", I("tensor_scalar", out=hstate[:], in0=hstate[:], scalar1=flagb[:, 0:1], scalar2=None,
                              op0=ALU.mult), r=[st_b, cst], w=[st_b])
                hr2 = hist_rc[:].rearrange("p c i -> p (c i)")
                k.op("dve", I("tensor_scalar", out=hr2, in0=hr2, scalar1=flagb[:, 0:1], scalar2=None, op0=ALU.mult),
                     r=[st_b, cst], w=[st_b])

        def attn(T):
            k.phase = f"{tag}{T['t0']}:attn"
            Wq, nb, pos0 = T["W"], T["nb"], T["pos0"]
            QT, QT_b = QT_ring.items[T["par"]]
            own_tile = boundary is not None and pos0 >= boundary
            chunks = [(c0, min(c0 + TW, pos0)) for c0 in range(0, pos0, TW)]
            for h in range(NH):
                o_t, o_b = o_banks[h % len(o_banks)]
                if T["via_scratch"]:
                    blist = [("diag", (pos0, pos0 + Wq), jj, 128) for jj in reversed(range(nb))]
                else:
                    blist = [("own", None, jj, min(128, Wq - jj * 128)) for jj in reversed(range(nb))]
                for (c0, c1) in reversed(chunks):
                    for jj in reversed(range((c1 - c0) // 128)):
                        blist.append(("past", (c0, c1), jj, 128))
                groups = []
                gi0 = 0
                while gi0 < len(blist):
                    if gi0 + 1 < len(blist) and blist[gi0][3] == 128 and blist[gi0 + 1][3] == 128:
                        groups.append(blist[gi0:gi0 + 2])
                        gi0 += 2
                    else:
                        groups.append(blist[gi0:gi0 + 1])
                        gi0 += 1
                Rprev = None
                cur_chunk = None
                bi = 0
                for grp in groups:
                    G = len(grp)
                    nk = grp[0][3]
                    zt, zbufs = atpair()
                    zbufs = zbufs[0:G]
                    infos = []
                    gbias = "unset"
                    for gi_, (kind, ch, jj, nk_) in enumerate(grp):
                        bias = None
                        if kind == "own":
                            lhs_k, lhs_k_b = KTst[:, h, jj * 128:jj * 128 + nk], KTst_b
                            lhs_v, lhs_v_b = Vb[0:nk, jj, h * 128:(h + 1) * 128], Vb_b
                        else:
                            if ch != cur_chunk:
                                cur_chunk = ch
                                c0, c1 = ch
                                kTc, kTc_b = kc_ring.next()
                                vc, vc_b = vc_ring.next()
                                k.dma("sp", kTc[:, 0:c1 - c0], KTs[tag][h, :, c0:c1], r=[KTs_b[tag][c0 // TW]],
                                      w=[kTc_b], semof=kTc_b)
                                k.dma("sp", vc[:, 0:(c1 - c0) // 128, :], Vs[tag][h, :, c0 // 128:c1 // 128, :],
                                      r=[Vs_b[tag][c0 // TW]], w=[vc_b], semof=vc_b)
                            lhs_k, lhs_k_b = kTc[:, jj * 128:(jj + 1) * 128], kTc_b
                            lhs_v, lhs_v_b = vc[:, jj, :], vc_b
                            if kind == "past" and own_tile and ch[0] < boundary:
                                bias = flagb[0:nk, 1:2]
                        assert gbias == "unset" or (gbias is None) == (bias is None)
                        gbias = bias
                        zv = zt[0:nk, gi_ * 512:gi_ * 512 + Wq]
                        k.op("pe", I("matmul", zv, lhsT=lhs_k, rhs=QT[:, h, 0:Wq], start=True, stop=True),
                             r=[lhs_k_b, QT_b], w=zbufs, inc=(gi_ == G - 1))
                        infos.append((kind, jj, lhs_v, lhs_v_b, zv))

                    def v3(t):
                        return t[0:nk, :].rearrange("p (b c) -> p b c", b=2)[:, 0:G, 0:Wq]

                    def mask(t, t_b):
                        for gi_, (kind, jj, _, _, _) in enumerate(infos):
                            if kind == "past":
                                continue
                            base = gi_ * 512
                            if jj > 0:
                                k.op("pool", I("memset", t[0:nk, base:base + jj * 128], 0.0), r=[t_b], w=[t_b])
                            qd = min(128, Wq - jj * 128)
                            k.op("pool", I("tensor_tensor", out=t[0:nk, base + jj * 128:base + jj * 128 + qd],
                                           in0=t[0:nk, base + jj * 128:base + jj * 128 + qd], in1=msk[0:nk, 0:qd],
                                           op=ALU.mult), r=[t_b, cst], w=[t_b])

                    ee, ee_b = a16p.next()
                    ekw = dict(out=v3(ee), in_=v3(zt), func=AF.Exp, scale=SCALE)
                    if gbias is not None:
                        ekw["bias"] = gbias
                    k.op("act", I("activation", **ekw), r=zbufs + [cst], w=[ee_b])
                    mask(ee, ee_b)
                    ss, ss_b = a16p.next()
                    k.op("act", I("activation", out=v3(ss), in_=v3(ee), func=AF.Ln, bias=1.0), r=[ee_b], w=[ss_b])
                    for gi_, (kind, jj, lhs_v, lhs_v_b, zv) in enumerate(infos):
                        ssl = ss[0:nk, gi_ * 512:gi_ * 512 + Wq]
                        lastb = bi == len(blist) - 1
                        k.op("pe", I("matmul", zv, lhsT=Lneg[0:nk, 0:nk], rhs=ssl, start=False,
                                     stop=(Rprev is None), skip_group_check=True),
                             r=[ss_b, cst] + zbufs, w=zbufs, inc=(Rprev is None))
                        if Rprev is not None:
                            rp, rp_b, rn = Rprev
                            k.op("pe", I("matmul", zv, lhsT=onesneg[0:rn, 0:nk], rhs=rp[0:rn, 0:Wq], start=False,
                                         stop=True, skip_group_check=True), r=[rp_b, cst] + zbufs, w=zbufs)
                        if not lastb:
                            rnw, rnw_b = a16r.next()
                            if Rprev is None:
                                if nk < 128:
                                    k.op("dve", I("memset", rnw[:, 0:Wq], 0.0), w=[rnw_b])
                                k.op("dve", I("tensor_copy", out=rnw[0:nk, 0:Wq], in_=ssl), r=[ss_b], w=[rnw_b])
                                Rprev = (rnw, rnw_b, 128)
                            else:
                                rp, rp_b, rn = Rprev
                                k.op("dve", I("tensor_tensor", out=rnw[0:rn, 0:Wq], in0=rp[0:rn, 0:Wq],
                                              in1=ss[0:rn, gi_ * 512:gi_ * 512 + Wq], op=ALU.add),
                                     r=[rp_b, ss_b], w=[rnw_b])
                                Rprev = (rnw, rnw_b, rn)
                        bi += 1
                    ww, ww_b = ee, ee_b
                    wkw = dict(out=v3(ww), in_=v3(zt), func=AF.Exp, scale=SCALE)
                    if gbias is not None:
                        wkw["bias"] = gbias
                    k.op("act", I("activation", **wkw), r=zbufs + [cst, ss_b], w=[ww_b])
                    mask(ww, ww_b)
                    for gi_, (kind, jj, lhs_v, lhs_v_b, zv) in enumerate(infos):
                        first = (bi - G + gi_) == 0
                        lastb = (bi - G + gi_) == len(blist) - 1
                        k.op("pe", I("matmul", o_t[:, 0:Wq], lhsT=lhs_v, rhs=ww[0:nk, gi_ * 512:gi_ * 512 + Wq],
                                     start=first, stop=lastb), r=[lhs_v_b, ww_b], w=[o_b])
                    yield 1.5 * G
                k.op("dve", I("tensor_copy", out=yaT[:, h, 0:Wq], in_=o_t[:, 0:Wq]), r=[o_b], w=[yaT_b])

        def merge(T):
            k.phase = f"{tag}{T['t0']}:merge"
            t0, Wq, nb, P = T["t0"], T["W"], T["nb"], T["P"]
            out0 = T["out0"]
            uT, uT_b = uT_ring.items[T["par"]]
            yrT, yrT_b = yr_ring.items[0]
            mT, mT_b = yrT, yrT_b
            for pi, (key, wsrc, gi, yT, yT_b) in enumerate((("pr", w_pr_v, 5, yrT, yrT_b), ("pa", w_pa_v, 6, yaT, yaT_b))):
                for c in range(8):
                    cl = (c % 4) * 128
                    if c % 4 == 0:
                        wp, wp_b = wload(key, c // 4)
                        wg, wg_b = wload(f"in{gi}", c // 4)
                    psp = mmbank()
                    for kc in range(8):
                        k.op("pe", I("matmul", psp[0][:, 0:Wq], lhsT=wp[:, kc, cl:cl + 128],
                                     rhs=yT[:, kc, 0:Wq], start=(kc == 0), stop=(kc == 7)),
                             r=[yT_b, wp_b], w=[psp[1]], inc=(kc == 7))
                    psg = mmbank()
                    for kc in range(8):
                        k.op("pe", I("matmul", psg[0][:, 0:Wq], lhsT=wg[:, kc, cl:cl + 128],
                                     rhs=uT[:, kc, 0:Wq], start=(kc == 0), stop=(kc == 7)),
                             r=[uT_b, wg_b], w=[psg[1]], inc=(kc == 7))
                    sg, sg_b = f32r.next()
                    k.op("act", I("activation", out=sg[:, 0:Wq], in_=psg[0][:, 0:Wq], func=AF.Sigmoid),
                         r=[psg[1]], w=[sg_b])
                    if pi == 0:
                        k.op("dve", I("tensor_tensor", out=m1[:, c, 0:Wq], in0=psp[0][:, 0:Wq], in1=sg[:, 0:Wq],
                                      op=ALU.mult), r=[psp[1], sg_b], w=[m1_b])
                    else:
                        k.op("dve", I("tensor_tensor", out=sg[:, 0:Wq], in0=psp[0][:, 0:Wq], in1=sg[:, 0:Wq],
                                      op=ALU.mult), r=[psp[1], sg_b], w=[sg_b])
                        k.op("pool", I("tensor_tensor", out=mT[:, c, 0:Wq], in0=sg[:, 0:Wq], in1=m1[:, c, 0:Wq],
                                       op=ALU.add), r=[sg_b, m1_b], w=[mT_b])
                    yield 4.0
        def post_rest(T):
            k.phase = f"{tag}{T['t0']}:post"
            t0, Wq, nb, P = T["t0"], T["W"], T["nb"], T["P"]
            out0 = T["out0"]
            yrT, yrT_b = yr_ring.items[0]
            mT, mT_b = yrT, yrT_b
            load_x(T)
            for half in range(2):
                wo, wo_b = wload("out", half)
                for blk in range(nb):
                    ps = mmbank()
                    for kc in range(8):
                        k.op("pe", I("matmul", ps[0][0:P, :], lhsT=mT[:, kc, blk * 128:blk * 128 + P],
                                     rhs=wo[:, kc, :], start=(kc == 0), stop=(kc == 7)),
                             r=[mT_b, wo_b], w=[ps[1]], inc=(kc == 7))
                    k.op("dve", I("tensor_tensor", out=X[0:P, blk, half * 512:(half + 1) * 512], in0=ps[0][0:P, :],
                                  in1=X[0:P, blk, half * 512:(half + 1) * 512], op=ALU.add), r=[ps[1], X_b], w=[X_b])
                    yield 2.0
            u2T, u2T_b = yrT, yrT_b
            rms_to_T(lambda blk: (X[0:P, blk, :], X_b), nb, P, ln2T, u2T, u2T_b)
            for g3 in range(3):
                for cc in range(8):
                    c = g3 * 8 + cc
                    cl = (cc % 4) * 128
                    if cc % 4 == 0:
                        wga, wga_b = wload(f"up{g3}", cc // 4)
                        if out0 is not None:
                            wva, wva_b = wload(f"up{g3 + 3}", cc // 4)
                    psa = mmbank()
                    for kc in range(8):
                        k.op("pe", I("matmul", psa[0][:, 0:Wq], lhsT=wga[:, kc, cl:cl + 128],
                                     rhs=u2T[:, kc, 0:Wq], start=(kc == 0), stop=(kc == 7)),
                             r=[u2T_b, wga_b], w=[psa[1]], inc=(kc == 7))
                    gp, gp_b = f32r.next()
                    k.op("dve", I("tensor_copy", out=gp[:, 0:2], in_=hist_fc[:, c, :]), r=[fc_b], w=[gp_b])
                    k.op("dve", I("tensor_copy", out=gp[:, 2:2 + Wq], in_=psa[0][:, 0:Wq]), r=[psa[1]], w=[gp_b])
                    k.op("dve", I("tensor_copy", out=hist_fc[:, c, :], in_=gp[:, Wq:Wq + 2]), r=[gp_b], w=[fc_b])
                    if out0 is None:
                        yield 2.0
                        continue
                    psv = mmbank()
                    for kc in range(8):
                        k.op("pe", I("matmul", psv[0][:, 0:Wq], lhsT=wva[:, kc, cl:cl + 128],
                                     rhs=u2T[:, kc, 0:Wq], start=(kc == 0), stop=(kc == 7)),
                             r=[u2T_b, wva_b], w=[psv[1]], inc=(kc == 7))
                    gc, gc_b = f32r.next()
                    k.op("dve", I("tensor_scalar", out=gc[:, 0:Wq], in0=gp[:, 0:Wq], scalar1=fwT[:, c, 0:1],
                                  scalar2=fbT[:, c:c + 1], op0=ALU.mult, op1=ALU.add), r=[gp_b, cst], w=[gc_b])
                    for i in range(1, 3):
                        k.op("dve", I("scalar_tensor_tensor", out=gc[:, 0:Wq], in0=gp[:, i:i + Wq],
                                      scalar=fwT[:, c, i:i + 1], in1=gc[:, 0:Wq], op0=ALU.mult, op1=ALU.add),
                             r=[gp_b, gc_b, cst], w=[gc_b])
                    k.op("act", I("activation", out=gc[:, 0:Wq], in_=gc[:, 0:Wq], func=AF.Gelu_apprx_tanh),
                         r=[gc_b], w=[gc_b])
                    k.op("dve", I("tensor_tensor", out=hT[:, c, 0:Wq], in0=psv[0][:, 0:Wq], in1=gc[:, 0:Wq],
                                  op=ALU.mult), r=[psv[1], gc_b], w=[m1_b if c < 16 else hT_hi_b])
                    yield 4.0
            if T["halo"]:
                hf2 = hist_fc[:].rearrange("p c i -> p (c i)")
                k.op("dve", I("tensor_scalar", out=hf2, in0=hf2, scalar1=flagb[:, 0:1], scalar2=None, op0=ALU.mult),
                     r=[fc_b, cst], w=[fc_b])
            if out0 is None:
                return
            for half in range(2):
                wds = [wload(f"dn{half}", 0, 8 * j, 8 * j + 8) for j in range(3)]
                for blk in range(nb):
                    ps = mmbank()
                    for kc in range(24):
                        wsl, wsl_b = wds[kc // 8]
                        k.op("pe", I("matmul", ps[0][0:P, :], lhsT=hT[:, kc, blk * 128:blk * 128 + P],
                                     rhs=wsl[:, kc % 8, :], start=(kc == 0), stop=(kc == 23)),
                             r=[m1_b, hT_hi_b, wsl_b], w=[ps[1]], inc=(kc == 23))
                    k.op("dve", I("tensor_tensor", out=X[0:P, blk, half * 512:(half + 1) * 512], in0=ps[0][0:P, :],
                                  in1=X[0:P, blk, half * 512:(half + 1) * 512], op=ALU.add), r=[ps[1], X_b], w=[X_b])
                    yield 2.0
            k.dma("sp", o["y"][out0:out0 + Wq, :].rearrange("(b p) c -> p b c", p=P), X[0:P, 0:nb, :],
                  r=[X_b], w=[], semof=X_b, is_out=True)

        def drain(g):
            for _ in g:
                pass

        def n_blocks(T):
            return NH * (T["nb"] + sum((min(c0 + TW, T["pos0"]) - c0) // 128 for c0 in range(0, T["pos0"], TW)))

        def w_pre(T):
            return 2.0 * (3 if T["full"] else 2) * 2 * T["nb"] + 0.1

        def w_post(T):
            if T["out0"] is None:
                return 2.0 * 2 * T["nb"] + 24 * 2.0
            return 2.0 * 2 * T["nb"] + 24 * 4.0 + 2.0 * 2 * T["nb"]

        def interleave(ga, wa, gbs, wb):
            gb = (x for g in gbs for x in g)
            ca = cb = 0.0
            da = db = False
            while not (da and db):
                if db or (not da and ca * wb <= cb * wa):
                    try:
                        ca += next(ga)
                    except StopIteration:
                        da = True
                else:
                    try:
                        cb += next(gb)
                    except StopIteration:
                        db = True

        fulls = [T for T in tiles if T["full"]]
        npref = 0
        for T in tiles:
            if not T["full"]:
                pfx[0] = True
                drain(pre(T))
                if tag == "p":
                    cast_rest(npref, uT_ring.items[T["par"]][1])
                npref += 1
                drain(pre_rnn(T))
                pfx[0] = False
        if tag == "p":
            for j in range(npref, 4):
                cast_rest(j, None)
            if npref:
                st0, st0_b = stat.next()
                k.op("dve", I("memset", st0[0:1, 0:1], 0.0), r=alias_bufs,
                     w=[st0_b, m1_b, hT_hi_b, QT_ring.items[0][1]])
        if between is not None:
            between()
        if fulls:
            drain(pre(fulls[0]))
            drain(pre_rnn(fulls[0]))
        import itertools
        for j, T in enumerate(fulls):
            ga = attn(T)
            nbh = n_blocks(T) // NH
            if j > 0:
                head0 = itertools.islice(ga, nbh - 1)
                interleave(head0, 1.5 * (nbh - 1), [merge(fulls[j - 1])], 64.0)
            gbs, wb = [], 0.0
            if j > 0:
                gbs.append(post_rest(fulls[j - 1]))
                wb += w_post(fulls[j - 1])
                gbs.append(pre_rnn(T))
                wb += 48.0
            if j + 1 < len(fulls):
                gbs.append(pre(fulls[j + 1]))
                wb += w_pre(fulls[j + 1])
            interleave(ga, 1.5 * (n_blocks(T) - (nbh - 1 if j > 0 else 0)), gbs, max(wb, 1.0))
        if fulls:
            drain(merge(fulls[-1]))
            drain(post_rest(fulls[-1]))

        for i in range(3):
            k.dma("sp", o["rc"][i:i + 1, :].rearrange("o (c p) -> p (o c)", p=128), hist_rc[:, :, i], r=[st_b], w=[],
                  semof=st_b, slow=True, is_out=True)
        k.dma("sp", o["h"].rearrange("o (c p) -> p (o c)", p=128), hstate[:], r=[st_b], w=[], semof=st_b, slow=True,
              is_out=True)
        for i in range(2):
            k.dma("sp", o["fc"][i:i + 1, :].rearrange("o (c p) -> p (o c)", p=128), hist_fc[:, :, i], r=[fc_b], w=[],
                  semof=fc_b, slow=True, is_out=True)

    for blk in range(PAST // 128):
        kin, kin_b = kin_ring.next()
        k.dma("pool", kin[:, :], ck[blk * 128:(blk + 1) * 128, :], r=[], w=[kin_b], semof=kin_b)
        to_T(kin, kin_b, 128, KTst, KTst_b, blk % 4)
        if blk % 4 == 3:
            c0 = (blk // 4) * TW
            k.dma("sp", KTs["s"].rearrange("h d t -> d h t")[:, :, c0:c0 + TW], KTst[:, :, :],
                  r=[KTst_b], w=[KTs_b["s"][blk // 4]], semof=KTs_b["s"][blk // 4])
    cv_v = cv.rearrange("(b p) (h d) -> h p b d", p=128, h=NH)
    for ci in range(PAST // TW):
        k.dma_multi("pool", [(Vs["s"][h, :, ci * 4:ci * 4 + 4, :], cv_v[h, :, ci * 4:ci * 4 + 4, :]) for h in range(NH)],
                    r=[], w=[Vs_b["s"][ci]], semof=Vs_b["s"][ci])

    tiles_p = []
    t = 0
    while t < H - 128:
        w_ = min(TW, H - 128 - t)
        tiles_p.append(dict(t0=t, W=w_, full=False, out0=None))
        t += w_
    tiles_p.append(dict(t0=H - 128, W=128, full=True, out0=None, halo=True))
    for j in range(H // TW):
        tiles_p.append(dict(t0=H + TW * j, W=TW, full=True, out0=TW * j))
    run_seq("p", x_p, 0, tiles_p, None, boundary=H,
            between=lambda: run_seq("s", x_s, PAST, [dict(t0=0, W=DEC, full=True, out0=0)], (s_rc, s_h, s_fc)))

    order = k.schedule()
    out_tags = k.emit(order)
    E = k.engs["sp"]
    best = {}
    for sem, val in out_tags:
        if sem.name not in best or best[sem.name][1] < val:
            best[sem.name] = (sem, val)
    for sem, val in best.values():
        E.h.wait_ge(sem, val)
    k.stats = dict(nsem=k.nsem, sb_bytes=k.sb_bytes, sim_us=k.sim_ns / 1e3, nunits=len(k.units),
                   counts={n: e.count for n, e in k.engs.items()})
    es.close()
    return nc, k.stats


def host_consts():
    bf = ml_dtypes.bfloat16
    ident = np.eye(128, dtype=np.float32).astype(bf)
    kk = np.arange(128)
    inv = np.float32(11.3125)
    L = (-inv * (kk[:, None] >= kk[None, :]).astype(np.float32)).astype(bf)
    ones = (-inv * np.ones((128, 128), np.float32)).astype(bf)
    msk = (kk[:, None] < kk[None, :]).astype(np.float32)
    return dict(c_ident=ident, c_L=L, c_ones=ones, c_mask=msk.astype(bf))


_W_NAMES = ["ln1", "w_in", "rnn_conv_w", "rnn_conv_b", "lru_wa", "lru_ba", "lru_wx", "lru_bx", "lru_lambda",
            "q_norm_g", "k_norm_g", "w_proj_rnn", "w_proj_attn", "w_out", "ln2", "w_up", "ffn_conv_w",
            "ffn_conv_b", "w_down"]


def make_in_maps(inputs, n_cores):
    f = lambda a: np.ascontiguousarray(np.asarray(a, dtype=np.float32))
    xp = f(inputs["x_prompt"])
    xsm = f(inputs["x_sample"])
    B, SEQ, _ = xp.shape
    DB, DEC, _ = xsm.shape
    H = SEQ // 2
    PAST = inputs["cache_k"].shape[2]
    ckk = f(inputs["cache_k"])[0].reshape(DB, PAST, D)
    cvv = f(inputs["cache_v"])[0].reshape(DB, PAST, D)
    src = f(inputs["state_rnn_conv"])[0]
    sh = f(inputs["state_rnn_h"])[0]
    sfc = f(inputs["state_ffn_conv"])[0]
    shared = {}
    for n in _W_NAMES:
        a = f(inputs[n])[0]
        if a.ndim == 1:
            a = a[None, :]
        shared[n] = np.ascontiguousarray(a)
    shared.update(host_consts())
    maps = []
    for c in range(n_cores):
        b, g = (c // 2) % B, c % 2
        m = dict(shared)
        if g == 1:
            m["x_p"] = xp[b]
            flag = np.array([1.0, 0.0], np.float32)
        else:
            m["x_p"] = np.ascontiguousarray(np.concatenate([xp[b, :H], xp[b, :H]], axis=0))
            flag = np.array([0.0, -30000.0], np.float32)
        m["c_flag"] = np.ascontiguousarray(np.broadcast_to(flag[None, :], (128, 2)))
        m["x_s"] = xsm[c % DB]
        m["ck"] = ckk[c % DB]
        m["cv"] = cvv[c % DB]
        m["s_rc"] = src[c % DB]
        m["s_h"] = sh[c % DB][None, :]
        m["s_fc"] = sfc[c % DB]
        maps.append(m)
    return maps, (B, SEQ, DB, DEC, PAST)


_CACHE = {}


def run(inputs, n_cores=N_CORES):
    maps, (B, SEQ, DB, DEC, PAST) = make_in_maps(inputs, n_cores)
    H = SEQ // 2
    key = (SEQ, PAST, DEC)
    if key not in _CACHE:
        _CACHE[key] = build_program(SEQ, PAST, DEC)
    nc, stats = _CACHE[key]
    res = run_bass_kernel_spmd(nc, maps, core_ids=list(range(n_cores)))
    R = res.results
    nb = min(B, n_cores // 2)
    ns = min(DB, n_cores)

    def halves(name, shape):
        return np.stack([np.concatenate([np.asarray(R[2 * b][name], dtype=np.float32).reshape(shape),
                                         np.asarray(R[2 * b + 1][name], dtype=np.float32).reshape(shape)], axis=0)
                         for b in range(nb)])

    def fin(name, shape):
        return np.stack([np.asarray(R[2 * b + 1][name], dtype=np.float32).reshape(shape) for b in range(nb)])

    def samp(name, shape):
        return np.stack([np.asarray(R[c][name], dtype=np.float32).reshape(shape) for c in range(ns)])

    y_p = halves("y_p", (H, D))
    k_p = halves("k_p", (H, NH, HD))[None]
    v_p = halves("v_p", (H, NH, HD))[None]
    rc_p = fin("rc_p", (3, D))[None]
    h_p = fin("h_p", (D,))[None]
    fc_p = fin("fc_p", (2, DFF))[None]
    y_s = samp("y_s", (DEC, D))
    k_s = samp("k_s", (DEC, NH, HD))[None]
    v_s = samp("v_s", (DEC, NH, HD))[None]
    rc_s = samp("rc_s", (3, D))[None]
    h_s = samp("h_s", (D,))[None]
    fc_s = samp("fc_s", (2, DFF))[None]
    return (y_p, y_s, k_p, v_p, rc_p, h_p, fc_p, k_s, v_s, rc_s, h_s, fc_s)


def kernel(**inputs):
    return run(inputs, N_CORES)
```
